# Optimizing a Trainium2 kernel written in Bass

```python
import jax, jax.numpy as jnp
from jax import lax
import numpy as np

D_MODEL = 1024
BATCH = 32
SEQ = 2048
DEPTH = 1
DEC_BATCH = 8
DEC_SEQ = 4096
PAST_LEN = 128

HEAD_DIM = 128
A_Q_HEADS = 8
A_KV_HEADS = 2
A_HALF_WINDOW = 128
A_BLOCK = 128
B_GROUPS = ((128, 1), (512, 4), (2048, 16))
B_HEADS = 4
B_BLOCK = 64
D_FF = 4 * D_MODEL
ROPE_THETA = 10000.0
EPS = 1e-6
NEG = -1e30

A_Q = A_Q_HEADS * HEAD_DIM
A_KV = A_KV_HEADS * HEAD_DIM
B_W = B_HEADS * HEAD_DIM
N_B = len(B_GROUPS)
D_IN = A_Q + 2 * A_KV + 3 * N_B * B_W + 2 * D_MODEL

kernel_name = "hybrid_window_dilated_encoder"


def rmsnorm(x, g):
    xf = x.astype(jnp.float32)
    y = xf * lax.rsqrt(jnp.mean(xf * xf, axis=-1, keepdims=True) + EPS)
    return (y * g.astype(jnp.float32)).astype(x.dtype)


def rope(t, pos):
    half = t.shape[-1] // 2
    inv = jnp.power(ROPE_THETA, -jnp.arange(half, dtype=jnp.float32) / half)
    ang = pos.astype(jnp.float32)[:, None] * inv[None, :]
    cos = jnp.cos(ang)[None, :, None, :]
    sin = jnp.sin(ang)[None, :, None, :]
    tf = t.astype(jnp.float32)
    t1, t2 = tf[..., :half], tf[..., half:]
    return jnp.concatenate([t1 * cos - t2 * sin, t1 * sin + t2 * cos], axis=-1).astype(t.dtype)


def banded_attention(q, k, v, half_window, block, sink=None):
    bsz, L, H, hd = q.shape
    hkv = k.shape[2]
    grp = H // hkv
    nb = -(-L // block)
    Lp = nb * block
    pad = Lp - L
    q = jnp.pad(q * (hd ** -0.5), ((0, 0), (0, pad), (0, 0), (0, 0)))
    k = jnp.pad(k, ((0, 0), (block, block + pad), (0, 0), (0, 0)))
    v = jnp.pad(v, ((0, 0), (block, block + pad), (0, 0), (0, 0)))
    qb = q.reshape(bsz, nb, block, hkv, grp, hd)

    def bands(t):
        t = t.reshape(bsz, nb + 2, block, hkv, hd)
        return jnp.concatenate([t[:, :-2], t[:, 1:-1], t[:, 2:]], axis=2)

    kb, vb = bands(k), bands(v)
    s = jnp.einsum('bnqhgd,bnkhd->bnhgqk', qb, kb).astype(jnp.float32)
    qpos = jnp.arange(Lp).reshape(nb, block, 1)
    kpos = (jnp.arange(nb)[:, None, None] - 1) * block + jnp.arange(3 * block)[None, None, :]
    mask = (jnp.abs(kpos - qpos) <= half_window) & (kpos >= 0) & (kpos < L)
    s = jnp.where(mask[None, :, None, None], s, NEG)
    m = jnp.max(s, axis=-1)
    if sink is not None:
        sk = sink.astype(jnp.float32).reshape(hkv, grp)[None, None, :, :, None]
        m = jnp.maximum(m, sk)
    p = jnp.exp(s - m[..., None])
    den = jnp.sum(p, axis=-1)
    if sink is not None:
        den = den + jnp.exp(sk - m)
    o = jnp.einsum('bnhgqk,bnkhd->bnqhgd', p, vb.astype(jnp.float32))
    o = o / jnp.transpose(den, (0, 1, 4, 2, 3))[..., None]
    o = o.reshape(bsz, Lp, H, hd)[:, :L].astype(v.dtype)
    lse = jnp.transpose(m + jnp.log(den), (0, 1, 4, 2, 3)).reshape(bsz, Lp, H)[:, :L]
    return o, lse


def dilated_attention(q, k, v, window, dilation):
    bsz, L, H, hd = q.shape
    Ld = L // dilation

    def fold(t):
        return jnp.transpose(t.reshape(bsz, Ld, dilation, H, hd), (0, 2, 1, 3, 4)).reshape(bsz * dilation, Ld, H, hd)

    o, lse = banded_attention(fold(q), fold(k), fold(v), (window // 2) // dilation, B_BLOCK)
    o = jnp.transpose(o.reshape(bsz, dilation, Ld, H, hd), (0, 2, 1, 3, 4)).reshape(bsz, L, H, hd)
    lse = jnp.transpose(lse.reshape(bsz, dilation, Ld, H), (0, 2, 1, 3)).reshape(bsz, L, H)
    return o, lse


def encoder_layer(x, w_in, sink, w_a, w_b, w_o, g_pre_mix, g_post_mix, g_pre_mlp, g_post_mlp, w_1, w_2):
    bsz, L, _ = x.shape
    pos = jnp.arange(L)
    h = rmsnorm(x, g_pre_mix)
    z = h @ w_in
    widths = [A_Q, A_KV, A_KV] + [B_W] * (3 * N_B) + [D_MODEL]
    parts = jnp.split(z, [int(c) for c in np.cumsum(widths)], axis=-1)

    qa = rope(parts[0].reshape(bsz, L, A_Q_HEADS, HEAD_DIM), pos)
    ka = rope(parts[1].reshape(bsz, L, A_KV_HEADS, HEAD_DIM), pos)
    va = parts[2].reshape(bsz, L, A_KV_HEADS, HEAD_DIM)
    oa, _ = banded_attention(qa, ka, va, A_HALF_WINDOW, A_BLOCK, sink)
    ya = oa.reshape(bsz, L, A_Q) @ w_a

    outs, lses = [], []
    for g, (window, dilation) in enumerate(B_GROUPS):
        qg = rope(parts[3 + 3 * g].reshape(bsz, L, B_HEADS, HEAD_DIM), pos)
        kg = rope(parts[4 + 3 * g].reshape(bsz, L, B_HEADS, HEAD_DIM), pos)
        vg = parts[5 + 3 * g].reshape(bsz, L, B_HEADS, HEAD_DIM)
        o, lse = dilated_attention(qg, kg, vg, window, dilation)
        outs.append(o)
        lses.append(lse)
    wts = jax.nn.softmax(jnp.stack(lses), axis=0)
    ob = jnp.sum(wts[..., None] * jnp.stack(outs).astype(jnp.float32), axis=0).astype(x.dtype)
    yb = ob.reshape(bsz, L, B_W) @ w_b

    gate_a, gate_b = parts[-2], parts[-1]
    mix = (jax.nn.sigmoid(gate_a) * ya + jax.nn.sigmoid(gate_b) * yb) @ w_o
    x = x + rmsnorm(mix, g_post_mix)

    u = jax.nn.relu(rmsnorm(x, g_pre_mlp) @ w_1)
    x = x + rmsnorm((u * u) @ w_2, g_post_mlp)
    return x


def trunk(x, w_in, sink, w_a, w_b, w_o, g_pre_mix, g_post_mix, g_pre_mlp, g_post_mlp, w_1, w_2):
    for l in range(DEPTH):
        x = encoder_layer(x, w_in[l], sink[l], w_a[l], w_b[l], w_o[l], g_pre_mix[l], g_post_mix[l],
                          g_pre_mlp[l], g_post_mlp[l], w_1[l], w_2[l])
    return x


def setup_inputs(seed: int = 0) -> dict:
    key = jax.random.key(seed)
    ks = jax.random.split(key, 14)
    f32 = jnp.float32

    def nrm(k, shape, scale):
        return jax.random.normal(k, shape, f32) * scale

    return {
        "x_prompt": nrm(ks[0], (BATCH, SEQ, D_MODEL), 1.0),
        "x_sample": nrm(ks[1], (DEC_BATCH, DEC_SEQ, D_MODEL), 1.0),
        "w_in": nrm(ks[2], (DEPTH, D_MODEL, D_IN), D_MODEL ** -0.5),
        "sink": nrm(ks[3], (DEPTH, A_Q_HEADS), 0.5),
        "w_a": nrm(ks[4], (DEPTH, A_Q, D_MODEL), A_Q ** -0.5),
        "w_b": nrm(ks[5], (DEPTH, B_W, D_MODEL), B_W ** -0.5),
        "w_o": nrm(ks[6], (DEPTH, D_MODEL, D_MODEL), D_MODEL ** -0.5),
        "g_pre_mix": 1.0 + nrm(ks[7], (DEPTH, D_MODEL), 0.1),
        "g_post_mix": 1.0 + nrm(ks[8], (DEPTH, D_MODEL), 0.1),
        "g_pre_mlp": 1.0 + nrm(ks[9], (DEPTH, D_MODEL), 0.1),
        "g_post_mlp": 1.0 + nrm(ks[10], (DEPTH, D_MODEL), 0.1),
        "w_1": nrm(ks[11], (DEPTH, D_MODEL, D_FF), D_MODEL ** -0.5),
        "w_2": nrm(ks[12], (DEPTH, D_FF, D_MODEL), D_FF ** -0.5),
    }


def reference(x_prompt, x_sample, w_in, sink, w_a, w_b, w_o, g_pre_mix, g_post_mix, g_pre_mlp, g_post_mlp, w_1, w_2):
    y_prompt = trunk(x_prompt, w_in, sink, w_a, w_b, w_o, g_pre_mix, g_post_mix, g_pre_mlp, g_post_mlp, w_1, w_2)
    y_sample = trunk(x_sample, w_in, sink, w_a, w_b, w_o, g_pre_mix, g_post_mix, g_pre_mlp, g_post_mlp, w_1, w_2)
    return (y_prompt, y_sample)
```

```python
import math
from contextlib import ExitStack

import numpy as np
import concourse.bass as bass
import concourse.mybir as mybir
from concourse.bass_utils import run_bass_kernel_spmd

F32 = mybir.dt.float32
BF16 = mybir.dt.bfloat16
I32 = mybir.dt.int32
ALU = mybir.AluOpType
AF = mybir.ActivationFunctionType

D = 1024
DIN = 8192
DFF = 4096
HD = 128
EPS = 1e-6
TT = 512
SCALE = HD ** -0.5
PI = float(np.pi)
TWO_PI = float(2 * np.pi)
N_CORES = 8
B_DIL = (1, 4, 16)

ENGS = ["sync", "scalar", "vector", "gpsimd", "tensor"]


class Ev:
    def __init__(self, sem):
        self.sem = sem
        self.count = 0


class Op:
    __slots__ = ("eng", "fn", "deps", "sig", "val", "sem", "kind")

    def __init__(self, eng, fn, deps, kind):
        self.eng, self.fn, self.deps, self.kind = eng, fn, list(deps), kind
        self.sig = False
        self.val = None
        self.sem = None


class Prog:
    def __init__(self, nc, eng_sems, evs, extra=()):
        self.nc = nc
        self.eng_sems = eng_sems
        self.evs = evs
        self.extra = list(extra)
        for ev in evs:
            assert ev.count == 0
        self.ops = {e: [] for e in ENGS}

    def op(self, eng, fn, deps=()):
        o = Op(eng, fn, [d for d in deps if d is not None], "c")
        self.ops[eng].append(o)
        return o

    def dma(self, eng, fn, ev, deps=()):
        o = Op(eng, fn, [d for d in deps if d is not None], "d")
        ev.count += 16
        o.val = ev.count
        o.sem = ev.sem
        o.sig = True
        self.ops[eng].append(o)
        return o

    def emit(self, loops=(), pre=None):
        nc = self.nc
        for e in ENGS:
            for o in self.ops[e]:
                for d in o.deps:
                    d.sig = True
        last = {}
        for e in ENGS:
            c = 0
            for o in self.ops[e]:
                if o.kind == "c":
                    last[e] = o
            if e in last:
                last[e].sig = True
            for o in self.ops[e]:
                if o.kind == "c" and o.sig:
                    c += 1
                    o.val = c
                    o.sem = self.eng_sems[e]
        finals = [(o.sem, o.val) for o in last.values()] + [(ev.sem, ev.count) for ev in self.evs + self.extra if ev.count > 0]
        all_sems = list(self.eng_sems.values()) + [ev.sem for ev in self.evs]

        def run(ctx):
            if pre is not None:
                pre(ctx)
            for ename in ENGS:
                eng = getattr(nc, ename)
                seen = {}

                def wait(sem, val):
                    k = id(sem)
                    if seen.get(k, 0) >= val:
                        return
                    eng.wait_ge(sem, val)
                    seen[k] = val
                for o in self.ops[ename]:
                    for d in o.deps:
                        wait(d.sem, d.val)
                    ins = o.fn(eng, ctx)
                    if o.kind == "d":
                        ins.then_inc(o.sem, 16)
                    elif o.sig:
                        ins.then_inc(o.sem, 1)
                for (s_, v_) in finals:
                    wait(s_, v_)
            nc.all_engine_barrier()
            for s_ in all_sems:
                nc.gpsimd.sem_clear(s_)
            nc.all_engine_barrier()

        if len(loops) == 0:
            run({})
        elif len(loops) == 1:
            with nc.Fori(0, loops[0], hint_back_edge=True) as a0:
                run({"i0": a0})
        else:
            with nc.Fori(0, loops[0]) as a0:
                with nc.Fori(0, loops[1]) as a1:
                    run({"i0": a0, "i1": a1})
        for ev in self.evs:
            ev.count = 0


def dsl(start, size):
    return slice(start, start + size) if isinstance(start, int) else bass.ds(start, size)


def qk_col(ci):
    if ci < 8:
        return 128 * ci
    if ci < 10:
        return 1024 + 128 * (ci - 8)
    g, r = divmod(ci - 10, 8)
    base = 1536 + 1536 * g
    return base + 128 * r if r < 4 else base + 512 + 128 * (r - 4)


V_GROUPS = [(1280, 256, 0), (1536 + 1024, 512, 256), (3072 + 1024, 512, 768), (4608 + 1024, 512, 1280)]


class H:
    def __init__(self, P):
        self.P = P

    def tt(self, eng, out, in0, in1, op, deps=()):
        return self.P.op(eng, lambda e, ctx: e.tensor_tensor(out=out, in0=in0, in1=in1, op=op), deps)

    def ts(self, eng, out, in0, s1, op0, deps=(), s2=None, op1=None):
        if op1 is None:
            return self.P.op(eng, lambda e, ctx: e.tensor_scalar(out=out, in0=in0, scalar1=s1, scalar2=None, op0=op0), deps)
        return self.P.op(eng, lambda e, ctx: e.tensor_scalar(out=out, in0=in0, scalar1=s1, scalar2=s2, op0=op0, op1=op1), deps)

    def stt(self, eng, out, in0, scalar, in1, op0, op1, deps=()):
        return self.P.op(eng, lambda e, ctx: e.scalar_tensor_tensor(out=out, in0=in0, scalar=scalar, in1=in1, op0=op0, op1=op1), deps)

    def act(self, out, in_, func, deps=(), scale=None, bias=None, accum=None):
        kw = {}
        if scale is not None:
            kw["scale"] = scale
        if bias is not None:
            kw["bias"] = bias
        if accum is not None:
            kw["accum_out"] = accum
        return self.P.op("scalar", lambda e, ctx: e.activation(out=out, in_=in_, func=func, **kw), deps)

    def cp(self, eng, out, in_, deps=()):
        return self.P.op(eng, lambda e, ctx: e.tensor_copy(out=out, in_=in_), deps)

    def recip(self, out, in_, deps=()):
        return self.P.op("vector", lambda e, ctx: e.reciprocal(out=out, in_=in_), deps)

    def memset(self, eng, out, val, deps=()):
        return self.P.op(eng, lambda e, ctx: e.memset(out, val), deps)

    def asel(self, out, in_, cmp, fill, base, pattern, cm, deps=()):
        return self.P.op("gpsimd", lambda e, ctx: e.affine_select(out=out, in_=in_, compare_op=cmp, fill=fill, base=base, pattern=pattern, channel_multiplier=cm), deps)

    def dma(self, eng, out, in_, ev, deps=()):
        def f(e, ctx):
            o_ = out(ctx) if callable(out) else out
            i_ = in_(ctx) if callable(in_) else in_
            return e.dma_start(out=o_, in_=i_)
        return self.P.dma(eng, f, ev, deps)

    def mms(self, lst, deps=()):
        lst = list(lst)

        def f(e, ctx):
            ins = None
            for (out, lhsT, rhs, st, sp) in lst:
                ins = e.matmul(out, lhsT=lhsT, rhs=rhs, start=st, stop=sp)
            return ins
        return self.P.op("tensor", f, deps)

    def trs(self, lst, ident, deps=()):
        lst = list(lst)

        def f(e, ctx):
            ins = None
            for (out, in_) in lst:
                ins = e.transpose(out=out, in_=in_, identity=ident)
            return ins
        return self.P.op("tensor", f, deps)


def build_program(groups, debug=False, phases=(0, 1, 2, 3, 4)):
    NT = sum(c * L for c, L in groups)
    LMAX = max(L for _, L in groups)
    NTI = 2
    NTI3 = 2
    assert NT % (TT * NTI3) == 0

    nc = bass.Bass("TRN2", target_bir_lowering=False)
    NB = NT // 1024
    x8 = [nc.dram_tensor(f"x{k}", [NB, 128, D], F32, kind="ExternalInput").ap() for k in range(8)]
    w_in_d = nc.dram_tensor("w_in", [D, DIN], F32, kind="ExternalInput").ap()
    sink_d = nc.dram_tensor("sink", [1, 8], F32, kind="ExternalInput").ap()
    w_a_d = nc.dram_tensor("w_a", [D, D], F32, kind="ExternalInput").ap()
    w_b_d = nc.dram_tensor("w_b", [512, D], F32, kind="ExternalInput").ap()
    w_o_d = nc.dram_tensor("w_o", [D, D], F32, kind="ExternalInput").ap()
    g1_d = nc.dram_tensor("g_pre_mix", [1, D], F32, kind="ExternalInput").ap()
    g2_d = nc.dram_tensor("g_post_mix", [1, D], F32, kind="ExternalInput").ap()
    g3_d = nc.dram_tensor("g_pre_mlp", [1, D], F32, kind="ExternalInput").ap()
    g4_d = nc.dram_tensor("g_post_mlp", [1, D], F32, kind="ExternalInput").ap()
    w_1_d = nc.dram_tensor("w_1", [D, DFF], F32, kind="ExternalInput").ap()
    w_2_d = nc.dram_tensor("w_2", [DFF, D], F32, kind="ExternalInput").ap()
    invf_d = nc.dram_tensor("invf", [128, 1], F32, kind="ExternalInput").ap()
    y8 = [nc.dram_tensor(f"y{k}", [NB, 128, D], F32, kind="ExternalOutput").ap() for k in range(8)]

    skind = "ExternalOutput" if debug else "Internal"
    BK = 1024
    qk_t = [nc.dram_tensor(f"qk{c}", [128, NT], BF16, kind="Internal").ap() for c in range(34)]
    v_s = nc.dram_tensor("v_s", [NT, 1792], BF16, kind=skind).ap()
    gt_t = [nc.dram_tensor(f"gt{c}", [128, NT], F32, kind="Internal").ap() for c in range(16)]
    o_t = [nc.dram_tensor(f"o{c}", [128, NT], BF16, kind="Internal").ap() for c in range(12)]
    x1_8 = [nc.dram_tensor(f"x1_{k}", [NB, 128, D], F32, kind="Internal").ap() for k in range(8)]
    cos_s = nc.dram_tensor("cos_s", [NB, 128, BK], F32, kind="Internal").ap()
    sin_s = nc.dram_tensor("sin_s", [NB, 128, BK], F32, kind="Internal").ap()

    with ExitStack() as gs:
        evs = []
        sem_ctr = [0]

        def new_sem(nm):
            sem_ctr[0] += 1
            return gs.enter_context(nc.semaphore(f"{nm}{sem_ctr[0]}"))

        def new_ev(nm="ev"):
            e = Ev(new_sem(nm))
            evs.append(e)
            return e

        ENG_SEMS = {e: new_sem("p" + e[:2]) for e in ENGS}

        def new_prog(extra=()):
            return Prog(nc, ENG_SEMS, evs, extra)

        EVW = [Ev(new_sem("wld")) for _ in range(5)]

        def sb(es, name, shape, dt):
            return es.enter_context(nc.sbuf_tensor(name, shape, dt))

        def pstile(es, name, shape, dt=F32):
            return es.enter_context(nc.psum_tensor(name, shape, dt))

        EVP = [new_ev() for _ in range(24)]

        ident_t = sb(gs, "ident", [128, 128], BF16)
        ones_t = sb(gs, "ones_b", [128, 128], BF16)
        maskA = sb(gs, "maskA", [128, 3, 128], BF16)
        maskB = sb(gs, "maskB", [128, 256], BF16)
        esink = sb(gs, "esink", [128, 8], F32)
        ident = ident_t[:]
        ones_b = ones_t[:]

        posblk = []
        for (c_, L_) in groups:
            for _ in range(c_):
                posblk += list(range(L_ // BK))
        gbase = []
        o = 0
        for (c, L) in groups:
            gbase.append(o)
            o += c * L

        if 0 in phases:
            with ExitStack() as es:
                P = new_prog()
                h = H(P)
                invf = sb(es, "invf_sb", [128, 1], F32)
                sgn = sb(es, "sgn", [128, 1], F32)
                posi = sb(es, "posi", [128, LMAX], I32)
                ang = sb(es, "ang", [128, LMAX], F32)
                t1 = sb(es, "t1", [128, LMAX], F32)
                t2 = sb(es, "t2", [128, LMAX], F32)
                ki = sb(es, "ki", [128, LMAX], I32)
                res = sb(es, "res", [128, LMAX], F32)
                ld = h.dma("sync", invf[:], invf_d, EVP[0])
                lds = h.dma("sync", esink[:], sink_d.partition_broadcast(128), EVP[1])
                h.memset("gpsimd", ident, 0.0)
                h.asel(ident, ident, ALU.not_equal, 1.0, 0, [[-1, 128]], 1)
                h.memset("gpsimd", ones_b, 1.0)
                h.memset("gpsimd", maskA[:], 1.0)
                h.asel(maskA[:, 0, :], maskA[:, 0, :], ALU.is_ge, 0.0, 0, [[1, 128]], -1)
                h.asel(maskA[:, 2, :], maskA[:, 2, :], ALU.is_ge, 0.0, 0, [[-1, 128]], 1)
                h.memset("gpsimd", maskB[:], 1.0)
                h.asel(maskB[:], maskB[:], ALU.is_ge, 0.0, 0, [[1, 256]], -1)
                h.asel(maskB[:], maskB[:], ALU.is_ge, 0.0, 128, [[-1, 256]], 1)
                h.memset("gpsimd", sgn[0:64, :], 1.0)
                sg = h.memset("gpsimd", sgn[64:128, :], -1.0)
                io = P.op("gpsimd", lambda e, ctx: e.iota(posi[:], pattern=[[1, LMAX]], base=0, channel_multiplier=0))
                h.act(esink[:], esink[:], AF.Exp, [lds])
                a0 = h.cp("vector", t1[:], posi[:], [io])
                a1 = h.ts("vector", ang[:], t1[:], invf[:, 0:1], ALU.mult, [a0, ld])

                def table(src_dep, shift, dst_dram, sign_ap, evx):
                    d = h.ts("vector", t2[:], ang[:], shift, ALU.add, [src_dep], s2=1.0 / TWO_PI, op1=ALU.mult)
                    d = h.cp("vector", ki[:], t2[:], [d])
                    d = h.cp("vector", t2[:], ki[:], [d])
                    d0 = h.ts("vector", t1[:], ang[:], shift, ALU.add, [d])
                    d = h.stt("vector", res[:], t2[:], -TWO_PI, t1[:], ALU.mult, ALU.add, [d0])
                    d = h.ts("vector", t2[:], res[:], PI, ALU.is_gt, [d], s2=-TWO_PI, op1=ALU.mult)
                    d = h.tt("vector", res[:], res[:], t2[:], ALU.add, [d])
                    d = h.ts("vector", t2[:], res[:], -PI, ALU.is_lt, [d], s2=TWO_PI, op1=ALU.mult)
                    d = h.tt("vector", res[:], res[:], t2[:], ALU.add, [d])
                    d = h.ts("vector", res[:], res[:], PI, ALU.min, [d], s2=-PI, op1=ALU.max)
                    d = h.act(res[:], res[:], AF.Sin, [d])
                    if sign_ap is not None:
                        d = h.ts("vector", res[:], res[:], sign_ap, ALU.mult, [d, sg])
                    l = None
                    for U_, pb in enumerate(posblk):
                        l = h.dma("sync", dst_dram[U_], res[:, pb * BK:(pb + 1) * BK], evx, [d])
                    return l

                st = table(a1, PI / 2, cos_s, None, EVP[2])
                table(st, 0.0, sin_s, sgn[:, 0:1], EVP[3])
                P.emit()

        if 1 in phases:
            with ExitStack() as es:
                w_sb = sb(es, "w_in_sb", [128, 8, DIN], BF16)
                g1b = sb(es, "g1b", [128, D], F32)
                stg = [sb(es, f"stg{i}", [128, 2, TT], F32) for i in range(2)]
                NXB = 2
                xb = [sb(es, f"xb{i}", [128, D], F32) for i in range(NXB)]
                junk = sb(es, "junk1", [128, D], BF16)
                hn = [sb(es, f"hn{i}", [128, D], BF16) for i in range(2)]
                ssq = [sb(es, f"ssq{i}", [128, 1], F32) for i in range(2)]
                rstd = [sb(es, f"rstd{i}", [128, 1], F32) for i in range(2)]
                hT = [sb(es, f"hT{i}", [128, 8, TT], BF16) for i in range(2)]
                RI_ = TT * NTI
                cosb2 = sb(es, "cosb2", [128, RI_], F32)
                sinb2 = sb(es, "sinb2", [128, RI_], F32)
                cosb = [cosb2[:, i * TT:(i + 1) * TT] for i in range(2)]
                sinb = [sinb2[:, i * TT:(i + 1) * TT] for i in range(2)]
                NRT = 3
                rt_a = [sb(es, f"rta{i}", [128, TT], F32) for i in range(NRT)]
                rt_c = [sb(es, f"rtc{i}", [128, TT], F32) for i in range(NRT)]
                NST = 2
                stq = [sb(es, f"stq{i}", [128, 4, TT], BF16) for i in range(NST)]
                stv = [sb(es, f"stv{i}", [128, 1792], BF16) for i in range(2)]
                NPS = 5
                psb = [pstile(es, f"ps1_{i}", [128, 512]) for i in range(NPS)]
                pst = [pstile(es, f"pst_{i}", [128, 8, 128], BF16) for i in range(2)]
                ev_x = EVP[1:1 + NXB]
                ev_tab = EVP[4:6]
                ev_stq = EVP[6:8]
                ev_stv = EVP[8:10]
                ev_stg = EVP[11:13]

                P = new_prog([EVW[1]])
                h = H(P)
                for kc in range(8):
                    for hh in range(4):
                        h.dma("gpsimd", w_sb[:, kc, hh * 2048:(hh + 1) * 2048], w_in_d[kc * 128:(kc + 1) * 128, hh * 2048:(hh + 1) * 2048], EVW[1])
                h.dma("sync", g1b[:], g1_d.partition_broadcast(128), EVP[10])
                P.emit()

                RI = TT * NTI
                assert RI == BK
                v_v1 = v_s.rearrange("(n r) c -> n r c", r=RI)

                v_l = nc.dram_tensor("v_l1", [RI, 1792], BF16, kind="Internal").ap()

                def p1_body(loops):
                    T0 = lambda ctx: ctx["i0"]
                    P0 = lambda ctx: ctx["i0"]
                    stores = []
                    P = new_prog()
                    h = H(P)
                    xb_rd = [None] * NXB
                    hn_rd = [None] * 2
                    rt_rd = [None] * NRT
                    stq_rd = [None] * NST
                    stv_rd = [None] * 2
                    stg_rd = [None] * 2
                    ps_rd = [[] for _ in range(NPS)]
                    pst_rd = [None] * 2
                    ssq_rd = [None] * 2
                    cnt = {"x": 0, "ps": 0, "rt": 0, "stq": 0, "stv": 0, "pst": 0}
                    tl_box = [None]
                    for tl in range(NTI):
                        toff = tl * TT
                        hb = tl % 2
                        hT_w = []
                        for s in range(4):
                            xi = cnt["x"] % NXB
                            cnt["x"] += 1
                            roff = toff + 128 * s
                            xl = h.dma("sync", xb[xi][:], (lambda ctx, k=tl * 4 + s: x8[k][T0(ctx)]), ev_x[xi], [xb_rd[xi]])
                            if tl == 0 and s == 1:
                                h.dma("sync", cosb2[:], (lambda ctx: cos_s[P0(ctx)]), ev_tab[0])
                                tl_box[0] = h.dma("sync", sinb2[:], (lambda ctx: sin_s[P0(ctx)]), ev_tab[0])
                            sp = s % 2
                            sq = h.act(junk[:], xb[xi][:], AF.Square, [xl, ssq_rd[sp]], accum=ssq[sp][:])
                            r1 = h.act(rstd[sp][:], ssq[sp][:], AF.Sqrt, [sq, hn_rd[sp]], scale=1.0 / D, bias=EPS)
                            r2 = h.recip(rstd[sp][:], rstd[sp][:], [r1])
                            ssq_rd[sp] = r2
                            hq = h.stt("vector", hn[sp][:], xb[xi][:], rstd[sp][:, 0:1], g1b[:], ALU.mult, ALU.mult, [r2, xl, hn_rd[sp]])
                            xb_rd[xi] = hq
                            pi_ = cnt["pst"] % 2
                            cnt["pst"] += 1
                            trp = h.trs([(pst[pi_][:, kc, :], hn[sp][:, kc * 128:(kc + 1) * 128]) for kc in range(8)], ident, [hq, pst_rd[pi_]])
                            hn_rd[sp] = trp
                            evc = h.cp("vector", hT[hb][:, :, 128 * s:128 * (s + 1)], pst[pi_][:], [trp])
                            pst_rd[pi_] = evc
                            hT_w.append(evc)

                        def fm_matmul(col, deps):
                            pi = cnt["ps"] % NPS
                            cnt["ps"] += 1
                            m = h.mms([(psb[pi][:], w_sb[:, kc, col:col + 128], hT[hb][:, kc, :], kc == 0, kc == 7) for kc in range(8)], deps + ps_rd[pi])
                            ps_rd[pi] = []
                            return pi, m

                        first = hT_w
                        tl_ = tl_box[0]
                        for c0 in range(0, 34, 4):
                            nch = min(4, 34 - c0)
                            si = cnt["stq"] % NST
                            cnt["stq"] += 1
                            writers = []
                            for j in range(nch):
                                pi, m = fm_matmul(qk_col(c0 + j), first)
                                first = []
                                ri = cnt["rt"] % NRT
                                cnt["rt"] += 1
                                d1 = h.tt("vector", rt_a[ri][0:64, :], psb[pi][64:128, :], sinb2[64:128, hb * TT:(hb + 1) * TT], ALU.mult, [m, tl_, rt_rd[ri]])
                                d2 = h.tt("vector", rt_a[ri][64:128, :], psb[pi][0:64, :], sinb2[0:64, hb * TT:(hb + 1) * TT], ALU.mult, [m, tl_, rt_rd[ri]])
                                d3 = h.tt("vector", rt_c[ri][:], psb[pi][:], cosb2[:, hb * TT:(hb + 1) * TT], ALU.mult, [m, tl_, rt_rd[ri]])
                                ps_rd[pi] = [d3]
                                d4 = h.tt("gpsimd", stq[si][:, j, :], rt_c[ri][:], rt_a[ri][:], ALU.add, [d1, d2, d3, stq_rd[si]])
                                rt_rd[ri] = d4
                                writers.append(d4)
                            for j in range(nch):
                                stq_rd[si] = h.dma("sync", (lambda ctx, c=c0 + j, o=toff: qk_t[c][:, bass.ds(T0(ctx) * BK + o, TT)]), stq[si][:, j, :], ev_stq[si], writers)
                        for c0 in range(0, 16, 2):
                            si = (c0 // 2) % 2
                            writers = []
                            for j in range(2):
                                pi, m = fm_matmul(6144 + 128 * (c0 + j), [])
                                a = h.act(stg[si][:, j, :], psb[pi][:], AF.Sigmoid, [m, stg_rd[si]])
                                ps_rd[pi] = [a]
                                writers.append(a)
                            for j in range(2):
                                stg_rd[si] = h.dma("scalar", (lambda ctx, c=c0 + j, o=toff: gt_t[c][:, bass.ds(T0(ctx) * BK + o, TT)]), stg[si][:, j, :], ev_stg[si], writers)
                        for s in range(4):
                            vi = cnt["stv"] % 2
                            cnt["stv"] += 1
                            writers = []
                            for (wc, width, vc) in V_GROUPS:
                                pi = cnt["ps"] % NPS
                                cnt["ps"] += 1
                                m = h.mms([(psb[pi][:, 0:width], hT[hb][:, kc, 128 * s:128 * (s + 1)], w_sb[:, kc, wc:wc + width], kc == 0, kc == 7) for kc in range(8)], ps_rd[pi])
                                a = h.act(stv[vi][:, vc:vc + width], psb[pi][:, 0:width], AF.Copy, [m, stv_rd[vi]])
                                ps_rd[pi] = [a]
                                writers.append(a)
                            roff = toff + 128 * s
                            stv_rd[vi] = h.dma("scalar", (lambda ctx, o=roff: v_s[bass.ds(T0(ctx) * BK + o, 128), :]), stv[vi][:], ev_stv[vi], writers)
                    P.emit(loops)

                p1_body((NB,))

        if 2 in phases:
            with ExitStack() as es:
                NQ = 2
                qT = [sb(es, f"qT{i}", [128, 4, LMAX], BF16) for i in range(NQ)]
                kT = [sb(es, f"kT{i}", [128, LMAX], BF16) for i in range(NQ)]
                vt = [sb(es, f"vt{i}", [128, LMAX // 128, 128], BF16) for i in range(NQ)]
                NPT = 16
                pT = [sb(es, f"pT{i}", [128, 512], BF16) for i in range(NPT)]
                accOD = sb(es, "accOD", [128, 2, LMAX], F32)
                rdn = [sb(es, f"rdn{i}", [128, 512], F32) for i in range(2)]
                ost = [sb(es, f"ost{i}", [128, 512], BF16) for i in range(2)]
                ostB = [sb(es, f"ostB{i}", [128, LMAX], BF16) for i in range(2)]
                NS = 4
                psS = [pstile(es, f"psS{i}", [128, 512]) for i in range(NS)]
                psO = [pstile(es, f"psO{i}", [128, 512]) for i in range(2)]
                psD = [pstile(es, f"psD{i}", [128, 512]) for i in range(2)]
                ev_q = EVP[0:2]
                ev_ost = EVP[2:4]
                ev_ostB = EVP[4:6]
                JCH = 8

                v_l2 = nc.dram_tensor("v_l2", [LMAX, 1792], BF16, kind="Internal").ap()
                ostA = sb(es, "ostA", [128, 4, LMAX], BF16)

                def p2_body(UF, L, loops):
                    SO = lambda ctx: ctx["U"]
                    stores = []
                    P = new_prog()
                    h = H(P)
                    q_rd = [[] for _ in range(NQ)]
                    pT_rd = [[] for _ in range(NPT)]
                    psS_rd = [None] * NS
                    psO_rd = [None] * 2
                    rdn_rd = [None] * 2
                    ost_rd = [None] * 2
                    ostB_rd = [None] * 2
                    cnt = {"q": 0, "pT": 0, "S": 0, "O": 0, "ost": 0, "ostB": 0}
                    nb = L // 128
                    v_v = v_s.rearrange("(n r) c -> n r c", r=L)
                    cin2 = h.dma("sync", v_l2[0:L, :], (lambda ctx: v_v[SO(ctx)]), EVP[14])
                    ostA_rd = []

                    def row(c):
                        return lambda ctx: qk_t[c][:, bass.ds(SO(ctx) * L, L)]

                    def load_v(dst, src_fn, nbk, ev, deps):
                        l = None
                        for j0 in range(0, nbk, JCH):
                            j1 = min(nbk, j0 + JCH)
                            l = h.dma("sync", dst[:, j0:j1, :], (lambda ctx, j0=j0, j1=j1: src_fn(ctx)[:, j0:j1, :]), ev, deps)
                        return l

                    for hk in range(2):
                        qi = cnt["q"] % NQ
                        cnt["q"] += 1
                        dq = q_rd[qi]
                        for hh_ in range(4):
                            h.dma("sync", qT[qi][:, hh_, 0:L], row(4 * hk + hh_), ev_q[qi], dq + [cin2])
                        h.dma("sync", kT[qi][:, 0:L], row(8 + hk), ev_q[qi], dq)
                        l3 = load_v(vt[qi], (lambda ctx, hk=hk: v_l2[0:L, 128 * hk:128 * hk + 128].rearrange("(j p) c -> p j c", p=128)), nb, ev_q[qi], dq)
                        q_rd[qi] = []
                        pts = {}
                        last_pe = [None]
                        last_d3 = [None]

                        def do_pv_A(i):
                            oi = cnt["O"] % 2
                            cnt["O"] += 1
                            js = [j for j in (i - 1, i, i + 1) if 0 <= j < nb]
                            deps = [pts[(j, hh)][1] for j in js for hh in range(4)] + [psO_rd[oi], l3]
                            lst = []
                            for dst, use_v in ((psO[oi], True), (psD[oi], False)):
                                for hh in range(4):
                                    for n, j in enumerate(js):
                                        pi_, _, qlo = pts[(j, hh)]
                                        blk = i - qlo
                                        lhs = vt[qi][:, j, :] if use_v else ones_b
                                        lst.append((dst[:, 128 * hh:128 * (hh + 1)], lhs, pT[pi_][:, 128 * blk:128 * (blk + 1)], n == 0, n == len(js) - 1))
                            m = h.mms(lst, deps)
                            last_pe[0] = m
                            for j in js:
                                for hh in range(4):
                                    pT_rd[pts[(j, hh)][0]].append(m)
                            d1 = h.tt("vector", rdn[oi][:].rearrange("p (h q) -> p h q", h=4), psD[oi][:].rearrange("p (h q) -> p h q", h=4),
                                      esink[:, 4 * hk:4 * hk + 4].unsqueeze(2).to_broadcast([128, 4, 128]), ALU.add, [m, rdn_rd[oi]])
                            d2 = h.recip(rdn[oi][:], rdn[oi][:], [d1])
                            d3 = h.tt("vector", ostA[:, :, 128 * i:128 * (i + 1)], psO[oi][:].rearrange("p (h q) -> p h q", h=4), rdn[oi][:].rearrange("p (h q) -> p h q", h=4), ALU.mult, [d2] + ostA_rd)
                            psO_rd[oi] = d3
                            rdn_rd[oi] = d3
                            last_d3[0] = d3

                        for j in range(nb):
                            qlo = max(0, j - 1)
                            qhi = min(nb, j + 2)
                            nq = qhi - qlo
                            for hh in range(4):
                                s_i = cnt["S"] % NS
                                cnt["S"] += 1
                                pi_ = cnt["pT"] % NPT
                                cnt["pT"] += 1
                                m = h.mms([(psS[s_i][:, 0:128 * nq], kT[qi][:, 128 * j:128 * (j + 1)], qT[qi][:, hh, 128 * qlo:128 * (qlo + nq)], True, True)], [l3, psS_rd[s_i]])
                                last_pe[0] = m
                                a = h.act(pT[pi_][:, 0:128 * nq], psS[s_i][:, 0:128 * nq], AF.Exp, [m] + pT_rd[pi_], scale=SCALE)
                                pT_rd[pi_] = []
                                psS_rd[s_i] = a
                                rdy = a
                                for qb in range(qlo, qhi):
                                    if qb == j:
                                        continue
                                    mb = 0 if qb == j - 1 else 2
                                    blk = qb - qlo
                                    rdy = h.tt("gpsimd", pT[pi_][:, 128 * blk:128 * (blk + 1)], pT[pi_][:, 128 * blk:128 * (blk + 1)], maskA[:, mb, :], ALU.mult, [rdy])
                                pts[(j, hh)] = (pi_, rdy, qlo)
                            if j >= 2:
                                do_pv_A(j - 2)
                        if nb >= 2:
                            do_pv_A(nb - 2)
                        do_pv_A(nb - 1)
                        q_rd[qi] = [last_pe[0]]
                        ostA_rd = []
                        for hh_ in range(4):
                            ostA_rd.append(h.dma("sync", (lambda ctx, c=4 * hk + hh_: o_t[c][:, bass.ds(SO(ctx) * L, L)]), ostA[:, hh_, 0:L], ev_ost[0], [last_d3[0]]))
                        ostA_rd = [ostA_rd[-1]]

                    acc_guard = None
                    bslots = [psO[0], psO[1], psD[0], psD[1]]
                    bslot_rd = [psO_rd[0], psO_rd[1], psO_rd[0], psO_rd[1]]
                    cnt["OB"] = 0
                    for hb_ in range(4):
                        bi = cnt["ostB"] % 2
                        cnt["ostB"] += 1
                        last_acc = [None, None]
                        for g, dil in enumerate(B_DIL):
                            Lf = L // dil
                            nbf = Lf // 128
                            qi = cnt["q"] % NQ
                            cnt["q"] += 1
                            dq = q_rd[qi]
                            cq = 10 + 8 * g + hb_
                            ck = 10 + 8 * g + 4 + hb_
                            vc = 256 + 512 * g + 128 * hb_
                            h.dma("sync", qT[qi][:, 0, 0:L], row(cq), ev_q[qi], dq)
                            h.dma("sync", kT[qi][:, 0:L], row(ck), ev_q[qi], dq)
                            l3 = None
                            for r in range(dil):
                                l3 = load_v(vt[qi][:, r * nbf:(r + 1) * nbf, :],
                                            (lambda ctx, vc=vc, dil=dil, r=r: v_l2[0:L, vc:vc + 128].rearrange("(j p r) c -> r p j c", p=128, r=dil)[r]),
                                            nbf, ev_q[qi], dq)
                            q_rd[qi] = []
                            last_pe = [None]
                            qv = qT[qi][:, 0, 0:L].rearrange("p (s r) -> p r s", r=dil)
                            kv = kT[qi][:, 0:L].rearrange("p (s r) -> p r s", r=dil)
                            aOD = accOD[:, :, 0:L].rearrange("p t (s r) -> p t r s", r=dil)
                            pts = {}
                            pend = []
                            step = [0]
                            LAG = 2

                            def do_pv_B(r, i):
                                sl = cnt["OB"] % 4
                                cnt["OB"] += 1
                                bank = bslots[sl]
                                contrib = [(i, 64, 128, 0)]
                                if i - 1 >= 0:
                                    contrib.append((i - 1, 192, 64, 0))
                                if i + 1 < nbf:
                                    contrib.append((i + 1, 0, 64, 64))
                                deps = [pts[(r, j)][1] for (j, _, _, _) in contrib] + [bslot_rd[sl], l3]
                                lst = []
                                for doff, use_v in ((0, True), (128, False)):
                                    for n, (j, c0, w, d0) in enumerate(contrib):
                                        pi_, _, clo = pts[(r, j)]
                                        lhs = vt[qi][:, r * nbf + j, :] if use_v else ones_b
                                        lst.append((bank[:, doff + d0:doff + d0 + w], lhs, pT[pi_][:, c0 - clo:c0 - clo + w], n == 0, n == len(contrib) - 1))
                                m = h.mms(lst, deps)
                                last_pe[0] = m
                                for (j, _, _, _) in contrib:
                                    pT_rd[pts[(r, j)][0]].append(m)
                                dsl_ = slice(128 * i, 128 * (i + 1))
                                bankv = bank[:, 0:256].rearrange("p (t q) -> p t q", t=2)
                                if g == 0:
                                    d2 = h.cp("vector", aOD[:, :, r, dsl_], bankv, [m, acc_guard])
                                else:
                                    d2 = h.tt("vector", aOD[:, :, r, dsl_], bankv, aOD[:, :, r, dsl_], ALU.add, [m])
                                bslot_rd[sl] = d2
                                last_acc[0], last_acc[1] = d2, d2

                            for r in range(dil):
                                for j in range(nbf):
                                    q_lo = 128 * j - 64
                                    c_lo = 64 if j == 0 else 0
                                    c_hi = 192 if j == nbf - 1 else 256
                                    nqc = c_hi - c_lo
                                    s_i = cnt["S"] % NS
                                    cnt["S"] += 1
                                    pi_ = cnt["pT"] % NPT
                                    cnt["pT"] += 1
                                    m = h.mms([(psS[s_i][:, 0:nqc], kv[:, r, 128 * j:128 * (j + 1)], qv[:, r, q_lo + c_lo:q_lo + c_lo + nqc], True, True)], [l3, psS_rd[s_i]])
                                    last_pe[0] = m
                                    a = h.act(pT[pi_][:, 0:nqc], psS[s_i][:, 0:nqc], AF.Exp, [m] + pT_rd[pi_], scale=SCALE)
                                    pT_rd[pi_] = []
                                    psS_rd[s_i] = a
                                    mk_ = h.tt("gpsimd", pT[pi_][:, 0:nqc], pT[pi_][:, 0:nqc], maskB[:, c_lo:c_lo + nqc], ALU.mult, [a])
                                    pts[(r, j)] = (pi_, mk_, c_lo)
                                    if j >= 1:
                                        pend.append((r, j - 1, step[0]))
                                    if j == nbf - 1:
                                        pend.append((r, j, step[0]))
                                    step[0] += 1
                                    while pend and pend[0][2] <= step[0] - 1 - LAG:
                                        r_, i_, _ = pend.pop(0)
                                        do_pv_B(r_, i_)
                            while pend:
                                r_, i_, _ = pend.pop(0)
                                do_pv_B(r_, i_)
                            q_rd[qi] = [last_pe[0]]
                        d1 = h.recip(accOD[:, 1, 0:L], accOD[:, 1, 0:L], [last_acc[0], last_acc[1]])
                        d2 = h.tt("gpsimd", ostB[bi][:, 0:L], accOD[:, 0, 0:L], accOD[:, 1, 0:L], ALU.mult, [d1, ostB_rd[bi]])
                        acc_guard = d2
                        ostB_rd[bi] = h.dma("sync", (lambda ctx, c=8 + hb_: o_t[c][:, bass.ds(SO(ctx) * L, L)]), ostB[bi][:, 0:L], ev_ostB[bi], [d2])
                    P.emit(loops, pre=lambda ctx: ctx.__setitem__("U", nc.sync.snap(UF(ctx))))

                for gi, (cg, L) in enumerate(groups):
                    base = gbase[gi]
                    p2_body(lambda ctx, base=base, L=L: base // L + ctx["i0"], L, (cg,))

        if 3 in phases:
            with ExitStack() as es:
                wa = sb(es, "wa", [128, 8, D], BF16)
                wb_ = sb(es, "wb", [128, 4, D], BF16)
                wo = sb(es, "wo", [128, 8, D], BF16)
                g2b = sb(es, "g2b", [128, D], F32)
                oT = [sb(es, f"oT{i}", [128, 12, TT], BF16) for i in range(2)]
                gT = [sb(es, f"gT{i}", [128, 16, TT], F32) for i in range(2)]
                mixT = [sb(es, f"mixT{i}", [128, 8, TT], BF16) for i in range(2)]
                ta = [sb(es, f"ta{i}", [128, TT], F32) for i in range(2)]
                tb = [sb(es, f"tb{i}", [128, TT], F32) for i in range(2)]
                xb3 = [sb(es, f"xb3{i}", [128, D], F32) for i in range(2)]
                nrm = [sb(es, f"nrm{i}", [128, D], F32) for i in range(2)]
                junk3 = sb(es, "junk3", [128, 512], BF16)
                ssa = [sb(es, f"ssa{i}", [128, 2], F32) for i in range(2)]
                rs3 = [sb(es, f"rs3{i}", [128, 1], F32) for i in range(2)]
                psY = [pstile(es, f"psY{i}", [128, 512]) for i in range(4)]
                psM = [pstile(es, f"psM{i}", [128, 2, 512]) for i in range(2)]
                ev_o = EVP[1:3]
                ev_x3 = EVP[3:5]
                ev_x1 = EVP[5:7]
                P = new_prog([EVW[3]])
                h = H(P)
                for kc in range(8):
                    h.dma("gpsimd", wa[:, kc, :], w_a_d[kc * 128:(kc + 1) * 128, :], EVW[3])
                    h.dma("gpsimd", wo[:, kc, :], w_o_d[kc * 128:(kc + 1) * 128, :], EVW[3])
                for kc in range(4):
                    h.dma("gpsimd", wb_[:, kc, :], w_b_d[kc * 128:(kc + 1) * 128, :], EVW[3])
                h.dma("sync", g2b[:], g2_d.partition_broadcast(128), EVP[10])
                P.emit()

                P = new_prog()
                h = H(P)
                T0 = lambda ctx: ctx["i0"]
                R3 = TT * NTI3
                assert R3 == BK
                o_rd = [[] for _ in range(2)]
                mix_rd = [[] for _ in range(2)]
                psY_rd = [None] * 4
                psM_rd = [None] * 2
                tab_rd3 = [None] * 2
                xb3_rd = [None] * 2
                ss_rd = [None] * 2
                cnt = {"Y": 0, "t": 0, "M": 0, "x": 0}
                assert NTI3 == 2
                lgs = []
                for tl in range(NTI3):
                    toff = tl * TT
                    b = tl % 2
                    lg = None
                    for c_ in range(12):
                        lg = h.dma("sync", oT[b][:, c_, :], (lambda ctx, c=c_, o=toff: o_t[c][:, bass.ds(T0(ctx) * BK + o, TT)]), ev_o[b])
                    for c_ in range(16):
                        lg = h.dma("sync", gT[b][:, c_, :], (lambda ctx, c=c_, o=toff: gt_t[c][:, bass.ds(T0(ctx) * BK + o, TT)]), ev_o[b])
                    lgs.append(lg)
                for tl in range(NTI3):
                    toff = tl * TT
                    b = tl % 2
                    lg = lgs[tl]
                    o_rd[b] = []
                    mix_w = []
                    lastr = []
                    for m_ in range(8):
                        ya = cnt["Y"] % 4
                        cnt["Y"] += 1
                        yb = cnt["Y"] % 4
                        cnt["Y"] += 1
                        ma = h.mms([(psY[ya][:], wa[:, kc, 128 * m_:128 * (m_ + 1)], oT[b][:, kc, :], kc == 0, kc == 7) for kc in range(8)], [lg, psY_rd[ya]])
                        mb = h.mms([(psY[yb][:], wb_[:, kc, 128 * m_:128 * (m_ + 1)], oT[b][:, 8 + kc, :], kc == 0, kc == 3) for kc in range(4)], [lg, psY_rd[yb]])
                        tt_ = cnt["t"] % 2
                        cnt["t"] += 1
                        d1 = h.tt("vector", ta[tt_][:], psY[ya][:], gT[b][:, m_, :], ALU.mult, [ma, lg, tab_rd3[tt_]])
                        d2 = h.tt("vector", tb[tt_][:], psY[yb][:], gT[b][:, 8 + m_, :], ALU.mult, [mb, lg, tab_rd3[tt_]])
                        psY_rd[ya] = d1
                        psY_rd[yb] = d2
                        d3 = h.tt("gpsimd", mixT[b][:, m_, :], ta[tt_][:], tb[tt_][:], ALU.add, [d1, d2] + mix_rd[b])
                        tab_rd3[tt_] = d3
                        mix_w.append(d3)
                        lastr = [ma, mb, d2]
                    mix_rd[b] = []
                    o_rd[b] = lastr
                    for s in range(4):
                        mi = cnt["M"] % 2
                        cnt["M"] += 1
                        xi = cnt["x"] % 2
                        cnt["x"] += 1
                        roff = toff + 128 * s
                        xl = h.dma("sync", xb3[xi][:], (lambda ctx, k=tl * 4 + s: x8[k][T0(ctx)]), ev_x3[xi], [xb3_rd[xi]])
                        lst = []
                        for hf in range(2):
                            for kc in range(8):
                                lst.append((psM[mi][:, hf, :], mixT[b][:, kc, 128 * s:128 * (s + 1)], wo[:, kc, 512 * hf:512 * (hf + 1)], kc == 0, kc == 7))
                        mo = h.mms(lst, mix_w + [psM_rd[mi]])
                        mix_w = []
                        mix_rd[b] = [mo]
                        s0 = h.act(junk3[:], psM[mi][:, 0, :], AF.Square, [mo, ss_rd[xi]], accum=ssa[xi][:, 0:1])
                        s1 = h.act(junk3[:], psM[mi][:, 1, :], AF.Square, [mo, s0], accum=ssa[xi][:, 1:2])
                        r1 = h.tt("vector", rs3[xi][:], ssa[xi][:, 0:1], ssa[xi][:, 1:2], ALU.add, [s0, s1, xb3_rd[xi]])
                        r2 = h.act(rs3[xi][:], rs3[xi][:], AF.Sqrt, [r1], scale=1.0 / D, bias=EPS)
                        r3 = h.recip(rs3[xi][:], rs3[xi][:], [r2])
                        ss_rd[xi] = r3
                        n1 = h.act(nrm[xi][:].rearrange("p (h c) -> p h c", h=2), psM[mi][:], AF.Copy, [r3, xb3_rd[xi]], scale=rs3[xi][:, 0:1])
                        psM_rd[mi] = n1
                        n2 = h.tt("vector", nrm[xi][:], nrm[xi][:], g2b[:], ALU.mult, [n1])
                        n3 = h.tt("gpsimd", xb3[xi][:], nrm[xi][:], xb3[xi][:], ALU.add, [n2, xl])
                        xb3_rd[xi] = h.dma("sync", (lambda ctx, k=tl * 4 + s: x1_8[k][T0(ctx)]), xb3[xi][:], ev_x1[xi], [n3])
                P.emit((NT // (TT * NTI3),))

        if 4 in phases:
            with ExitStack() as es:
                w1 = sb(es, "w1", [128, 8, DFF], BF16)
                w2 = sb(es, "w2", [128, 32, D], BF16)
                g3b = sb(es, "g3b", [128, D], F32)
                g4b = sb(es, "g4b", [128, D], F32)
                xa = [sb(es, f"xa{i}", [128, D], F32) for i in range(1)]
                xr = [sb(es, f"xr{i}", [128, D], F32) for i in range(1)]
                junk4 = sb(es, "junk4", [128, D], BF16)
                hn4 = [sb(es, f"hn4{i}", [128, D], BF16) for i in range(1)]
                ssq4 = [sb(es, f"ssq4{i}", [128, 1], F32) for i in range(2)]
                rstd4 = [sb(es, f"rstd4{i}", [128, 1], F32) for i in range(2)]
                h2T = [sb(es, f"h2T{i}", [128, 8, TT], BF16) for i in range(2)]
                uT = sb(es, "uT", [128, 32, TT], BF16)
                ur = [sb(es, f"ur{i}", [128, TT], F32) for i in range(2)]
                nr4 = [sb(es, f"nr4{i}", [128, D], F32) for i in range(1)]
                ssb = [sb(es, f"ssb{i}", [128, 2], F32) for i in range(2)]
                rs4 = [sb(es, f"rs4{i}", [128, 1], F32) for i in range(2)]
                psU = [pstile(es, f"psU{i}", [128, 512]) for i in range(3)]
                psT4 = pstile(es, "psT4", [128, 8, 128], BF16)
                psO4 = [pstile(es, f"psO4{i}", [128, 2, 512]) for i in range(2)]
                ev_xa = EVP[1:3]
                ev_xr = EVP[3:5]
                ev_y = EVP[5:7]
                P = new_prog([EVW[4]])
                h = H(P)
                for kc in range(8):
                    for hh in range(2):
                        h.dma("gpsimd", w1[:, kc, hh * 2048:(hh + 1) * 2048], w_1_d[kc * 128:(kc + 1) * 128, hh * 2048:(hh + 1) * 2048], EVW[4])
                for kc in range(32):
                    h.dma("gpsimd", w2[:, kc, :], w_2_d[kc * 128:(kc + 1) * 128, :], EVW[4])
                h.dma("sync", g3b[:], g3_d.partition_broadcast(128), EVP[10])
                h.dma("sync", g4b[:], g4_d.partition_broadcast(128), EVP[11])
                P.emit()

                P = new_prog()
                h = H(P)
                T0 = lambda ctx: ctx["i0"]
                R4 = TT * NTI
                NTI4 = 4 if NT % (TT * 4) == 0 else 2
                UB = lambda ctx, tl: ctx["i0"] * (NTI4 // 2) + tl // 2
                h2T_rd = [None] * 2
                assert R4 == BK
                xa_rd = [None] * 2
                xr_rd = [None] * 2
                hn_rd = [None] * 2
                ssq_rd = [None] * 2
                uT_rd = []
                psU_rd = [None] * 3
                ur_rd = [None] * 2
                psT_rd = None
                psO_rd = [None] * 2
                ss_rd = [None] * 2
                cnt = {"x": 0, "U": 0, "R": 0, "O": 0, "y": 0}

                def norm_part(tl):
                    nonlocal psT_rd
                    toff = tl * TT
                    b = tl % 2
                    hw = []
                    for s in range(4):
                        xi = 0
                        roff = toff + 128 * s
                        xl = h.dma("sync", xa[xi][:], (lambda ctx, k=(tl % 2) * 4 + s, tl=tl: x1_8[k][UB(ctx, tl)]), ev_xa[xi], [xa_rd[xi]])
                        sq = h.act(junk4[:], xa[xi][:], AF.Square, [xl, ssq_rd[xi]], accum=ssq4[xi][:])
                        r1 = h.act(rstd4[xi][:], ssq4[xi][:], AF.Sqrt, [sq, hn_rd[xi]], scale=1.0 / D, bias=EPS)
                        r2 = h.recip(rstd4[xi][:], rstd4[xi][:], [r1])
                        ssq_rd[xi] = r2
                        hq = h.stt("vector", hn4[xi][:], xa[xi][:], rstd4[xi][:, 0:1], g3b[:], ALU.mult, ALU.mult, [r2, xl, hn_rd[xi]])
                        xa_rd[xi] = hq
                        trp = h.trs([(psT4[:, kc, :], hn4[xi][:, kc * 128:(kc + 1) * 128]) for kc in range(8)], ident, [hq, psT_rd])
                        hn_rd[xi] = trp
                        evc = h.cp("vector", h2T[b][:, :, 128 * s:128 * (s + 1)], psT4[:], [trp, h2T_rd[b]])
                        psT_rd = evc
                        hw.append(evc)
                    return hw

                hw_next = norm_part(0)
                for tl in range(NTI4):
                    toff = tl * TT
                    b = tl % 2
                    hw = hw_next
                    u_w = []
                    first = hw + uT_rd
                    uT_rd = []
                    for m_ in range(32):
                        ui = cnt["U"] % 3
                        cnt["U"] += 1
                        ri = cnt["R"] % 2
                        cnt["R"] += 1
                        m1 = h.mms([(psU[ui][:], w1[:, kc, 128 * m_:128 * (m_ + 1)], h2T[b][:, kc, :], kc == 0, kc == 7) for kc in range(8)], first + [psU_rd[ui]])
                        first = []
                        a = h.act(ur[ri][:], psU[ui][:], AF.Relu, [m1, ur_rd[ri]])
                        psU_rd[ui] = a
                        eng = "gpsimd" if m_ % 2 == 0 else "vector"
                        u2 = h.tt(eng, uT[:, m_, :], ur[ri][:], ur[ri][:], ALU.mult, [a])
                        ur_rd[ri] = u2
                        u_w.append(u2)
                    h2T_rd[b] = m1
                    if tl + 1 < NTI4:
                        hw_next = norm_part(tl + 1)
                    for s in range(4):
                        oi = cnt["O"] % 2
                        cnt["O"] += 1
                        xi = 0
                        roff = toff + 128 * s
                        xl = h.dma("sync", xr[xi][:], (lambda ctx, k=(tl % 2) * 4 + s, tl=tl: x1_8[k][UB(ctx, tl)]), ev_xr[xi], [xr_rd[xi]])
                        lst = []
                        for hf in range(2):
                            for kc in range(32):
                                lst.append((psO4[oi][:, hf, :], uT[:, kc, 128 * s:128 * (s + 1)], w2[:, kc, 512 * hf:512 * (hf + 1)], kc == 0, kc == 31))
                        m2 = h.mms(lst, u_w + [psO_rd[oi]])
                        u_w = []
                        uT_rd = [m2]
                        s0 = h.act(junk4[:, 0:512], psO4[oi][:, 0, :], AF.Square, [m2, ss_rd[xi]], accum=ssb[xi][:, 0:1])
                        s1 = h.act(junk4[:, 512:1024], psO4[oi][:, 1, :], AF.Square, [m2, s0], accum=ssb[xi][:, 1:2])
                        r1 = h.tt("vector", rs4[xi][:], ssb[xi][:, 0:1], ssb[xi][:, 1:2], ALU.add, [s0, s1, xr_rd[xi]])
                        r2 = h.act(rs4[xi][:], rs4[xi][:], AF.Sqrt, [r1], scale=1.0 / D, bias=EPS)
                        r3 = h.recip(rs4[xi][:], rs4[xi][:], [r2])
                        ss_rd[xi] = r3
                        n1 = h.act(nr4[xi][:].rearrange("p (h c) -> p h c", h=2), psO4[oi][:], AF.Copy, [r3, xr_rd[xi]], scale=rs4[xi][:, 0:1])
                        psO_rd[oi] = n1
                        n2 = h.tt("vector", nr4[xi][:], nr4[xi][:], g4b[:], ALU.mult, [n1])
                        n3 = h.tt("gpsimd", xr[xi][:], nr4[xi][:], xr[xi][:], ALU.add, [n2, xl])
                        xr_rd[xi] = h.dma("sync", (lambda ctx, k=(tl % 2) * 4 + s, tl=tl: y8[k][UB(ctx, tl)]), xr[xi][:], ev_y[xi], [n3])
                P.emit((NT // (TT * NTI4),))
    return nc


def _invf():
    half = 64
    inv = np.power(np.float32(10000.0), -np.arange(half, dtype=np.float32) / np.float32(half)).astype(np.float32)
    return np.concatenate([inv, inv]).reshape(128, 1).astype(np.float32)


_CACHE = {}


def kernel(x_prompt, x_sample, w_in, sink, w_a, w_b, w_o, g_pre_mix, g_post_mix, g_pre_mlp, g_post_mlp, w_1, w_2):
    x_prompt = np.asarray(x_prompt, dtype=np.float32)
    x_sample = np.asarray(x_sample, dtype=np.float32)
    if "nc" not in _CACHE:
        _CACHE["nc"] = build_program([(4, 2048), (1, 4096)])
    nc = _CACHE["nc"]
    f = lambda a: np.ascontiguousarray(np.asarray(a, dtype=np.float32))
    shared = {
        "w_in": f(w_in[0]), "sink": f(sink[0]).reshape(1, 8), "w_a": f(w_a[0]), "w_b": f(w_b[0]), "w_o": f(w_o[0]),
        "g_pre_mix": f(g_pre_mix[0]).reshape(1, D), "g_post_mix": f(g_post_mix[0]).reshape(1, D),
        "g_pre_mlp": f(g_pre_mlp[0]).reshape(1, D), "g_post_mlp": f(g_post_mlp[0]).reshape(1, D),
        "w_1": f(w_1[0]), "w_2": f(w_2[0]), "invf": _invf(),
    }
    in_maps = []
    for c in range(N_CORES):
        xc = np.concatenate([x_prompt[4 * c:4 * c + 4].reshape(4 * 2048, D), x_sample[c].reshape(4096, D)], axis=0)
        m = dict(shared)
        xr_ = xc.reshape(-1, 8, 128, D)
        for k in range(8):
            m[f"x{k}"] = np.ascontiguousarray(xr_[:, k])
        in_maps.append(m)
    res = run_bass_kernel_spmd(nc, in_maps, core_ids=list(range(N_CORES)))
    y_prompt = np.empty((32, 2048, D), np.float32)
    y_sample = np.empty((8, 4096, D), np.float32)
    for c in range(N_CORES):
        y = np.stack([res.results[c][f"y{k}"] for k in range(8)], axis=1).reshape(-1, D)
        y_prompt[4 * c:4 * c + 4] = y[:8192].reshape(4, 2048, D)
        y_sample[c] = y[8192:].reshape(4096, D)
    return (y_prompt, y_sample)
```

```python
import math
from contextlib import ExitStack

import numpy as np
import concourse.bass as bass
import concourse.mybir as mybir
from concourse.bass_utils import run_bass_kernel_spmd

F32 = mybir.dt.float32
BF16 = mybir.dt.bfloat16
I32 = mybir.dt.int32
ALU = mybir.AluOpType
AF = mybir.ActivationFunctionType

D = 1024
DIN = 8192
DFF = 4096
HD = 128
EPS = 1e-6
TT = 512
SCALE = HD ** -0.5
PI = float(np.pi)
TWO_PI = float(2 * np.pi)
N_CORES = 8
B_DIL = (1, 4, 16)

ENGS = ["sync", "scalar", "vector", "gpsimd", "tensor"]


class Ev:
    def __init__(self, sem):
        self.sem = sem
        self.count = 0


class Op:
    __slots__ = ("eng", "fn", "deps", "sig", "val", "sem", "kind")

    def __init__(self, eng, fn, deps, kind):
        self.eng, self.fn, self.deps, self.kind = eng, fn, list(deps), kind
        self.sig = False
        self.val = None
        self.sem = None


class Prog:
    def __init__(self, nc, eng_sems, evs, extra=()):
        self.nc = nc
        self.eng_sems = eng_sems
        self.evs = evs
        self.extra = list(extra)
        for ev in evs:
            assert ev.count == 0
        self.ops = {e: [] for e in ENGS}

    def op(self, eng, fn, deps=()):
        o = Op(eng, fn, [d for d in deps if d is not None], "c")
        self.ops[eng].append(o)
        return o

    def dma(self, eng, fn, ev, deps=()):
        o = Op(eng, fn, [d for d in deps if d is not None], "d")
        ev.count += 16
        o.val = ev.count
        o.sem = ev.sem
        o.sig = True
        self.ops[eng].append(o)
        return o

    def emit(self, loops=(), pre=None):
        nc = self.nc
        for e in ENGS:
            for o in self.ops[e]:
                for d in o.deps:
                    d.sig = True
        last = {}
        for e in ENGS:
            c = 0
            for o in self.ops[e]:
                if o.kind == "c":
                    last[e] = o
            if e in last:
                last[e].sig = True
            for o in self.ops[e]:
                if o.kind == "c" and o.sig:
                    c += 1
                    o.val = c
                    o.sem = self.eng_sems[e]
        finals = [(o.sem, o.val) for o in last.values()] + [(ev.sem, ev.count) for ev in self.evs + self.extra if ev.count > 0]
        all_sems = list(self.eng_sems.values()) + [ev.sem for ev in self.evs]

        def run(ctx):
            if pre is not None:
                pre(ctx)
            for ename in ENGS:
                eng = getattr(nc, ename)
                seen = {}

                def wait(sem, val):
                    k = id(sem)
                    if seen.get(k, 0) >= val:
                        return
                    eng.wait_ge(sem, val)
                    seen[k] = val
                for o in self.ops[ename]:
                    for d in o.deps:
                        wait(d.sem, d.val)
                    ins = o.fn(eng, ctx)
                    if o.kind == "d":
                        ins.then_inc(o.sem, 16)
                    elif o.sig:
                        ins.then_inc(o.sem, 1)
                for (s_, v_) in finals:
                    wait(s_, v_)
            nc.all_engine_barrier()
            for s_ in all_sems:
                nc.gpsimd.sem_clear(s_)
            nc.all_engine_barrier()

        if len(loops) == 0:
            run({})
        elif len(loops) == 1:
            with nc.Fori(0, loops[0], hint_back_edge=True) as a0:
                run({"i0": a0})
        else:
            with nc.Fori(0, loops[0]) as a0:
                with nc.Fori(0, loops[1]) as a1:
                    run({"i0": a0, "i1": a1})
        for ev in self.evs:
            ev.count = 0


def dsl(start, size):
    return slice(start, start + size) if isinstance(start, int) else bass.ds(start, size)


def qk_col(ci):
    if ci < 8:
        return 128 * ci
    if ci < 10:
        return 1024 + 128 * (ci - 8)
    g, r = divmod(ci - 10, 8)
    base = 1536 + 1536 * g
    return base + 128 * r if r < 4 else base + 512 + 128 * (r - 4)


V_GROUPS = [(1280, 256, 0), (1536 + 1024, 512, 256), (3072 + 1024, 512, 768), (4608 + 1024, 512, 1280)]


class H:
    def __init__(self, P):
        self.P = P

    def tt(self, eng, out, in0, in1, op, deps=()):
        return self.P.op(eng, lambda e, ctx: e.tensor_tensor(out=out, in0=in0, in1=in1, op=op), deps)

    def ts(self, eng, out, in0, s1, op0, deps=(), s2=None, op1=None):
        if op1 is None:
            return self.P.op(eng, lambda e, ctx: e.tensor_scalar(out=out, in0=in0, scalar1=s1, scalar2=None, op0=op0), deps)
        return self.P.op(eng, lambda e, ctx: e.tensor_scalar(out=out, in0=in0, scalar1=s1, scalar2=s2, op0=op0, op1=op1), deps)

    def stt(self, eng, out, in0, scalar, in1, op0, op1, deps=()):
        return self.P.op(eng, lambda e, ctx: e.scalar_tensor_tensor(out=out, in0=in0, scalar=scalar, in1=in1, op0=op0, op1=op1), deps)

    def act(self, out, in_, func, deps=(), scale=None, bias=None, accum=None):
        kw = {}
        if scale is not None:
            kw["scale"] = scale
        if bias is not None:
            kw["bias"] = bias
        if accum is not None:
            kw["accum_out"] = accum
        return self.P.op("scalar", lambda e, ctx: e.activation(out=out, in_=in_, func=func, **kw), deps)

    def cp(self, eng, out, in_, deps=()):
        return self.P.op(eng, lambda e, ctx: e.tensor_copy(out=out, in_=in_), deps)

    def recip(self, out, in_, deps=()):
        return self.P.op("vector", lambda e, ctx: e.reciprocal(out=out, in_=in_), deps)

    def memset(self, eng, out, val, deps=()):
        return self.P.op(eng, lambda e, ctx: e.memset(out, val), deps)

    def asel(self, out, in_, cmp, fill, base, pattern, cm, deps=()):
        return self.P.op("gpsimd", lambda e, ctx: e.affine_select(out=out, in_=in_, compare_op=cmp, fill=fill, base=base, pattern=pattern, channel_multiplier=cm), deps)

    def dma(self, eng, out, in_, ev, deps=()):
        def f(e, ctx):
            o_ = out(ctx) if callable(out) else out
            i_ = in_(ctx) if callable(in_) else in_
            return e.dma_start(out=o_, in_=i_)
        return self.P.dma(eng, f, ev, deps)

    def mms(self, lst, deps=()):
        lst = list(lst)

        def f(e, ctx):
            ins = None
            for (out, lhsT, rhs, st, sp) in lst:
                ins = e.matmul(out, lhsT=lhsT, rhs=rhs, start=st, stop=sp)
            return ins
        return self.P.op("tensor", f, deps)

    def trs(self, lst, ident, deps=()):
        lst = list(lst)

        def f(e, ctx):
            ins = None
            for (out, in_) in lst:
                ins = e.transpose(out=out, in_=in_, identity=ident)
            return ins
        return self.P.op("tensor", f, deps)


def build_program(groups, debug=False, phases=(0, 1, 2, 3, 4)):
    NT = sum(c * L for c, L in groups)
    LMAX = max(L for _, L in groups)
    NTI = 2
    NTI3 = 4
    assert NT % (TT * NTI3) == 0

    nc = bass.Bass("TRN2", target_bir_lowering=False)
    NB = NT // 1024
    x8 = [nc.dram_tensor(f"x{k}", [NB, 128, D], F32, kind="ExternalInput").ap() for k in range(8)]
    w_in_d = nc.dram_tensor("w_in", [D, DIN], F32, kind="ExternalInput").ap()
    sink_d = nc.dram_tensor("sink", [1, 8], F32, kind="ExternalInput").ap()
    w_a_d = nc.dram_tensor("w_a", [D, D], F32, kind="ExternalInput").ap()
    w_b_d = nc.dram_tensor("w_b", [512, D], F32, kind="ExternalInput").ap()
    w_o_d = nc.dram_tensor("w_o", [D, D], F32, kind="ExternalInput").ap()
    g1_d = nc.dram_tensor("g_pre_mix", [1, D], F32, kind="ExternalInput").ap()
    g2_d = nc.dram_tensor("g_post_mix", [1, D], F32, kind="ExternalInput").ap()
    g3_d = nc.dram_tensor("g_pre_mlp", [1, D], F32, kind="ExternalInput").ap()
    g4_d = nc.dram_tensor("g_post_mlp", [1, D], F32, kind="ExternalInput").ap()
    w_1_d = nc.dram_tensor("w_1", [D, DFF], F32, kind="ExternalInput").ap()
    w_2_d = nc.dram_tensor("w_2", [DFF, D], F32, kind="ExternalInput").ap()
    invf_d = nc.dram_tensor("invf", [128, 1], F32, kind="ExternalInput").ap()
    y8 = [nc.dram_tensor(f"y{k}", [NB, 128, D], F32, kind="ExternalOutput").ap() for k in range(8)]

    skind = "ExternalOutput" if debug else "Internal"
    BK = 1024
    qk_t = [nc.dram_tensor(f"qk{c}", [128, NT], BF16, kind="Internal").ap() for c in range(34)]
    v_s = nc.dram_tensor("v_s", [NT, 1792], BF16, kind=skind).ap()
    gt_t = [nc.dram_tensor(f"gt{c}", [128, NT], F32, kind="Internal").ap() for c in range(16)]
    o_t = [nc.dram_tensor(f"o{c}", [128, NT], BF16, kind="Internal").ap() for c in range(12)]
    x1_8 = [nc.dram_tensor(f"x1_{k}", [NB, 128, D], F32, kind="Internal").ap() for k in range(8)]
    cos_s = nc.dram_tensor("cos_s", [NB, 128, BK], F32, kind="Internal").ap()
    sin_s = nc.dram_tensor("sin_s", [NB, 128, BK], F32, kind="Internal").ap()

    with ExitStack() as gs:
        evs = []
        sem_ctr = [0]

        def new_sem(nm):
            sem_ctr[0] += 1
            return gs.enter_context(nc.semaphore(f"{nm}{sem_ctr[0]}"))

        def new_ev(nm="ev"):
            e = Ev(new_sem(nm))
            evs.append(e)
            return e

        ENG_SEMS = {e: new_sem("p" + e[:2]) for e in ENGS}

        def new_prog(extra=()):
            return Prog(nc, ENG_SEMS, evs, extra)

        EVW = [Ev(new_sem("wld")) for _ in range(5)]

        def sb(es, name, shape, dt):
            return es.enter_context(nc.sbuf_tensor(name, shape, dt))

        def pstile(es, name, shape, dt=F32):
            return es.enter_context(nc.psum_tensor(name, shape, dt))

        EVP = [new_ev() for _ in range(24)]

        ident_t = sb(gs, "ident", [128, 128], BF16)
        ones_t = sb(gs, "ones_b", [128, 128], BF16)
        maskA = sb(gs, "maskA", [128, 3, 128], BF16)
        maskB = sb(gs, "maskB", [128, 256], BF16)
        esink = sb(gs, "esink", [128, 8], F32)
        ident = ident_t[:]
        ones_b = ones_t[:]

        posblk = []
        for (c_, L_) in groups:
            for _ in range(c_):
                posblk += list(range(L_ // BK))
        gbase = []
        o = 0
        for (c, L) in groups:
            gbase.append(o)
            o += c * L

        if 0 in phases:
            with ExitStack() as es:
                P = new_prog()
                h = H(P)
                invf = sb(es, "invf_sb", [128, 1], F32)
                sgn = sb(es, "sgn", [128, 1], F32)
                posi = sb(es, "posi", [128, LMAX], I32)
                ang = sb(es, "ang", [128, LMAX], F32)
                t1 = sb(es, "t1", [128, LMAX], F32)
                t2 = sb(es, "t2", [128, LMAX], F32)
                ki = sb(es, "ki", [128, LMAX], I32)
                res = sb(es, "res", [128, LMAX], F32)
                ld = h.dma("sync", invf[:], invf_d, EVP[0])
                lds = h.dma("sync", esink[:], sink_d.partition_broadcast(128), EVP[1])
                h.memset("gpsimd", ident, 0.0)
                h.asel(ident, ident, ALU.not_equal, 1.0, 0, [[-1, 128]], 1)
                h.memset("gpsimd", ones_b, 1.0)
                h.memset("gpsimd", maskA[:], 1.0)
                h.asel(maskA[:, 0, :], maskA[:, 0, :], ALU.is_ge, 0.0, 0, [[1, 128]], -1)
                h.asel(maskA[:, 2, :], maskA[:, 2, :], ALU.is_ge, 0.0, 0, [[-1, 128]], 1)
                h.memset("gpsimd", maskB[:], 1.0)
                h.asel(maskB[:], maskB[:], ALU.is_ge, 0.0, 0, [[1, 256]], -1)
                h.asel(maskB[:], maskB[:], ALU.is_ge, 0.0, 128, [[-1, 256]], 1)
                h.memset("gpsimd", sgn[0:64, :], 1.0)
                sg = h.memset("gpsimd", sgn[64:128, :], -1.0)
                io = P.op("gpsimd", lambda e, ctx: e.iota(posi[:], pattern=[[1, LMAX]], base=0, channel_multiplier=0))
                h.act(esink[:], esink[:], AF.Exp, [lds])
                a0 = h.cp("vector", t1[:], posi[:], [io])
                a1 = h.ts("vector", ang[:], t1[:], invf[:, 0:1], ALU.mult, [a0, ld])

                def table(src_dep, shift, dst_dram, sign_ap, evx):
                    d = h.ts("vector", t2[:], ang[:], shift, ALU.add, [src_dep], s2=1.0 / TWO_PI, op1=ALU.mult)
                    d = h.cp("vector", ki[:], t2[:], [d])
                    d = h.cp("vector", t2[:], ki[:], [d])
                    d0 = h.ts("vector", t1[:], ang[:], shift, ALU.add, [d])
                    d = h.stt("vector", res[:], t2[:], -TWO_PI, t1[:], ALU.mult, ALU.add, [d0])
                    d = h.ts("vector", t2[:], res[:], PI, ALU.is_gt, [d], s2=-TWO_PI, op1=ALU.mult)
                    d = h.tt("vector", res[:], res[:], t2[:], ALU.add, [d])
                    d = h.ts("vector", t2[:], res[:], -PI, ALU.is_lt, [d], s2=TWO_PI, op1=ALU.mult)
                    d = h.tt("vector", res[:], res[:], t2[:], ALU.add, [d])
                    d = h.ts("vector", res[:], res[:], PI, ALU.min, [d], s2=-PI, op1=ALU.max)
                    d = h.act(res[:], res[:], AF.Sin, [d])
                    if sign_ap is not None:
                        d = h.ts("vector", res[:], res[:], sign_ap, ALU.mult, [d, sg])
                    l = None
                    for U_, pb in enumerate(posblk):
                        l = h.dma("sync", dst_dram[U_], res[:, pb * BK:(pb + 1) * BK], evx, [d])
                    return l

                st = table(a1, PI / 2, cos_s, None, EVP[2])
                table(st, 0.0, sin_s, sgn[:, 0:1], EVP[3])
                P.emit()

        if 1 in phases:
            with ExitStack() as es:
                w_sb = sb(es, "w_in_sb", [128, 8, DIN], BF16)
                g1b = sb(es, "g1b", [128, D], F32)
                stg = [sb(es, f"stg{i}", [128, 2, TT], F32) for i in range(2)]
                NXB = 2
                xb = [sb(es, f"xb{i}", [128, D], F32) for i in range(NXB)]
                junk = sb(es, "junk1", [128, D], BF16)
                hn = [sb(es, f"hn{i}", [128, D], BF16) for i in range(2)]
                ssq = [sb(es, f"ssq{i}", [128, 1], F32) for i in range(2)]
                rstd = [sb(es, f"rstd{i}", [128, 1], F32) for i in range(2)]
                hT = [sb(es, f"hT{i}", [128, 8, TT], BF16) for i in range(2)]
                RI_ = TT * NTI
                cosb2 = sb(es, "cosb2", [128, RI_], F32)
                sinb2 = sb(es, "sinb2", [128, RI_], F32)
                cosb = [cosb2[:, i * TT:(i + 1) * TT] for i in range(2)]
                sinb = [sinb2[:, i * TT:(i + 1) * TT] for i in range(2)]
                NRT = 3
                rt_a = [sb(es, f"rta{i}", [128, TT], F32) for i in range(NRT)]
                rt_c = [sb(es, f"rtc{i}", [128, TT], F32) for i in range(NRT)]
                NST = 2
                stq = [sb(es, f"stq{i}", [128, 4, TT], BF16) for i in range(NST)]
                stv = [sb(es, f"stv{i}", [128, 1792], BF16) for i in range(2)]
                NPS = 5
                psb = [pstile(es, f"ps1_{i}", [128, 512]) for i in range(NPS)]
                pst = [pstile(es, f"pst_{i}", [128, 8, 128], BF16) for i in range(2)]
                ev_x = EVP[1:1 + NXB]
                ev_tab = EVP[4:6]
                ev_stq = EVP[6:8]
                ev_stv = EVP[8:10]
                ev_stg = EVP[11:13]

                P = new_prog([EVW[1]])
                h = H(P)
                for kc in range(8):
                    for hh in range(4):
                        h.dma("gpsimd", w_sb[:, kc, hh * 2048:(hh + 1) * 2048], w_in_d[kc * 128:(kc + 1) * 128, hh * 2048:(hh + 1) * 2048], EVW[1])
                h.dma("sync", g1b[:], g1_d.partition_broadcast(128), EVP[10])
                P.emit()

                RI = TT * NTI
                assert RI == BK
                v_v1 = v_s.rearrange("(n r) c -> n r c", r=RI)

                v_l = nc.dram_tensor("v_l1", [RI, 1792], BF16, kind="Internal").ap()

                def p1_body(loops):
                    T0 = lambda ctx: ctx["i0"]
                    P0 = lambda ctx: ctx["i0"]
                    stores = []
                    P = new_prog()
                    h = H(P)
                    xb_rd = [None] * NXB
                    hn_rd = [None] * 2
                    rt_rd = [None] * NRT
                    stq_rd = [None] * NST
                    stv_rd = [None] * 2
                    stg_rd = [None] * 2
                    ps_rd = [[] for _ in range(NPS)]
                    pst_rd = [None] * 2
                    ssq_rd = [None] * 2
                    cnt = {"x": 0, "ps": 0, "rt": 0, "stq": 0, "stv": 0, "pst": 0}
                    tl_box = [None]
                    for tl in range(NTI):
                        toff = tl * TT
                        hb = tl % 2
                        hT_w = []
                        for s in range(4):
                            xi = cnt["x"] % NXB
                            cnt["x"] += 1
                            roff = toff + 128 * s
                            xl = h.dma("sync", xb[xi][:], (lambda ctx, k=tl * 4 + s: x8[k][T0(ctx)]), ev_x[xi], [xb_rd[xi]])
                            if tl == 0 and s == 1:
                                h.dma("sync", cosb2[:], (lambda ctx: cos_s[P0(ctx)]), ev_tab[0])
                                tl_box[0] = h.dma("sync", sinb2[:], (lambda ctx: sin_s[P0(ctx)]), ev_tab[0])
                            sp = s % 2
                            sq = h.act(junk[:], xb[xi][:], AF.Square, [xl, ssq_rd[sp]], accum=ssq[sp][:])
                            r1 = h.act(rstd[sp][:], ssq[sp][:], AF.Sqrt, [sq, hn_rd[sp]], scale=1.0 / D, bias=EPS)
                            r2 = h.recip(rstd[sp][:], rstd[sp][:], [r1])
                            ssq_rd[sp] = r2
                            hq = h.stt("vector", hn[sp][:], xb[xi][:], rstd[sp][:, 0:1], g1b[:], ALU.mult, ALU.mult, [r2, xl, hn_rd[sp]])
                            xb_rd[xi] = hq
                            pi_ = cnt["pst"] % 2
                            cnt["pst"] += 1
                            trp = h.trs([(pst[pi_][:, kc, :], hn[sp][:, kc * 128:(kc + 1) * 128]) for kc in range(8)], ident, [hq, pst_rd[pi_]])
                            hn_rd[sp] = trp
                            evc = h.cp("vector", hT[hb][:, :, 128 * s:128 * (s + 1)], pst[pi_][:], [trp])
                            pst_rd[pi_] = evc
                            hT_w.append(evc)

                        def fm_matmul(col, deps):
                            pi = cnt["ps"] % NPS
                            cnt["ps"] += 1
                            m = h.mms([(psb[pi][:], w_sb[:, kc, col:col + 128], hT[hb][:, kc, :], kc == 0, kc == 7) for kc in range(8)], deps + ps_rd[pi])
                            ps_rd[pi] = []
                            return pi, m

                        first = hT_w
                        tl_ = tl_box[0]
                        for c0 in range(0, 34, 4):
                            nch = min(4, 34 - c0)
                            si = cnt["stq"] % NST
                            cnt["stq"] += 1
                            writers = []
                            for j in range(nch):
                                pi, m = fm_matmul(qk_col(c0 + j), first)
                                first = []
                                ri = cnt["rt"] % NRT
                                cnt["rt"] += 1
                                d1 = h.tt("vector", rt_a[ri][0:64, :], psb[pi][64:128, :], sinb2[64:128, hb * TT:(hb + 1) * TT], ALU.mult, [m, tl_, rt_rd[ri]])
                                d2 = h.tt("vector", rt_a[ri][64:128, :], psb[pi][0:64, :], sinb2[0:64, hb * TT:(hb + 1) * TT], ALU.mult, [m, tl_, rt_rd[ri]])
                                d3 = h.tt("vector", rt_c[ri][:], psb[pi][:], cosb2[:, hb * TT:(hb + 1) * TT], ALU.mult, [m, tl_, rt_rd[ri]])
                                ps_rd[pi] = [d3]
                                d4 = h.tt("gpsimd", stq[si][:, j, :], rt_c[ri][:], rt_a[ri][:], ALU.add, [d1, d2, d3, stq_rd[si]])
                                rt_rd[ri] = d4
                                writers.append(d4)
                            for j in range(nch):
                                stq_rd[si] = h.dma("sync", (lambda ctx, c=c0 + j, o=toff: qk_t[c][:, bass.ds(T0(ctx) * BK + o, TT)]), stq[si][:, j, :], ev_stq[si], writers)
                        for c0 in range(0, 16, 2):
                            si = (c0 // 2) % 2
                            writers = []
                            for j in range(2):
                                pi, m = fm_matmul(6144 + 128 * (c0 + j), [])
                                a = h.act(stg[si][:, j, :], psb[pi][:], AF.Sigmoid, [m, stg_rd[si]])
                                ps_rd[pi] = [a]
                                writers.append(a)
                            for j in range(2):
                                stg_rd[si] = h.dma("scalar", (lambda ctx, c=c0 + j, o=toff: gt_t[c][:, bass.ds(T0(ctx) * BK + o, TT)]), stg[si][:, j, :], ev_stg[si], writers)
                        for s in range(4):
                            vi = cnt["stv"] % 2
                            cnt["stv"] += 1
                            writers = []
                            for (wc, width, vc) in V_GROUPS:
                                pi = cnt["ps"] % NPS
                                cnt["ps"] += 1
                                m = h.mms([(psb[pi][:, 0:width], hT[hb][:, kc, 128 * s:128 * (s + 1)], w_sb[:, kc, wc:wc + width], kc == 0, kc == 7) for kc in range(8)], ps_rd[pi])
                                a = h.act(stv[vi][:, vc:vc + width], psb[pi][:, 0:width], AF.Copy, [m, stv_rd[vi]])
                                ps_rd[pi] = [a]
                                writers.append(a)
                            roff = toff + 128 * s
                            stv_rd[vi] = h.dma("scalar", (lambda ctx, o=roff: v_s[bass.ds(T0(ctx) * BK + o, 128), :]), stv[vi][:], ev_stv[vi], writers)
                    P.emit(loops)

                p1_body((NB,))

        if 2 in phases:
            with ExitStack() as es:
                NQ = 2
                qT = [sb(es, f"qT{i}", [128, 4, LMAX], BF16) for i in range(NQ)]
                kT = [sb(es, f"kT{i}", [128, LMAX], BF16) for i in range(NQ)]
                vt = [sb(es, f"vt{i}", [128, LMAX // 128, 128], BF16) for i in range(NQ)]
                NPT = 16
                pT = [sb(es, f"pT{i}", [128, 512], BF16) for i in range(NPT)]
                accOD = sb(es, "accOD", [128, 2, LMAX], F32)
                rdn = [sb(es, f"rdn{i}", [128, 512], F32) for i in range(2)]
                ost = [sb(es, f"ost{i}", [128, 512], BF16) for i in range(2)]
                ostB = [sb(es, f"ostB{i}", [128, LMAX], BF16) for i in range(2)]
                NS = 4
                psS = [pstile(es, f"psS{i}", [128, 512]) for i in range(NS)]
                psO = [pstile(es, f"psO{i}", [128, 512]) for i in range(2)]
                psD = [pstile(es, f"psD{i}", [128, 512]) for i in range(2)]
                ev_q = EVP[0:2]
                ev_ost = EVP[2:4]
                ev_ostB = EVP[4:6]
                JCH = 8

                v_l2 = nc.dram_tensor("v_l2", [LMAX, 1792], BF16, kind="Internal").ap()
                ostA = sb(es, "ostA", [128, 4, LMAX], BF16)

                def p2_body(UF, L, loops):
                    SO = lambda ctx: ctx["U"]
                    stores = []
                    P = new_prog()
                    h = H(P)
                    q_rd = [[] for _ in range(NQ)]
                    pT_rd = [[] for _ in range(NPT)]
                    psS_rd = [None] * NS
                    psO_rd = [None] * 2
                    rdn_rd = [None] * 2
                    ost_rd = [None] * 2
                    ostB_rd = [None] * 2
                    cnt = {"q": 0, "pT": 0, "S": 0, "O": 0, "ost": 0, "ostB": 0}
                    nb = L // 128
                    v_v = v_s.rearrange("(n r) c -> n r c", r=L)
                    cin2 = h.dma("sync", v_l2[0:L, :], (lambda ctx: v_v[SO(ctx)]), EVP[14])
                    ostA_rd = []

                    def row(c):
                        return lambda ctx: qk_t[c][:, bass.ds(SO(ctx) * L, L)]

                    def load_v(dst, src_fn, nbk, ev, deps):
                        l = None
                        for j0 in range(0, nbk, JCH):
                            j1 = min(nbk, j0 + JCH)
                            l = h.dma("sync", dst[:, j0:j1, :], (lambda ctx, j0=j0, j1=j1: src_fn(ctx)[:, j0:j1, :]), ev, deps)
                        return l

                    for hk in range(2):
                        qi = cnt["q"] % NQ
                        cnt["q"] += 1
                        dq = q_rd[qi]
                        for hh_ in range(4):
                            h.dma("sync", qT[qi][:, hh_, 0:L], row(4 * hk + hh_), ev_q[qi], dq + [cin2])
                        h.dma("sync", kT[qi][:, 0:L], row(8 + hk), ev_q[qi], dq)
                        l3 = load_v(vt[qi], (lambda ctx, hk=hk: v_l2[0:L, 128 * hk:128 * hk + 128].rearrange("(j p) c -> p j c", p=128)), nb, ev_q[qi], dq)
                        q_rd[qi] = []
                        pts = {}
                        last_pe = [None]
                        last_d3 = [None]

                        def do_pv_A(i):
                            oi = cnt["O"] % 2
                            cnt["O"] += 1
                            js = [j for j in (i - 1, i, i + 1) if 0 <= j < nb]
                            deps = [pts[(j, hh)][1] for j in js for hh in range(4)] + [psO_rd[oi], l3]
                            lst = []
                            for dst, use_v in ((psO[oi], True), (psD[oi], False)):
                                for hh in range(4):
                                    for n, j in enumerate(js):
                                        pi_, _, qlo = pts[(j, hh)]
                                        blk = i - qlo
                                        lhs = vt[qi][:, j, :] if use_v else ones_b
                                        lst.append((dst[:, 128 * hh:128 * (hh + 1)], lhs, pT[pi_][:, 128 * blk:128 * (blk + 1)], n == 0, n == len(js) - 1))
                            m = h.mms(lst, deps)
                            last_pe[0] = m
                            for j in js:
                                for hh in range(4):
                                    pT_rd[pts[(j, hh)][0]].append(m)
                            d1 = h.tt("vector", rdn[oi][:].rearrange("p (h q) -> p h q", h=4), psD[oi][:].rearrange("p (h q) -> p h q", h=4),
                                      esink[:, 4 * hk:4 * hk + 4].unsqueeze(2).to_broadcast([128, 4, 128]), ALU.add, [m, rdn_rd[oi]])
                            d2 = h.recip(rdn[oi][:], rdn[oi][:], [d1])
                            d3 = h.tt("vector", ostA[:, :, 128 * i:128 * (i + 1)], psO[oi][:].rearrange("p (h q) -> p h q", h=4), rdn[oi][:].rearrange("p (h q) -> p h q", h=4), ALU.mult, [d2] + ostA_rd)
                            psO_rd[oi] = d3
                            rdn_rd[oi] = d3
                            last_d3[0] = d3

                        for j in range(nb):
                            qlo = max(0, j - 1)
                            qhi = min(nb, j + 2)
                            nq = qhi - qlo
                            for hh in range(4):
                                s_i = cnt["S"] % NS
                                cnt["S"] += 1
                                pi_ = cnt["pT"] % NPT
                                cnt["pT"] += 1
                                m = h.mms([(psS[s_i][:, 0:128 * nq], kT[qi][:, 128 * j:128 * (j + 1)], qT[qi][:, hh, 128 * qlo:128 * (qlo + nq)], True, True)], [l3, psS_rd[s_i]])
                                last_pe[0] = m
                                a = h.act(pT[pi_][:, 0:128 * nq], psS[s_i][:, 0:128 * nq], AF.Exp, [m] + pT_rd[pi_], scale=SCALE)
                                pT_rd[pi_] = []
                                psS_rd[s_i] = a
                                rdy = a
                                for qb in range(qlo, qhi):
                                    if qb == j:
                                        continue
                                    mb = 0 if qb == j - 1 else 2
                                    blk = qb - qlo
                                    rdy = h.tt("gpsimd", pT[pi_][:, 128 * blk:128 * (blk + 1)], pT[pi_][:, 128 * blk:128 * (blk + 1)], maskA[:, mb, :], ALU.mult, [rdy])
                                pts[(j, hh)] = (pi_, rdy, qlo)
                            if j >= 2:
                                do_pv_A(j - 2)
                        if nb >= 2:
                            do_pv_A(nb - 2)
                        do_pv_A(nb - 1)
                        q_rd[qi] = [last_pe[0]]
                        ostA_rd = []
                        for hh_ in range(4):
                            ostA_rd.append(h.dma("sync", (lambda ctx, c=4 * hk + hh_: o_t[c][:, bass.ds(SO(ctx) * L, L)]), ostA[:, hh_, 0:L], ev_ost[0], [last_d3[0]]))
                        ostA_rd = [ostA_rd[-1]]

                    acc_guard = None
                    bslots = [psO[0], psO[1], psD[0], psD[1]]
                    bslot_rd = [psO_rd[0], psO_rd[1], psO_rd[0], psO_rd[1]]
                    cnt["OB"] = 0
                    for hb_ in range(4):
                        bi = cnt["ostB"] % 2
                        cnt["ostB"] += 1
                        last_acc = [None, None]
                        for g, dil in enumerate(B_DIL):
                            Lf = L // dil
                            nbf = Lf // 128
                            qi = cnt["q"] % NQ
                            cnt["q"] += 1
                            dq = q_rd[qi]
                            cq = 10 + 8 * g + hb_
                            ck = 10 + 8 * g + 4 + hb_
                            vc = 256 + 512 * g + 128 * hb_
                            h.dma("sync", qT[qi][:, 0, 0:L], row(cq), ev_q[qi], dq)
                            h.dma("sync", kT[qi][:, 0:L], row(ck), ev_q[qi], dq)
                            l3 = None
                            for r in range(dil):
                                l3 = load_v(vt[qi][:, r * nbf:(r + 1) * nbf, :],
                                            (lambda ctx, vc=vc, dil=dil, r=r: v_l2[0:L, vc:vc + 128].rearrange("(j p r) c -> r p j c", p=128, r=dil)[r]),
                                            nbf, ev_q[qi], dq)
                            q_rd[qi] = []
                            last_pe = [None]
                            qv = qT[qi][:, 0, 0:L].rearrange("p (s r) -> p r s", r=dil)
                            kv = kT[qi][:, 0:L].rearrange("p (s r) -> p r s", r=dil)
                            aOD = accOD[:, :, 0:L].rearrange("p t (s r) -> p t r s", r=dil)
                            pts = {}
                            pend = []
                            step = [0]
                            LAG = 2

                            def do_pv_B(r, i):
                                sl = cnt["OB"] % 4
                                cnt["OB"] += 1
                                bank = bslots[sl]
                                contrib = [(i, 64, 128, 0)]
                                if i - 1 >= 0:
                                    contrib.append((i - 1, 192, 64, 0))
                                if i + 1 < nbf:
                                    contrib.append((i + 1, 0, 64, 64))
                                deps = [pts[(r, j)][1] for (j, _, _, _) in contrib] + [bslot_rd[sl], l3]
                                lst = []
                                for doff, use_v in ((0, True), (128, False)):
                                    for n, (j, c0, w, d0) in enumerate(contrib):
                                        pi_, _, clo = pts[(r, j)]
                                        lhs = vt[qi][:, r * nbf + j, :] if use_v else ones_b
                                        lst.append((bank[:, doff + d0:doff + d0 + w], lhs, pT[pi_][:, c0 - clo:c0 - clo + w], n == 0, n == len(contrib) - 1))
                                m = h.mms(lst, deps)
                                last_pe[0] = m
                                for (j, _, _, _) in contrib:
                                    pT_rd[pts[(r, j)][0]].append(m)
                                dsl_ = slice(128 * i, 128 * (i + 1))
                                bankv = bank[:, 0:256].rearrange("p (t q) -> p t q", t=2)
                                if g == 0:
                                    d2 = h.cp("vector", aOD[:, :, r, dsl_], bankv, [m, acc_guard])
                                else:
                                    d2 = h.tt("vector", aOD[:, :, r, dsl_], bankv, aOD[:, :, r, dsl_], ALU.add, [m])
                                bslot_rd[sl] = d2
                                last_acc[0], last_acc[1] = d2, d2

                            for r in range(dil):
                                for j in range(nbf):
                                    q_lo = 128 * j - 64
                                    c_lo = 64 if j == 0 else 0
                                    c_hi = 192 if j == nbf - 1 else 256
                                    nqc = c_hi - c_lo
                                    s_i = cnt["S"] % NS
                                    cnt["S"] += 1
                                    pi_ = cnt["pT"] % NPT
                                    cnt["pT"] += 1
                                    m = h.mms([(psS[s_i][:, 0:nqc], kv[:, r, 128 * j:128 * (j + 1)], qv[:, r, q_lo + c_lo:q_lo + c_lo + nqc], True, True)], [l3, psS_rd[s_i]])
                                    last_pe[0] = m
                                    a = h.act(pT[pi_][:, 0:nqc], psS[s_i][:, 0:nqc], AF.Exp, [m] + pT_rd[pi_], scale=SCALE)
                                    pT_rd[pi_] = []
                                    psS_rd[s_i] = a
                                    mk_ = h.tt("gpsimd", pT[pi_][:, 0:nqc], pT[pi_][:, 0:nqc], maskB[:, c_lo:c_lo + nqc], ALU.mult, [a])
                                    pts[(r, j)] = (pi_, mk_, c_lo)
                                    if j >= 1:
                                        pend.append((r, j - 1, step[0]))
                                    if j == nbf - 1:
                                        pend.append((r, j, step[0]))
                                    step[0] += 1
                                    while pend and pend[0][2] <= step[0] - 1 - LAG:
                                        r_, i_, _ = pend.pop(0)
                                        do_pv_B(r_, i_)
                            while pend:
                                r_, i_, _ = pend.pop(0)
                                do_pv_B(r_, i_)
                            q_rd[qi] = [last_pe[0]]
                        d1 = h.recip(accOD[:, 1, 0:L], accOD[:, 1, 0:L], [last_acc[0], last_acc[1]])
                        d2 = h.tt("gpsimd", ostB[bi][:, 0:L], accOD[:, 0, 0:L], accOD[:, 1, 0:L], ALU.mult, [d1, ostB_rd[bi]])
                        acc_guard = d2
                        ostB_rd[bi] = h.dma("sync", (lambda ctx, c=8 + hb_: o_t[c][:, bass.ds(SO(ctx) * L, L)]), ostB[bi][:, 0:L], ev_ostB[bi], [d2])
                    P.emit(loops, pre=lambda ctx: ctx.__setitem__("U", nc.sync.snap(UF(ctx))))

                for gi, (cg, L) in enumerate(groups):
                    base = gbase[gi]
                    p2_body(lambda ctx, base=base, L=L: base // L + ctx["i0"], L, (cg,))

        if 3 in phases:
            with ExitStack() as es:
                wa = sb(es, "wa", [128, 8, D], BF16)
                wb_ = sb(es, "wb", [128, 4, D], BF16)
                wo = sb(es, "wo", [128, 8, D], BF16)
                g2b = sb(es, "g2b", [128, D], F32)
                oT = [sb(es, f"oT{i}", [128, 12, TT], BF16) for i in range(2)]
                gT = [sb(es, f"gT{i}", [128, 16, TT], F32) for i in range(2)]
                mixT = [sb(es, f"mixT{i}", [128, 8, TT], BF16) for i in range(2)]
                ta = [sb(es, f"ta{i}", [128, TT], F32) for i in range(2)]
                tb = [sb(es, f"tb{i}", [128, TT], F32) for i in range(2)]
                xb3 = [sb(es, f"xb3{i}", [128, D], F32) for i in range(2)]
                nrm = [sb(es, f"nrm{i}", [128, D], F32) for i in range(2)]
                junk3 = sb(es, "junk3", [128, 512], BF16)
                ssa = [sb(es, f"ssa{i}", [128, 2], F32) for i in range(2)]
                rs3 = [sb(es, f"rs3{i}", [128, 1], F32) for i in range(2)]
                psY = [pstile(es, f"psY{i}", [128, 512]) for i in range(4)]
                psM = [pstile(es, f"psM{i}", [128, 2, 512]) for i in range(2)]
                ev_o = EVP[1:3]
                ev_x3 = EVP[3:5]
                ev_x1 = EVP[5:7]
                P = new_prog([EVW[3]])
                h = H(P)
                for kc in range(8):
                    h.dma("gpsimd", wa[:, kc, :], w_a_d[kc * 128:(kc + 1) * 128, :], EVW[3])
                    h.dma("gpsimd", wo[:, kc, :], w_o_d[kc * 128:(kc + 1) * 128, :], EVW[3])
                for kc in range(4):
                    h.dma("gpsimd", wb_[:, kc, :], w_b_d[kc * 128:(kc + 1) * 128, :], EVW[3])
                h.dma("sync", g2b[:], g2_d.partition_broadcast(128), EVP[10])
                P.emit()

                P = new_prog()
                h = H(P)
                T0 = lambda ctx: ctx["i0"]
                U3 = lambda ctx, tl: ctx["i0"] * (NTI3 // 2) + tl // 2

                def load_tile(tl, deps):
                    b = tl % 2
                    o = (tl % 2) * TT
                    l = None
                    for c_ in range(12):
                        l = h.dma("sync", oT[b][:, c_, :], (lambda ctx, c=c_, o=o, tl=tl: o_t[c][:, bass.ds(U3(ctx, tl) * BK + o, TT)]), ev_o[b], deps)
                    for c_ in range(16):
                        l = h.dma("sync", gT[b][:, c_, :], (lambda ctx, c=c_, o=o, tl=tl: gt_t[c][:, bass.ds(U3(ctx, tl) * BK + o, TT)]), ev_o[b], deps)
                    return l
                o_rd = [[] for _ in range(2)]
                mix_rd = [[] for _ in range(2)]
                psY_rd = [None] * 4
                psM_rd = [None] * 2
                tab_rd3 = [None] * 2
                xb3_rd = [None] * 2
                ss_rd = [None] * 2
                cnt = {"Y": 0, "t": 0, "M": 0, "x": 0}
                lgs = [load_tile(0, []), load_tile(1, [])] + [None] * (NTI3 - 2)
                for tl in range(NTI3):
                    toff = tl * TT
                    b = tl % 2
                    lg = lgs[tl]
                    o_rd[b] = []
                    mix_w = []
                    lastr = []
                    for m_ in range(8):
                        ya = cnt["Y"] % 4
                        cnt["Y"] += 1
                        yb = cnt["Y"] % 4
                        cnt["Y"] += 1
                        ma = h.mms([(psY[ya][:], wa[:, kc, 128 * m_:128 * (m_ + 1)], oT[b][:, kc, :], kc == 0, kc == 7) for kc in range(8)], [lg, psY_rd[ya]])
                        mb = h.mms([(psY[yb][:], wb_[:, kc, 128 * m_:128 * (m_ + 1)], oT[b][:, 8 + kc, :], kc == 0, kc == 3) for kc in range(4)], [lg, psY_rd[yb]])
                        tt_ = cnt["t"] % 2
                        cnt["t"] += 1
                        d1 = h.tt("vector", ta[tt_][:], psY[ya][:], gT[b][:, m_, :], ALU.mult, [ma, lg, tab_rd3[tt_]])
                        d2 = h.tt("vector", tb[tt_][:], psY[yb][:], gT[b][:, 8 + m_, :], ALU.mult, [mb, lg, tab_rd3[tt_]])
                        psY_rd[ya] = d1
                        psY_rd[yb] = d2
                        d3 = h.tt("gpsimd", mixT[b][:, m_, :], ta[tt_][:], tb[tt_][:], ALU.add, [d1, d2] + mix_rd[b])
                        tab_rd3[tt_] = d3
                        mix_w.append(d3)
                        lastr = [ma, mb, d2]
                    mix_rd[b] = []
                    o_rd[b] = lastr
                    if tl + 2 < NTI3:
                        lgs[tl + 2] = load_tile(tl + 2, lastr)
                    for s in range(4):
                        mi = cnt["M"] % 2
                        cnt["M"] += 1
                        xi = cnt["x"] % 2
                        cnt["x"] += 1
                        roff = toff + 128 * s
                        xl = h.dma("sync", xb3[xi][:], (lambda ctx, k=(tl % 2) * 4 + s, tl=tl: x8[k][U3(ctx, tl)]), ev_x3[xi], [xb3_rd[xi]])
                        lst = []
                        for hf in range(2):
                            for kc in range(8):
                                lst.append((psM[mi][:, hf, :], mixT[b][:, kc, 128 * s:128 * (s + 1)], wo[:, kc, 512 * hf:512 * (hf + 1)], kc == 0, kc == 7))
                        mo = h.mms(lst, mix_w + [psM_rd[mi]])
                        mix_w = []
                        mix_rd[b] = [mo]
                        s0 = h.act(junk3[:], psM[mi][:, 0, :], AF.Square, [mo, ss_rd[xi]], accum=ssa[xi][:, 0:1])
                        s1 = h.act(junk3[:], psM[mi][:, 1, :], AF.Square, [mo, s0], accum=ssa[xi][:, 1:2])
                        r1 = h.tt("vector", rs3[xi][:], ssa[xi][:, 0:1], ssa[xi][:, 1:2], ALU.add, [s0, s1, xb3_rd[xi]])
                        r2 = h.act(rs3[xi][:], rs3[xi][:], AF.Sqrt, [r1], scale=1.0 / D, bias=EPS)
                        r3 = h.recip(rs3[xi][:], rs3[xi][:], [r2])
                        ss_rd[xi] = r3
                        n1 = h.act(nrm[xi][:].rearrange("p (h c) -> p h c", h=2), psM[mi][:], AF.Copy, [r3, xb3_rd[xi]], scale=rs3[xi][:, 0:1])
                        psM_rd[mi] = n1
                        n2 = h.tt("vector", nrm[xi][:], nrm[xi][:], g2b[:], ALU.mult, [n1])
                        n3 = h.tt("gpsimd", xb3[xi][:], nrm[xi][:], xb3[xi][:], ALU.add, [n2, xl])
                        xb3_rd[xi] = h.dma("sync", (lambda ctx, k=(tl % 2) * 4 + s, tl=tl: x1_8[k][U3(ctx, tl)]), xb3[xi][:], ev_x1[xi], [n3])
                P.emit((NT // (TT * NTI3),))

        if 4 in phases:
            with ExitStack() as es:
                w1 = sb(es, "w1", [128, 8, DFF], BF16)
                w2 = sb(es, "w2", [128, 32, D], BF16)
                g3b = sb(es, "g3b", [128, D], F32)
                g4b = sb(es, "g4b", [128, D], F32)
                xa = [sb(es, f"xa{i}", [128, D], F32) for i in range(1)]
                xr = [sb(es, f"xr{i}", [128, D], F32) for i in range(1)]
                junk4 = sb(es, "junk4", [128, D], BF16)
                hn4 = [sb(es, f"hn4{i}", [128, D], BF16) for i in range(1)]
                ssq4 = [sb(es, f"ssq4{i}", [128, 1], F32) for i in range(2)]
                rstd4 = [sb(es, f"rstd4{i}", [128, 1], F32) for i in range(2)]
                h2T = [sb(es, f"h2T{i}", [128, 8, TT], BF16) for i in range(2)]
                uT = sb(es, "uT", [128, 32, TT], BF16)
                ur = [sb(es, f"ur{i}", [128, TT], F32) for i in range(2)]
                nr4 = [sb(es, f"nr4{i}", [128, D], F32) for i in range(1)]
                ssb = [sb(es, f"ssb{i}", [128, 2], F32) for i in range(2)]
                rs4 = [sb(es, f"rs4{i}", [128, 1], F32) for i in range(2)]
                psU = [pstile(es, f"psU{i}", [128, 512]) for i in range(3)]
                psT4 = pstile(es, "psT4", [128, 8, 128], BF16)
                psO4 = [pstile(es, f"psO4{i}", [128, 2, 512]) for i in range(2)]
                ev_xa = EVP[1:3]
                ev_xr = EVP[3:5]
                ev_y = EVP[5:7]
                P = new_prog([EVW[4]])
                h = H(P)
                for kc in range(8):
                    for hh in range(2):
                        h.dma("gpsimd", w1[:, kc, hh * 2048:(hh + 1) * 2048], w_1_d[kc * 128:(kc + 1) * 128, hh * 2048:(hh + 1) * 2048], EVW[4])
                for kc in range(32):
                    h.dma("gpsimd", w2[:, kc, :], w_2_d[kc * 128:(kc + 1) * 128, :], EVW[4])
                h.dma("sync", g3b[:], g3_d.partition_broadcast(128), EVP[10])
                h.dma("sync", g4b[:], g4_d.partition_broadcast(128), EVP[11])
                P.emit()

                P = new_prog()
                h = H(P)
                T0 = lambda ctx: ctx["i0"]
                R4 = TT * NTI
                NTI4 = 4 if NT % (TT * 4) == 0 else 2
                UB = lambda ctx, tl: ctx["i0"] * (NTI4 // 2) + tl // 2
                h2T_rd = [None] * 2
                assert R4 == BK
                xa_rd = [None] * 2
                xr_rd = [None] * 2
                hn_rd = [None] * 2
                ssq_rd = [None] * 2
                uT_rd = []
                psU_rd = [None] * 3
                ur_rd = [None] * 2
                psT_rd = None
                psO_rd = [None] * 2
                ss_rd = [None] * 2
                cnt = {"x": 0, "U": 0, "R": 0, "O": 0, "y": 0}

                def norm_sub(tl, s):
                    nonlocal psT_rd
                    b = tl % 2
                    if True:
                        xi = 0
                        xl = h.dma("sync", xa[xi][:], (lambda ctx, k=(tl % 2) * 4 + s, tl=tl: x1_8[k][UB(ctx, tl)]), ev_xa[xi], [xa_rd[xi]])
                        sq = h.act(junk4[:], xa[xi][:], AF.Square, [xl, ssq_rd[xi]], accum=ssq4[xi][:])
                        r1 = h.act(rstd4[xi][:], ssq4[xi][:], AF.Sqrt, [sq, hn_rd[xi]], scale=1.0 / D, bias=EPS)
                        r2 = h.recip(rstd4[xi][:], rstd4[xi][:], [r1])
                        ssq_rd[xi] = r2
                        hq = h.stt("vector", hn4[xi][:], xa[xi][:], rstd4[xi][:, 0:1], g3b[:], ALU.mult, ALU.mult, [r2, xl, hn_rd[xi]])
                        xa_rd[xi] = hq
                        trp = h.trs([(psT4[:, kc, :], hn4[xi][:, kc * 128:(kc + 1) * 128]) for kc in range(8)], ident, [hq, psT_rd])
                        hn_rd[xi] = trp
                        evc = h.cp("vector", h2T[b][:, :, 128 * s:128 * (s + 1)], psT4[:], [trp, h2T_rd[b]])
                        psT_rd = evc
                        return evc

                hw_next = [norm_sub(0, s_) for s_ in range(4)]
                for tl in range(NTI4):
                    toff = tl * TT
                    b = tl % 2
                    hw = hw_next
                    u_w = []
                    first = hw + uT_rd
                    uT_rd = []
                    for m_ in range(32):
                        ui = cnt["U"] % 3
                        cnt["U"] += 1
                        ri = cnt["R"] % 2
                        cnt["R"] += 1
                        m1 = h.mms([(psU[ui][:], w1[:, kc, 128 * m_:128 * (m_ + 1)], h2T[b][:, kc, :], kc == 0, kc == 7) for kc in range(8)], first + [psU_rd[ui]])
                        first = []
                        a = h.act(ur[ri][:], psU[ui][:], AF.Relu, [m1, ur_rd[ri]])
                        psU_rd[ui] = a
                        eng = "gpsimd" if m_ % 2 == 0 else "vector"
                        u2 = h.tt(eng, uT[:, m_, :], ur[ri][:], ur[ri][:], ALU.mult, [a])
                        ur_rd[ri] = u2
                        u_w.append(u2)
                    h2T_rd[b] = m1
                    hw_new = []
                    for s in range(4):
                        oi = cnt["O"] % 2
                        cnt["O"] += 1
                        xi = 0
                        roff = toff + 128 * s
                        xl = h.dma("sync", xr[xi][:], (lambda ctx, k=(tl % 2) * 4 + s, tl=tl: x1_8[k][UB(ctx, tl)]), ev_xr[xi], [xr_rd[xi]])
                        lst = []
                        for hf in range(2):
                            for kc in range(32):
                                lst.append((psO4[oi][:, hf, :], uT[:, kc, 128 * s:128 * (s + 1)], w2[:, kc, 512 * hf:512 * (hf + 1)], kc == 0, kc == 31))
                        m2 = h.mms(lst, u_w + [psO_rd[oi]])
                        u_w = []
                        uT_rd = [m2]
                        if tl + 1 < NTI4:
                            hw_new.append(norm_sub(tl + 1, s))
                        s0 = h.act(junk4[:, 0:512], psO4[oi][:, 0, :], AF.Square, [m2, ss_rd[xi]], accum=ssb[xi][:, 0:1])
                        s1 = h.act(junk4[:, 512:1024], psO4[oi][:, 1, :], AF.Square, [m2, s0], accum=ssb[xi][:, 1:2])
                        r1 = h.tt("vector", rs4[xi][:], ssb[xi][:, 0:1], ssb[xi][:, 1:2], ALU.add, [s0, s1, xr_rd[xi]])
                        r2 = h.act(rs4[xi][:], rs4[xi][:], AF.Sqrt, [r1], scale=1.0 / D, bias=EPS)
                        r3 = h.recip(rs4[xi][:], rs4[xi][:], [r2])
                        ss_rd[xi] = r3
                        n1 = h.act(nr4[xi][:].rearrange("p (h c) -> p h c", h=2), psO4[oi][:], AF.Copy, [r3, xr_rd[xi]], scale=rs4[xi][:, 0:1])
                        psO_rd[oi] = n1
                        n2 = h.tt("vector", nr4[xi][:], nr4[xi][:], g4b[:], ALU.mult, [n1])
                        n3 = h.tt("gpsimd", xr[xi][:], nr4[xi][:], xr[xi][:], ALU.add, [n2, xl])
                        xr_rd[xi] = h.dma("sync", (lambda ctx, k=(tl % 2) * 4 + s, tl=tl: y8[k][UB(ctx, tl)]), xr[xi][:], ev_y[xi], [n3])
                    if tl + 1 < NTI4:
                        hw_next = hw_new
                P.emit((NT // (TT * NTI4),))
    return nc


def _invf():
    half = 64
    inv = np.power(np.float32(10000.0), -np.arange(half, dtype=np.float32) / np.float32(half)).astype(np.float32)
    return np.concatenate([inv, inv]).reshape(128, 1).astype(np.float32)


_CACHE = {}


def kernel(x_prompt, x_sample, w_in, sink, w_a, w_b, w_o, g_pre_mix, g_post_mix, g_pre_mlp, g_post_mlp, w_1, w_2):
    x_prompt = np.asarray(x_prompt, dtype=np.float32)
    x_sample = np.asarray(x_sample, dtype=np.float32)
    if "nc" not in _CACHE:
        _CACHE["nc"] = build_program([(4, 2048), (1, 4096)])
    nc = _CACHE["nc"]
    f = lambda a: np.ascontiguousarray(np.asarray(a, dtype=np.float32))
    shared = {
        "w_in": f(w_in[0]), "sink": f(sink[0]).reshape(1, 8), "w_a": f(w_a[0]), "w_b": f(w_b[0]), "w_o": f(w_o[0]),
        "g_pre_mix": f(g_pre_mix[0]).reshape(1, D), "g_post_mix": f(g_post_mix[0]).reshape(1, D),
        "g_pre_mlp": f(g_pre_mlp[0]).reshape(1, D), "g_post_mlp": f(g_post_mlp[0]).reshape(1, D),
        "w_1": f(w_1[0]), "w_2": f(w_2[0]), "invf": _invf(),
    }
    in_maps = []
    for c in range(N_CORES):
        xc = np.concatenate([x_prompt[4 * c:4 * c + 4].reshape(4 * 2048, D), x_sample[c].reshape(4096, D)], axis=0)
        m = dict(shared)
        xr_ = xc.reshape(-1, 8, 128, D)
        for k in range(8):
            m[f"x{k}"] = np.ascontiguousarray(xr_[:, k])
        in_maps.append(m)
    res = run_bass_kernel_spmd(nc, in_maps, core_ids=list(range(N_CORES)))
    y_prompt = np.empty((32, 2048, D), np.float32)
    y_sample = np.empty((8, 4096, D), np.float32)
    for c in range(N_CORES):
        y = np.stack([res.results[c][f"y{k}"] for k in range(8)], axis=1).reshape(-1, D)
        y_prompt[4 * c:4 * c + 4] = y[:8192].reshape(4, 2048, D)
        y_sample[c] = y[8192:].reshape(4096, D)
    return (y_prompt, y_sample)
```

```python
import math
from contextlib import ExitStack

import numpy as np
import concourse.bass as bass
import concourse.mybir as mybir
from concourse.bass_utils import run_bass_kernel_spmd

F32 = mybir.dt.float32
BF16 = mybir.dt.bfloat16
I32 = mybir.dt.int32
ALU = mybir.AluOpType
AF = mybir.ActivationFunctionType

D = 1024
DIN = 8192
DFF = 4096
HD = 128
EPS = 1e-6
TT = 512
SCALE = HD ** -0.5
PI = float(np.pi)
TWO_PI = float(2 * np.pi)
N_CORES = 8
B_DIL = (1, 4, 16)

ENGS = ["sync", "scalar", "vector", "gpsimd", "tensor"]


class Ev:
    def __init__(self, sem):
        self.sem = sem
        self.count = 0


class Op:
    __slots__ = ("eng", "fn", "deps", "sig", "val", "sem", "kind")

    def __init__(self, eng, fn, deps, kind):
        self.eng, self.fn, self.deps, self.kind = eng, fn, list(deps), kind
        self.sig = False
        self.val = None
        self.sem = None


class Prog:
    def __init__(self, nc, eng_sems, evs, extra=()):
        self.nc = nc
        self.eng_sems = eng_sems
        self.evs = evs
        self.extra = list(extra)
        for ev in evs:
            assert ev.count == 0
        self.ops = {e: [] for e in ENGS}

    def op(self, eng, fn, deps=()):
        o = Op(eng, fn, [d for d in deps if d is not None], "c")
        self.ops[eng].append(o)
        return o

    def dma(self, eng, fn, ev, deps=()):
        o = Op(eng, fn, [d for d in deps if d is not None], "d")
        ev.count += 16
        o.val = ev.count
        o.sem = ev.sem
        o.sig = True
        self.ops[eng].append(o)
        return o

    def emit(self, loops=(), pre=None):
        nc = self.nc
        for e in ENGS:
            for o in self.ops[e]:
                for d in o.deps:
                    d.sig = True
        last = {}
        for e in ENGS:
            c = 0
            for o in self.ops[e]:
                if o.kind == "c":
                    last[e] = o
            if e in last:
                last[e].sig = True
            for o in self.ops[e]:
                if o.kind == "c" and o.sig:
                    c += 1
                    o.val = c
                    o.sem = self.eng_sems[e]
        finals = [(o.sem, o.val) for o in last.values()] + [(ev.sem, ev.count) for ev in self.evs + self.extra if ev.count > 0]
        all_sems = list(self.eng_sems.values()) + [ev.sem for ev in self.evs]

        def run(ctx):
            if pre is not None:
                pre(ctx)
            for ename in ENGS:
                eng = getattr(nc, ename)
                seen = {}

                def wait(sem, val):
                    k = id(sem)
                    if seen.get(k, 0) >= val:
                        return
                    eng.wait_ge(sem, val)
                    seen[k] = val
                for o in self.ops[ename]:
                    for d in o.deps:
                        wait(d.sem, d.val)
                    ins = o.fn(eng, ctx)
                    if o.kind == "d":
                        ins.then_inc(o.sem, 16)
                    elif o.sig:
                        ins.then_inc(o.sem, 1)
                for (s_, v_) in finals:
                    wait(s_, v_)
            nc.all_engine_barrier()
            for s_ in all_sems:
                nc.gpsimd.sem_clear(s_)
            nc.all_engine_barrier()

        if len(loops) == 0:
            run({})
        elif len(loops) == 1:
            with nc.Fori(0, loops[0], hint_back_edge=True) as a0:
                run({"i0": a0})
        else:
            with nc.Fori(0, loops[0]) as a0:
                with nc.Fori(0, loops[1]) as a1:
                    run({"i0": a0, "i1": a1})
        for ev in self.evs:
            ev.count = 0


def dsl(start, size):
    return slice(start, start + size) if isinstance(start, int) else bass.ds(start, size)


def qk_col(ci):
    if ci < 8:
        return 128 * ci
    if ci < 10:
        return 1024 + 128 * (ci - 8)
    g, r = divmod(ci - 10, 8)
    base = 1536 + 1536 * g
    return base + 128 * r if r < 4 else base + 512 + 128 * (r - 4)


V_GROUPS = [(1280, 256, 0), (1536 + 1024, 512, 256), (3072 + 1024, 512, 768), (4608 + 1024, 512, 1280)]


class H:
    def __init__(self, P):
        self.P = P

    def tt(self, eng, out, in0, in1, op, deps=()):
        return self.P.op(eng, lambda e, ctx: e.tensor_tensor(out=out, in0=in0, in1=in1, op=op), deps)

    def ts(self, eng, out, in0, s1, op0, deps=(), s2=None, op1=None):
        if op1 is None:
            return self.P.op(eng, lambda e, ctx: e.tensor_scalar(out=out, in0=in0, scalar1=s1, scalar2=None, op0=op0), deps)
        return self.P.op(eng, lambda e, ctx: e.tensor_scalar(out=out, in0=in0, scalar1=s1, scalar2=s2, op0=op0, op1=op1), deps)

    def stt(self, eng, out, in0, scalar, in1, op0, op1, deps=()):
        return self.P.op(eng, lambda e, ctx: e.scalar_tensor_tensor(out=out, in0=in0, scalar=scalar, in1=in1, op0=op0, op1=op1), deps)

    def act(self, out, in_, func, deps=(), scale=None, bias=None, accum=None):
        kw = {}
        if scale is not None:
            kw["scale"] = scale
        if bias is not None:
            kw["bias"] = bias
        if accum is not None:
            kw["accum_out"] = accum
        return self.P.op("scalar", lambda e, ctx: e.activation(out=out, in_=in_, func=func, **kw), deps)

    def cp(self, eng, out, in_, deps=()):
        return self.P.op(eng, lambda e, ctx: e.tensor_copy(out=out, in_=in_), deps)

    def recip(self, out, in_, deps=()):
        return self.P.op("vector", lambda e, ctx: e.reciprocal(out=out, in_=in_), deps)

    def memset(self, eng, out, val, deps=()):
        return self.P.op(eng, lambda e, ctx: e.memset(out, val), deps)

    def asel(self, out, in_, cmp, fill, base, pattern, cm, deps=()):
        return self.P.op("gpsimd", lambda e, ctx: e.affine_select(out=out, in_=in_, compare_op=cmp, fill=fill, base=base, pattern=pattern, channel_multiplier=cm), deps)

    def dma(self, eng, out, in_, ev, deps=()):
        def f(e, ctx):
            o_ = out(ctx) if callable(out) else out
            i_ = in_(ctx) if callable(in_) else in_
            return e.dma_start(out=o_, in_=i_)
        return self.P.dma(eng, f, ev, deps)

    def mms(self, lst, deps=()):
        lst = list(lst)

        def f(e, ctx):
            ins = None
            for (out, lhsT, rhs, st, sp) in lst:
                ins = e.matmul(out, lhsT=lhsT, rhs=rhs, start=st, stop=sp)
            return ins
        return self.P.op("tensor", f, deps)

    def trs(self, lst, ident, deps=()):
        lst = list(lst)

        def f(e, ctx):
            ins = None
            for (out, in_) in lst:
                ins = e.transpose(out=out, in_=in_, identity=ident)
            return ins
        return self.P.op("tensor", f, deps)


def build_program(groups, debug=False, phases=(0, 1, 2, 3, 4)):
    NT = sum(c * L for c, L in groups)
    LMAX = max(L for _, L in groups)
    NTI = 2
    NTI3 = 4
    assert NT % (TT * NTI3) == 0

    nc = bass.Bass("TRN2", target_bir_lowering=False)
    NB = NT // 1024
    x8 = [nc.dram_tensor(f"x{k}", [NB, 128, D], F32, kind="ExternalInput").ap() for k in range(8)]
    w_in_d = nc.dram_tensor("w_in", [D, DIN], F32, kind="ExternalInput").ap()
    sink_d = nc.dram_tensor("sink", [1, 8], F32, kind="ExternalInput").ap()
    w_a_d = nc.dram_tensor("w_a", [D, D], F32, kind="ExternalInput").ap()
    w_b_d = nc.dram_tensor("w_b", [512, D], F32, kind="ExternalInput").ap()
    w_o_d = nc.dram_tensor("w_o", [D, D], F32, kind="ExternalInput").ap()
    g1_d = nc.dram_tensor("g_pre_mix", [1, D], F32, kind="ExternalInput").ap()
    g2_d = nc.dram_tensor("g_post_mix", [1, D], F32, kind="ExternalInput").ap()
    g3_d = nc.dram_tensor("g_pre_mlp", [1, D], F32, kind="ExternalInput").ap()
    g4_d = nc.dram_tensor("g_post_mlp", [1, D], F32, kind="ExternalInput").ap()
    w_1_d = nc.dram_tensor("w_1", [D, DFF], F32, kind="ExternalInput").ap()
    w_2_d = nc.dram_tensor("w_2", [DFF, D], F32, kind="ExternalInput").ap()
    invf_d = nc.dram_tensor("invf", [128, 1], F32, kind="ExternalInput").ap()
    y8 = [nc.dram_tensor(f"y{k}", [NB, 128, D], F32, kind="ExternalOutput").ap() for k in range(8)]

    skind = "ExternalOutput" if debug else "Internal"
    BK = 1024
    qk_t = [nc.dram_tensor(f"qk{c}", [128, NT], BF16, kind="Internal").ap() for c in range(34)]
    v_s = nc.dram_tensor("v_s", [NT, 1792], BF16, kind=skind).ap()
    gt_t = [nc.dram_tensor(f"gt{c}", [128, NT], F32, kind="Internal").ap() for c in range(16)]
    o_t = [nc.dram_tensor(f"o{c}", [128, NT], BF16, kind="Internal").ap() for c in range(12)]
    x1_8 = [nc.dram_tensor(f"x1_{k}", [NB, 128, D], F32, kind="Internal").ap() for k in range(8)]
    cos_s = nc.dram_tensor("cos_s", [NB, 128, BK], F32, kind="Internal").ap()
    sin_s = nc.dram_tensor("sin_s", [NB, 128, BK], F32, kind="Internal").ap()

    with ExitStack() as gs:
        evs = []
        sem_ctr = [0]

        def new_sem(nm):
            sem_ctr[0] += 1
            return gs.enter_context(nc.semaphore(f"{nm}{sem_ctr[0]}"))

        def new_ev(nm="ev"):
            e = Ev(new_sem(nm))
            evs.append(e)
            return e

        ENG_SEMS = {e: new_sem("p" + e[:2]) for e in ENGS}

        def new_prog(extra=()):
            return Prog(nc, ENG_SEMS, evs, extra)

        EVW = [Ev(new_sem("wld")) for _ in range(5)]

        def sb(es, name, shape, dt):
            return es.enter_context(nc.sbuf_tensor(name, shape, dt))

        def pstile(es, name, shape, dt=F32):
            return es.enter_context(nc.psum_tensor(name, shape, dt))

        EVP = [new_ev() for _ in range(24)]

        ident_t = sb(gs, "ident", [128, 128], BF16)
        ones_t = sb(gs, "ones_b", [128, 128], BF16)
        maskA = sb(gs, "maskA", [128, 3, 128], BF16)
        maskB = sb(gs, "maskB", [128, 256], BF16)
        esink = sb(gs, "esink", [128, 8], F32)
        ident = ident_t[:]
        ones_b = ones_t[:]

        posblk = []
        for (c_, L_) in groups:
            for _ in range(c_):
                posblk += list(range(L_ // BK))
        gbase = []
        o = 0
        for (c, L) in groups:
            gbase.append(o)
            o += c * L

        if 0 in phases:
            with ExitStack() as es:
                P = new_prog()
                h = H(P)
                invf = sb(es, "invf_sb", [128, 1], F32)
                sgn = sb(es, "sgn", [128, 1], F32)
                posi = sb(es, "posi", [128, LMAX], I32)
                ang = sb(es, "ang", [128, LMAX], F32)
                t1 = sb(es, "t1", [128, LMAX], F32)
                t2 = sb(es, "t2", [128, LMAX], F32)
                ki = sb(es, "ki", [128, LMAX], I32)
                res = sb(es, "res", [128, LMAX], F32)
                ld = h.dma("sync", invf[:], invf_d, EVP[0])
                lds = h.dma("sync", esink[:], sink_d.partition_broadcast(128), EVP[1])
                h.memset("gpsimd", ident, 0.0)
                h.asel(ident, ident, ALU.not_equal, 1.0, 0, [[-1, 128]], 1)
                h.memset("gpsimd", ones_b, 1.0)
                h.memset("gpsimd", maskA[:], 1.0)
                h.asel(maskA[:, 0, :], maskA[:, 0, :], ALU.is_ge, 0.0, 0, [[1, 128]], -1)
                h.asel(maskA[:, 2, :], maskA[:, 2, :], ALU.is_ge, 0.0, 0, [[-1, 128]], 1)
                h.memset("gpsimd", maskB[:], 1.0)
                h.asel(maskB[:], maskB[:], ALU.is_ge, 0.0, 0, [[1, 256]], -1)
                h.asel(maskB[:], maskB[:], ALU.is_ge, 0.0, 128, [[-1, 256]], 1)
                h.memset("gpsimd", sgn[0:64, :], 1.0)
                sg = h.memset("gpsimd", sgn[64:128, :], -1.0)
                io = P.op("gpsimd", lambda e, ctx: e.iota(posi[:], pattern=[[1, LMAX]], base=0, channel_multiplier=0))
                h.act(esink[:], esink[:], AF.Exp, [lds])
                a0 = h.cp("vector", t1[:], posi[:], [io])
                a1 = h.ts("vector", ang[:], t1[:], invf[:, 0:1], ALU.mult, [a0, ld])

                def table(src_dep, shift, dst_dram, sign_ap, evx):
                    d = h.ts("vector", t2[:], ang[:], shift, ALU.add, [src_dep], s2=1.0 / TWO_PI, op1=ALU.mult)
                    d = h.cp("vector", ki[:], t2[:], [d])
                    d = h.cp("vector", t2[:], ki[:], [d])
                    d0 = h.ts("vector", t1[:], ang[:], shift, ALU.add, [d])
                    d = h.stt("vector", res[:], t2[:], -TWO_PI, t1[:], ALU.mult, ALU.add, [d0])
                    d = h.ts("vector", t2[:], res[:], PI, ALU.is_gt, [d], s2=-TWO_PI, op1=ALU.mult)
                    d = h.tt("vector", res[:], res[:], t2[:], ALU.add, [d])
                    d = h.ts("vector", t2[:], res[:], -PI, ALU.is_lt, [d], s2=TWO_PI, op1=ALU.mult)
                    d = h.tt("vector", res[:], res[:], t2[:], ALU.add, [d])
                    d = h.ts("vector", res[:], res[:], PI, ALU.min, [d], s2=-PI, op1=ALU.max)
                    d = h.act(res[:], res[:], AF.Sin, [d])
                    if sign_ap is not None:
                        d = h.ts("vector", res[:], res[:], sign_ap, ALU.mult, [d, sg])
                    l = None
                    for U_, pb in enumerate(posblk):
                        l = h.dma("sync", dst_dram[U_], res[:, pb * BK:(pb + 1) * BK], evx, [d])
                    return l

                st = table(a1, PI / 2, cos_s, None, EVP[2])
                table(st, 0.0, sin_s, sgn[:, 0:1], EVP[3])
                P.emit()

        if 1 in phases:
            with ExitStack() as es:
                w_sb = sb(es, "w_in_sb", [128, 8, DIN], BF16)
                g1b = sb(es, "g1b", [128, D], F32)
                stg = [sb(es, f"stg{i}", [128, 2, TT], F32) for i in range(2)]
                NXB = 2
                xb = [sb(es, f"xb{i}", [128, D], F32) for i in range(NXB)]
                junk = sb(es, "junk1", [128, D], BF16)
                hn = [sb(es, f"hn{i}", [128, D], BF16) for i in range(2)]
                ssq = [sb(es, f"ssq{i}", [128, 1], F32) for i in range(2)]
                rstd = [sb(es, f"rstd{i}", [128, 1], F32) for i in range(2)]
                hT = [sb(es, f"hT{i}", [128, 8, TT], BF16) for i in range(2)]
                RI_ = TT * NTI
                cosb2 = sb(es, "cosb2", [128, RI_], F32)
                sinb2 = sb(es, "sinb2", [128, RI_], F32)
                cosb = [cosb2[:, i * TT:(i + 1) * TT] for i in range(2)]
                sinb = [sinb2[:, i * TT:(i + 1) * TT] for i in range(2)]
                NRT = 3
                rt_a = [sb(es, f"rta{i}", [128, TT], F32) for i in range(NRT)]
                rt_c = [sb(es, f"rtc{i}", [128, TT], F32) for i in range(NRT)]
                NST = 2
                stq = [sb(es, f"stq{i}", [128, 4, TT], BF16) for i in range(NST)]
                stv = [sb(es, f"stv{i}", [128, 1792], BF16) for i in range(2)]
                NPS = 5
                psb = [pstile(es, f"ps1_{i}", [128, 512]) for i in range(NPS)]
                pst = [pstile(es, f"pst_{i}", [128, 8, 128], BF16) for i in range(2)]
                ev_x = EVP[1:1 + NXB]
                ev_tab = EVP[4:6]
                ev_stq = EVP[6:8]
                ev_stv = EVP[8:10]
                ev_stg = EVP[11:13]

                P = new_prog([EVW[1]])
                h = H(P)
                for kc in range(8):
                    for hh in range(4):
                        h.dma("gpsimd", w_sb[:, kc, hh * 2048:(hh + 1) * 2048], w_in_d[kc * 128:(kc + 1) * 128, hh * 2048:(hh + 1) * 2048], EVW[1])
                h.dma("sync", g1b[:], g1_d.partition_broadcast(128), EVP[10])
                P.emit()

                RI = TT * NTI
                assert RI == BK
                v_v1 = v_s.rearrange("(n r) c -> n r c", r=RI)

                v_l = nc.dram_tensor("v_l1", [RI, 1792], BF16, kind="Internal").ap()

                def p1_body(loops):
                    T0 = lambda ctx: ctx["i0"]
                    P0 = lambda ctx: ctx["i0"]
                    stores = []
                    P = new_prog()
                    h = H(P)
                    xb_rd = [None] * NXB
                    hn_rd = [None] * 2
                    rt_rd = [None] * NRT
                    stq_rd = [None] * NST
                    stv_rd = [None] * 2
                    stg_rd = [None] * 2
                    ps_rd = [[] for _ in range(NPS)]
                    pst_rd = [None] * 2
                    ssq_rd = [None] * 2
                    cnt = {"x": 0, "ps": 0, "rt": 0, "stq": 0, "stv": 0, "pst": 0}
                    tl_box = [None]
                    for tl in range(NTI):
                        toff = tl * TT
                        hb = tl % 2
                        hT_w = []
                        for s in range(4):
                            xi = cnt["x"] % NXB
                            cnt["x"] += 1
                            roff = toff + 128 * s
                            xl = h.dma("sync", xb[xi][:], (lambda ctx, k=tl * 4 + s: x8[k][T0(ctx)]), ev_x[xi], [xb_rd[xi]])
                            if tl == 0 and s == 1:
                                h.dma("sync", cosb2[:], (lambda ctx: cos_s[P0(ctx)]), ev_tab[0])
                                tl_box[0] = h.dma("sync", sinb2[:], (lambda ctx: sin_s[P0(ctx)]), ev_tab[0])
                            sp = s % 2
                            sq = h.act(junk[:], xb[xi][:], AF.Square, [xl, ssq_rd[sp]], accum=ssq[sp][:])
                            r1 = h.act(rstd[sp][:], ssq[sp][:], AF.Sqrt, [sq, hn_rd[sp]], scale=1.0 / D, bias=EPS)
                            r2 = h.recip(rstd[sp][:], rstd[sp][:], [r1])
                            ssq_rd[sp] = r2
                            hq = h.stt("vector", hn[sp][:], xb[xi][:], rstd[sp][:, 0:1], g1b[:], ALU.mult, ALU.mult, [r2, xl, hn_rd[sp]])
                            xb_rd[xi] = hq
                            pi_ = cnt["pst"] % 2
                            cnt["pst"] += 1
                            trp = h.trs([(pst[pi_][:, kc, :], hn[sp][:, kc * 128:(kc + 1) * 128]) for kc in range(8)], ident, [hq, pst_rd[pi_]])
                            hn_rd[sp] = trp
                            evc = h.cp("vector", hT[hb][:, :, 128 * s:128 * (s + 1)], pst[pi_][:], [trp])
                            pst_rd[pi_] = evc
                            hT_w.append(evc)

                        def fm_matmul(col, deps):
                            pi = cnt["ps"] % NPS
                            cnt["ps"] += 1
                            m = h.mms([(psb[pi][:], w_sb[:, kc, col:col + 128], hT[hb][:, kc, :], kc == 0, kc == 7) for kc in range(8)], deps + ps_rd[pi])
                            ps_rd[pi] = []
                            return pi, m

                        first = hT_w
                        tl_ = tl_box[0]
                        for c0 in range(0, 34, 4):
                            nch = min(4, 34 - c0)
                            si = cnt["stq"] % NST
                            cnt["stq"] += 1
                            writers = []
                            for j in range(nch):
                                pi, m = fm_matmul(qk_col(c0 + j), first)
                                first = []
                                ri = cnt["rt"] % NRT
                                cnt["rt"] += 1
                                d1 = h.tt("vector", rt_a[ri][0:64, :], psb[pi][64:128, :], sinb2[64:128, hb * TT:(hb + 1) * TT], ALU.mult, [m, tl_, rt_rd[ri]])
                                d2 = h.tt("vector", rt_a[ri][64:128, :], psb[pi][0:64, :], sinb2[0:64, hb * TT:(hb + 1) * TT], ALU.mult, [m, tl_, rt_rd[ri]])
                                d3 = h.tt("vector", rt_c[ri][:], psb[pi][:], cosb2[:, hb * TT:(hb + 1) * TT], ALU.mult, [m, tl_, rt_rd[ri]])
                                ps_rd[pi] = [d3]
                                d4 = h.tt("gpsimd", stq[si][:, j, :], rt_c[ri][:], rt_a[ri][:], ALU.add, [d1, d2, d3, stq_rd[si]])
                                rt_rd[ri] = d4
                                writers.append(d4)
                            for j in range(nch):
                                stq_rd[si] = h.dma("sync", (lambda ctx, c=c0 + j, o=toff: qk_t[c][:, bass.ds(T0(ctx) * BK + o, TT)]), stq[si][:, j, :], ev_stq[si], writers)
                        for c0 in range(0, 16, 2):
                            si = (c0 // 2) % 2
                            writers = []
                            for j in range(2):
                                pi, m = fm_matmul(6144 + 128 * (c0 + j), [])
                                a = h.act(stg[si][:, j, :], psb[pi][:], AF.Sigmoid, [m, stg_rd[si]])
                                ps_rd[pi] = [a]
                                writers.append(a)
                            for j in range(2):
                                stg_rd[si] = h.dma("scalar", (lambda ctx, c=c0 + j, o=toff: gt_t[c][:, bass.ds(T0(ctx) * BK + o, TT)]), stg[si][:, j, :], ev_stg[si], writers)
                        for s in range(4):
                            vi = cnt["stv"] % 2
                            cnt["stv"] += 1
                            writers = []
                            for (wc, width, vc) in V_GROUPS:
                                pi = cnt["ps"] % NPS
                                cnt["ps"] += 1
                                m = h.mms([(psb[pi][:, 0:width], hT[hb][:, kc, 128 * s:128 * (s + 1)], w_sb[:, kc, wc:wc + width], kc == 0, kc == 7) for kc in range(8)], ps_rd[pi])
                                a = h.act(stv[vi][:, vc:vc + width], psb[pi][:, 0:width], AF.Copy, [m, stv_rd[vi]])
                                ps_rd[pi] = [a]
                                writers.append(a)
                            roff = toff + 128 * s
                            stv_rd[vi] = h.dma("scalar", (lambda ctx, o=roff: v_s[bass.ds(T0(ctx) * BK + o, 128), :]), stv[vi][:], ev_stv[vi], writers)
                    P.emit(loops)

                p1_body((NB,))

        if 2 in phases:
            with ExitStack() as es:
                NQ = 2
                qT = [sb(es, f"qT{i}", [128, 4, LMAX], BF16) for i in range(NQ)]
                kT = [sb(es, f"kT{i}", [128, LMAX], BF16) for i in range(NQ)]
                vt = [sb(es, f"vt{i}", [128, LMAX // 128, 128], BF16) for i in range(NQ)]
                NPT = 16
                pT = [sb(es, f"pT{i}", [128, 512], BF16) for i in range(NPT)]
                accOD = sb(es, "accOD", [128, 2, LMAX], F32)
                rdn = [sb(es, f"rdn{i}", [128, 512], F32) for i in range(2)]
                ost = [sb(es, f"ost{i}", [128, 512], BF16) for i in range(2)]
                ostB = [sb(es, f"ostB{i}", [128, LMAX], BF16) for i in range(2)]
                NS = 4
                psS = [pstile(es, f"psS{i}", [128, 512]) for i in range(NS)]
                psO = [pstile(es, f"psO{i}", [128, 512]) for i in range(2)]
                psD = [pstile(es, f"psD{i}", [128, 512]) for i in range(2)]
                ev_q = EVP[0:2]
                ev_ost = EVP[2:4]
                ev_ostB = EVP[4:6]
                JCH = 8

                v_l2 = nc.dram_tensor("v_l2", [LMAX, 1792], BF16, kind="Internal").ap()
                ostA = sb(es, "ostA", [128, 4, LMAX], BF16)

                def p2_body(UF, L, loops):
                    SO = lambda ctx: ctx["U"]
                    stores = []
                    P = new_prog()
                    h = H(P)
                    q_rd = [[] for _ in range(NQ)]
                    pT_rd = [[] for _ in range(NPT)]
                    psS_rd = [None] * NS
                    psO_rd = [None] * 2
                    rdn_rd = [None] * 2
                    ost_rd = [None] * 2
                    ostB_rd = [None] * 2
                    cnt = {"q": 0, "pT": 0, "S": 0, "O": 0, "ost": 0, "ostB": 0}
                    nb = L // 128
                    v_v = v_s.rearrange("(n r) c -> n r c", r=L)
                    cin2 = h.dma("sync", v_l2[0:L, :], (lambda ctx: v_v[SO(ctx)]), EVP[14])
                    ostA_rd = []

                    def row(c):
                        return lambda ctx: qk_t[c][:, bass.ds(SO(ctx) * L, L)]

                    def load_v(dst, src_fn, nbk, ev, deps):
                        l = None
                        for j0 in range(0, nbk, JCH):
                            j1 = min(nbk, j0 + JCH)
                            l = h.dma("sync", dst[:, j0:j1, :], (lambda ctx, j0=j0, j1=j1: src_fn(ctx)[:, j0:j1, :]), ev, deps)
                        return l

                    for hk in range(2):
                        qi = cnt["q"] % NQ
                        cnt["q"] += 1
                        dq = q_rd[qi]
                        for hh_ in range(4):
                            h.dma("sync", qT[qi][:, hh_, 0:L], row(4 * hk + hh_), ev_q[qi], dq + [cin2])
                        h.dma("sync", kT[qi][:, 0:L], row(8 + hk), ev_q[qi], dq)
                        l3 = load_v(vt[qi], (lambda ctx, hk=hk: v_l2[0:L, 128 * hk:128 * hk + 128].rearrange("(j p) c -> p j c", p=128)), nb, ev_q[qi], dq)
                        q_rd[qi] = []
                        pts = {}
                        last_pe = [None]
                        last_d3 = [None]

                        def do_pv_A(i):
                            oi = cnt["O"] % 2
                            cnt["O"] += 1
                            js = [j for j in (i - 1, i, i + 1) if 0 <= j < nb]
                            deps = [pts[(j, hh)][1] for j in js for hh in range(4)] + [psO_rd[oi], l3]
                            lst = []
                            for dst, use_v in ((psO[oi], True), (psD[oi], False)):
                                for hh in range(4):
                                    for n, j in enumerate(js):
                                        pi_, _, qlo = pts[(j, hh)]
                                        blk = i - qlo
                                        lhs = vt[qi][:, j, :] if use_v else ones_b
                                        lst.append((dst[:, 128 * hh:128 * (hh + 1)], lhs, pT[pi_][:, 128 * blk:128 * (blk + 1)], n == 0, n == len(js) - 1))
                            m = h.mms(lst, deps)
                            last_pe[0] = m
                            for j in js:
                                for hh in range(4):
                                    pT_rd[pts[(j, hh)][0]].append(m)
                            d1 = h.tt("vector", rdn[oi][:].rearrange("p (h q) -> p h q", h=4), psD[oi][:].rearrange("p (h q) -> p h q", h=4),
                                      esink[:, 4 * hk:4 * hk + 4].unsqueeze(2).to_broadcast([128, 4, 128]), ALU.add, [m, rdn_rd[oi]])
                            l1 = h.act(rdn[oi][:], rdn[oi][:], AF.Ln, [d1])
                            d2 = h.act(rdn[oi][:], rdn[oi][:], AF.Exp, [l1], scale=-1.0)
                            d3 = h.tt("vector", ostA[:, :, 128 * i:128 * (i + 1)], psO[oi][:].rearrange("p (h q) -> p h q", h=4), rdn[oi][:].rearrange("p (h q) -> p h q", h=4), ALU.mult, [d2] + ostA_rd)
                            psO_rd[oi] = d3
                            rdn_rd[oi] = d3
                            last_d3[0] = d3

                        for j in range(nb):
                            qlo = max(0, j - 1)
                            qhi = min(nb, j + 2)
                            nq = qhi - qlo
                            for hh in range(4):
                                s_i = cnt["S"] % NS
                                cnt["S"] += 1
                                pi_ = cnt["pT"] % NPT
                                cnt["pT"] += 1
                                m = h.mms([(psS[s_i][:, 0:128 * nq], kT[qi][:, 128 * j:128 * (j + 1)], qT[qi][:, hh, 128 * qlo:128 * (qlo + nq)], True, True)], [l3, psS_rd[s_i]])
                                last_pe[0] = m
                                a = h.act(pT[pi_][:, 0:128 * nq], psS[s_i][:, 0:128 * nq], AF.Exp, [m] + pT_rd[pi_], scale=SCALE)
                                pT_rd[pi_] = []
                                psS_rd[s_i] = a
                                rdy = a
                                for qb in range(qlo, qhi):
                                    if qb == j:
                                        continue
                                    mb = 0 if qb == j - 1 else 2
                                    blk = qb - qlo
                                    rdy = h.tt("gpsimd", pT[pi_][:, 128 * blk:128 * (blk + 1)], pT[pi_][:, 128 * blk:128 * (blk + 1)], maskA[:, mb, :], ALU.mult, [rdy])
                                pts[(j, hh)] = (pi_, rdy, qlo)
                            if j >= 2:
                                do_pv_A(j - 2)
                        if nb >= 2:
                            do_pv_A(nb - 2)
                        do_pv_A(nb - 1)
                        q_rd[qi] = [last_pe[0]]
                        ostA_rd = []
                        for hh_ in range(4):
                            ostA_rd.append(h.dma("sync", (lambda ctx, c=4 * hk + hh_: o_t[c][:, bass.ds(SO(ctx) * L, L)]), ostA[:, hh_, 0:L], ev_ost[0], [last_d3[0]]))
                        ostA_rd = [ostA_rd[-1]]

                    acc_guard = None
                    bslots = [psO[0], psO[1], psD[0], psD[1]]
                    bslot_rd = [psO_rd[0], psO_rd[1], psO_rd[0], psO_rd[1]]
                    cnt["OB"] = 0
                    for hb_ in range(4):
                        bi = cnt["ostB"] % 2
                        cnt["ostB"] += 1
                        last_acc = [None, None]
                        for g, dil in enumerate(B_DIL):
                            Lf = L // dil
                            nbf = Lf // 128
                            qi = cnt["q"] % NQ
                            cnt["q"] += 1
                            dq = q_rd[qi]
                            cq = 10 + 8 * g + hb_
                            ck = 10 + 8 * g + 4 + hb_
                            vc = 256 + 512 * g + 128 * hb_
                            h.dma("sync", qT[qi][:, 0, 0:L], row(cq), ev_q[qi], dq)
                            h.dma("sync", kT[qi][:, 0:L], row(ck), ev_q[qi], dq)
                            l3 = None
                            for r in range(dil):
                                l3 = load_v(vt[qi][:, r * nbf:(r + 1) * nbf, :],
                                            (lambda ctx, vc=vc, dil=dil, r=r: v_l2[0:L, vc:vc + 128].rearrange("(j p r) c -> r p j c", p=128, r=dil)[r]),
                                            nbf, ev_q[qi], dq)
                            q_rd[qi] = []
                            last_pe = [None]
                            qv = qT[qi][:, 0, 0:L].rearrange("p (s r) -> p r s", r=dil)
                            kv = kT[qi][:, 0:L].rearrange("p (s r) -> p r s", r=dil)
                            aOD = accOD[:, :, 0:L].rearrange("p t (s r) -> p t r s", r=dil)
                            pts = {}
                            pend = []
                            step = [0]
                            LAG = 2

                            def do_pv_B(r, i):
                                sl = cnt["OB"] % 4
                                cnt["OB"] += 1
                                bank = bslots[sl]
                                contrib = [(i, 64, 128, 0)]
                                if i - 1 >= 0:
                                    contrib.append((i - 1, 192, 64, 0))
                                if i + 1 < nbf:
                                    contrib.append((i + 1, 0, 64, 64))
                                deps = [pts[(r, j)][1] for (j, _, _, _) in contrib] + [bslot_rd[sl], l3]
                                lst = []
                                for doff, use_v in ((0, True), (128, False)):
                                    for n, (j, c0, w, d0) in enumerate(contrib):
                                        pi_, _, clo = pts[(r, j)]
                                        lhs = vt[qi][:, r * nbf + j, :] if use_v else ones_b
                                        lst.append((bank[:, doff + d0:doff + d0 + w], lhs, pT[pi_][:, c0 - clo:c0 - clo + w], n == 0, n == len(contrib) - 1))
                                m = h.mms(lst, deps)
                                last_pe[0] = m
                                for (j, _, _, _) in contrib:
                                    pT_rd[pts[(r, j)][0]].append(m)
                                dsl_ = slice(128 * i, 128 * (i + 1))
                                bankv = bank[:, 0:256].rearrange("p (t q) -> p t q", t=2)
                                if g == 0:
                                    d2 = h.cp("vector", aOD[:, :, r, dsl_], bankv, [m, acc_guard])
                                else:
                                    d2 = h.tt("vector", aOD[:, :, r, dsl_], bankv, aOD[:, :, r, dsl_], ALU.add, [m])
                                bslot_rd[sl] = d2
                                last_acc[0], last_acc[1] = d2, d2

                            for r in range(dil):
                                for j in range(nbf):
                                    q_lo = 128 * j - 64
                                    c_lo = 64 if j == 0 else 0
                                    c_hi = 192 if j == nbf - 1 else 256
                                    nqc = c_hi - c_lo
                                    s_i = cnt["S"] % NS
                                    cnt["S"] += 1
                                    pi_ = cnt["pT"] % NPT
                                    cnt["pT"] += 1
                                    m = h.mms([(psS[s_i][:, 0:nqc], kv[:, r, 128 * j:128 * (j + 1)], qv[:, r, q_lo + c_lo:q_lo + c_lo + nqc], True, True)], [l3, psS_rd[s_i]])
                                    last_pe[0] = m
                                    a = h.act(pT[pi_][:, 0:nqc], psS[s_i][:, 0:nqc], AF.Exp, [m] + pT_rd[pi_], scale=SCALE)
                                    pT_rd[pi_] = []
                                    psS_rd[s_i] = a
                                    mk_ = h.tt("gpsimd", pT[pi_][:, 0:nqc], pT[pi_][:, 0:nqc], maskB[:, c_lo:c_lo + nqc], ALU.mult, [a])
                                    pts[(r, j)] = (pi_, mk_, c_lo)
                                    if j >= 1:
                                        pend.append((r, j - 1, step[0]))
                                    if j == nbf - 1:
                                        pend.append((r, j, step[0]))
                                    step[0] += 1
                                    while pend and pend[0][2] <= step[0] - 1 - LAG:
                                        r_, i_, _ = pend.pop(0)
                                        do_pv_B(r_, i_)
                            while pend:
                                r_, i_, _ = pend.pop(0)
                                do_pv_B(r_, i_)
                            q_rd[qi] = [last_pe[0]]
                        l1 = h.act(accOD[:, 1, 0:L], accOD[:, 1, 0:L], AF.Ln, [last_acc[0], last_acc[1]])
                        d1 = h.act(accOD[:, 1, 0:L], accOD[:, 1, 0:L], AF.Exp, [l1], scale=-1.0)
                        d2 = h.tt("gpsimd", ostB[bi][:, 0:L], accOD[:, 0, 0:L], accOD[:, 1, 0:L], ALU.mult, [d1, ostB_rd[bi]])
                        acc_guard = d2
                        ostB_rd[bi] = h.dma("sync", (lambda ctx, c=8 + hb_: o_t[c][:, bass.ds(SO(ctx) * L, L)]), ostB[bi][:, 0:L], ev_ostB[bi], [d2])
                    P.emit(loops, pre=lambda ctx: ctx.__setitem__("U", nc.sync.snap(UF(ctx))))

                for gi, (cg, L) in enumerate(groups):
                    base = gbase[gi]
                    p2_body(lambda ctx, base=base, L=L: base // L + ctx["i0"], L, (cg,))

        if 3 in phases:
            with ExitStack() as es:
                wa = sb(es, "wa", [128, 8, D], BF16)
                wb_ = sb(es, "wb", [128, 4, D], BF16)
                wo = sb(es, "wo", [128, 8, D], BF16)
                g2b = sb(es, "g2b", [128, D], F32)
                oT = [sb(es, f"oT{i}", [128, 12, TT], BF16) for i in range(2)]
                gT = [sb(es, f"gT{i}", [128, 16, TT], F32) for i in range(2)]
                mixT = [sb(es, f"mixT{i}", [128, 8, TT], BF16) for i in range(2)]
                ta = [sb(es, f"ta{i}", [128, TT], F32) for i in range(2)]
                tb = [sb(es, f"tb{i}", [128, TT], F32) for i in range(2)]
                xb3 = [sb(es, f"xb3{i}", [128, D], F32) for i in range(2)]
                nrm = [sb(es, f"nrm{i}", [128, D], F32) for i in range(2)]
                junk3 = sb(es, "junk3", [128, 512], BF16)
                ssa = [sb(es, f"ssa{i}", [128, 2], F32) for i in range(2)]
                rs3 = [sb(es, f"rs3{i}", [128, 1], F32) for i in range(2)]
                psY = [pstile(es, f"psY{i}", [128, 512]) for i in range(4)]
                psM = [pstile(es, f"psM{i}", [128, 2, 512]) for i in range(2)]
                ev_o = EVP[1:3]
                ev_x3 = EVP[3:5]
                ev_x1 = EVP[5:7]
                P = new_prog([EVW[3]])
                h = H(P)
                for kc in range(8):
                    h.dma("gpsimd", wa[:, kc, :], w_a_d[kc * 128:(kc + 1) * 128, :], EVW[3])
                    h.dma("gpsimd", wo[:, kc, :], w_o_d[kc * 128:(kc + 1) * 128, :], EVW[3])
                for kc in range(4):
                    h.dma("gpsimd", wb_[:, kc, :], w_b_d[kc * 128:(kc + 1) * 128, :], EVW[3])
                h.dma("sync", g2b[:], g2_d.partition_broadcast(128), EVP[10])
                P.emit()

                P = new_prog()
                h = H(P)
                T0 = lambda ctx: ctx["i0"]
                U3 = lambda ctx, tl: ctx["i0"] * (NTI3 // 2) + tl // 2

                def load_tile(tl, deps):
                    b = tl % 2
                    o = (tl % 2) * TT
                    l = None
                    for c_ in range(12):
                        l = h.dma("sync", oT[b][:, c_, :], (lambda ctx, c=c_, o=o, tl=tl: o_t[c][:, bass.ds(U3(ctx, tl) * BK + o, TT)]), ev_o[b], deps)
                    for c_ in range(16):
                        l = h.dma("sync", gT[b][:, c_, :], (lambda ctx, c=c_, o=o, tl=tl: gt_t[c][:, bass.ds(U3(ctx, tl) * BK + o, TT)]), ev_o[b], deps)
                    return l
                o_rd = [[] for _ in range(2)]
                mix_rd = [[] for _ in range(2)]
                psY_rd = [None] * 4
                psM_rd = [None] * 2
                tab_rd3 = [None] * 2
                xb3_rd = [None] * 2
                ss_rd = [None] * 2
                cnt = {"Y": 0, "t": 0, "M": 0, "x": 0}
                lgs = [load_tile(0, []), load_tile(1, [])] + [None] * (NTI3 - 2)
                for tl in range(NTI3):
                    toff = tl * TT
                    b = tl % 2
                    lg = lgs[tl]
                    o_rd[b] = []
                    mix_w = []
                    lastr = []
                    for m_ in range(8):
                        ya = cnt["Y"] % 4
                        cnt["Y"] += 1
                        yb = cnt["Y"] % 4
                        cnt["Y"] += 1
                        ma = h.mms([(psY[ya][:], wa[:, kc, 128 * m_:128 * (m_ + 1)], oT[b][:, kc, :], kc == 0, kc == 7) for kc in range(8)], [lg, psY_rd[ya]])
                        mb = h.mms([(psY[yb][:], wb_[:, kc, 128 * m_:128 * (m_ + 1)], oT[b][:, 8 + kc, :], kc == 0, kc == 3) for kc in range(4)], [lg, psY_rd[yb]])
                        tt_ = cnt["t"] % 2
                        cnt["t"] += 1
                        d1 = h.tt("vector", ta[tt_][:], psY[ya][:], gT[b][:, m_, :], ALU.mult, [ma, lg, tab_rd3[tt_]])
                        d2 = h.tt("vector", tb[tt_][:], psY[yb][:], gT[b][:, 8 + m_, :], ALU.mult, [mb, lg, tab_rd3[tt_]])
                        psY_rd[ya] = d1
                        psY_rd[yb] = d2
                        d3 = h.tt("gpsimd", mixT[b][:, m_, :], ta[tt_][:], tb[tt_][:], ALU.add, [d1, d2] + mix_rd[b])
                        tab_rd3[tt_] = d3
                        mix_w.append(d3)
                        lastr = [ma, mb, d2]
                    mix_rd[b] = []
                    o_rd[b] = lastr
                    if tl + 2 < NTI3:
                        lgs[tl + 2] = load_tile(tl + 2, lastr)
                    for s in range(4):
                        mi = cnt["M"] % 2
                        cnt["M"] += 1
                        xi = cnt["x"] % 2
                        cnt["x"] += 1
                        roff = toff + 128 * s
                        xl = h.dma("sync", xb3[xi][:], (lambda ctx, k=(tl % 2) * 4 + s, tl=tl: x8[k][U3(ctx, tl)]), ev_x3[xi], [xb3_rd[xi]])
                        lst = []
                        for hf in range(2):
                            for kc in range(8):
                                lst.append((psM[mi][:, hf, :], mixT[b][:, kc, 128 * s:128 * (s + 1)], wo[:, kc, 512 * hf:512 * (hf + 1)], kc == 0, kc == 7))
                        mo = h.mms(lst, mix_w + [psM_rd[mi]])
                        mix_w = []
                        mix_rd[b] = [mo]
                        s0 = h.act(junk3[:], psM[mi][:, 0, :], AF.Square, [mo, ss_rd[xi]], accum=ssa[xi][:, 0:1])
                        s1 = h.act(junk3[:], psM[mi][:, 1, :], AF.Square, [mo, s0], accum=ssa[xi][:, 1:2])
                        r1 = h.tt("vector", rs3[xi][:], ssa[xi][:, 0:1], ssa[xi][:, 1:2], ALU.add, [s0, s1, xb3_rd[xi]])
                        r2 = h.act(rs3[xi][:], rs3[xi][:], AF.Sqrt, [r1], scale=1.0 / D, bias=EPS)
                        r3 = h.recip(rs3[xi][:], rs3[xi][:], [r2])
                        ss_rd[xi] = r3
                        n1 = h.act(nrm[xi][:].rearrange("p (h c) -> p h c", h=2), psM[mi][:], AF.Copy, [r3, xb3_rd[xi]], scale=rs3[xi][:, 0:1])
                        psM_rd[mi] = n1
                        n2 = h.tt("vector", nrm[xi][:], nrm[xi][:], g2b[:], ALU.mult, [n1])
                        n3 = h.tt("gpsimd", xb3[xi][:], nrm[xi][:], xb3[xi][:], ALU.add, [n2, xl])
                        xb3_rd[xi] = h.dma("sync", (lambda ctx, k=(tl % 2) * 4 + s, tl=tl: x1_8[k][U3(ctx, tl)]), xb3[xi][:], ev_x1[xi], [n3])
                P.emit((NT // (TT * NTI3),))

        if 4 in phases:
            with ExitStack() as es:
                w1 = sb(es, "w1", [128, 8, DFF], BF16)
                w2 = sb(es, "w2", [128, 32, D], BF16)
                g3b = sb(es, "g3b", [128, D], F32)
                g4b = sb(es, "g4b", [128, D], F32)
                xa = [sb(es, f"xa{i}", [128, D], F32) for i in range(1)]
                xr = [sb(es, f"xr{i}", [128, D], F32) for i in range(1)]
                junk4 = sb(es, "junk4", [128, D], BF16)
                hn4 = [sb(es, f"hn4{i}", [128, D], BF16) for i in range(1)]
                ssq4 = [sb(es, f"ssq4{i}", [128, 1], F32) for i in range(2)]
                rstd4 = [sb(es, f"rstd4{i}", [128, 1], F32) for i in range(2)]
                h2T = [sb(es, f"h2T{i}", [128, 8, TT], BF16) for i in range(2)]
                uT = sb(es, "uT", [128, 32, TT], BF16)
                ur = [sb(es, f"ur{i}", [128, TT], F32) for i in range(2)]
                nr4 = [sb(es, f"nr4{i}", [128, D], F32) for i in range(1)]
                ssb = [sb(es, f"ssb{i}", [128, 2], F32) for i in range(2)]
                rs4 = [sb(es, f"rs4{i}", [128, 1], F32) for i in range(2)]
                psU = [pstile(es, f"psU{i}", [128, 512]) for i in range(3)]
                psT4 = pstile(es, "psT4", [128, 8, 128], BF16)
                psO4 = [pstile(es, f"psO4{i}", [128, 2, 512]) for i in range(2)]
                ev_xa = EVP[1:3]
                ev_xr = EVP[3:5]
                ev_y = EVP[5:7]
                P = new_prog([EVW[4]])
                h = H(P)
                for kc in range(8):
                    for hh in range(2):
                        h.dma("gpsimd", w1[:, kc, hh * 2048:(hh + 1) * 2048], w_1_d[kc * 128:(kc + 1) * 128, hh * 2048:(hh + 1) * 2048], EVW[4])
                for kc in range(32):
                    h.dma("gpsimd", w2[:, kc, :], w_2_d[kc * 128:(kc + 1) * 128, :], EVW[4])
                h.dma("sync", g3b[:], g3_d.partition_broadcast(128), EVP[10])
                h.dma("sync", g4b[:], g4_d.partition_broadcast(128), EVP[11])
                P.emit()

                P = new_prog()
                h = H(P)
                T0 = lambda ctx: ctx["i0"]
                R4 = TT * NTI
                NTI4 = 4 if NT % (TT * 4) == 0 else 2
                UB = lambda ctx, tl: ctx["i0"] * (NTI4 // 2) + tl // 2
                h2T_rd = [None] * 2
                assert R4 == BK
                xa_rd = [None] * 2
                xr_rd = [None] * 2
                hn_rd = [None] * 2
                ssq_rd = [None] * 2
                uT_rd = []
                psU_rd = [None] * 3
                ur_rd = [None] * 2
                psT_rd = None
                psO_rd = [None] * 2
                ss_rd = [None] * 2
                cnt = {"x": 0, "U": 0, "R": 0, "O": 0, "y": 0}

                def norm_sub(tl, s):
                    nonlocal psT_rd
                    b = tl % 2
                    if True:
                        xi = 0
                        xl = h.dma("sync", xa[xi][:], (lambda ctx, k=(tl % 2) * 4 + s, tl=tl: x1_8[k][UB(ctx, tl)]), ev_xa[xi], [xa_rd[xi]])
                        sq = h.act(junk4[:], xa[xi][:], AF.Square, [xl, ssq_rd[xi]], accum=ssq4[xi][:])
                        r1 = h.act(rstd4[xi][:], ssq4[xi][:], AF.Sqrt, [sq, hn_rd[xi]], scale=1.0 / D, bias=EPS)
                        r2 = h.recip(rstd4[xi][:], rstd4[xi][:], [r1])
                        ssq_rd[xi] = r2
                        hq = h.stt("vector", hn4[xi][:], xa[xi][:], rstd4[xi][:, 0:1], g3b[:], ALU.mult, ALU.mult, [r2, xl, hn_rd[xi]])
                        xa_rd[xi] = hq
                        trp = h.trs([(psT4[:, kc, :], hn4[xi][:, kc * 128:(kc + 1) * 128]) for kc in range(8)], ident, [hq, psT_rd])
                        hn_rd[xi] = trp
                        evc = h.cp("vector", h2T[b][:, :, 128 * s:128 * (s + 1)], psT4[:], [trp, h2T_rd[b]])
                        psT_rd = evc
                        return evc

                hw_next = [norm_sub(0, s_) for s_ in range(4)]
                for tl in range(NTI4):
                    toff = tl * TT
                    b = tl % 2
                    hw = hw_next
                    u_w = []
                    first = hw + uT_rd
                    uT_rd = []
                    for m_ in range(32):
                        ui = cnt["U"] % 3
                        cnt["U"] += 1
                        ri = cnt["R"] % 2
                        cnt["R"] += 1
                        m1 = h.mms([(psU[ui][:], w1[:, kc, 128 * m_:128 * (m_ + 1)], h2T[b][:, kc, :], kc == 0, kc == 7) for kc in range(8)], first + [psU_rd[ui]])
                        first = []
                        a = h.act(ur[ri][:], psU[ui][:], AF.Relu, [m1, ur_rd[ri]])
                        psU_rd[ui] = a
                        eng = "gpsimd" if m_ % 2 == 0 else "vector"
                        u2 = h.tt(eng, uT[:, m_, :], ur[ri][:], ur[ri][:], ALU.mult, [a])
                        ur_rd[ri] = u2
                        u_w.append(u2)
                    h2T_rd[b] = m1
                    hw_new = []
                    for s in range(4):
                        oi = cnt["O"] % 2
                        cnt["O"] += 1
                        xi = 0
                        roff = toff + 128 * s
                        xl = h.dma("sync", xr[xi][:], (lambda ctx, k=(tl % 2) * 4 + s, tl=tl: x1_8[k][UB(ctx, tl)]), ev_xr[xi], [xr_rd[xi]])
                        lst = []
                        for hf in range(2):
                            for kc in range(32):
                                lst.append((psO4[oi][:, hf, :], uT[:, kc, 128 * s:128 * (s + 1)], w2[:, kc, 512 * hf:512 * (hf + 1)], kc == 0, kc == 31))
                        m2 = h.mms(lst, u_w + [psO_rd[oi]])
                        u_w = []
                        uT_rd = [m2]
                        if tl + 1 < NTI4:
                            hw_new.append(norm_sub(tl + 1, s))
                        s0 = h.act(junk4[:, 0:512], psO4[oi][:, 0, :], AF.Square, [m2, ss_rd[xi]], accum=ssb[xi][:, 0:1])
                        s1 = h.act(junk4[:, 512:1024], psO4[oi][:, 1, :], AF.Square, [m2, s0], accum=ssb[xi][:, 1:2])
                        r1 = h.tt("vector", rs4[xi][:], ssb[xi][:, 0:1], ssb[xi][:, 1:2], ALU.add, [s0, s1, xr_rd[xi]])
                        r2 = h.act(rs4[xi][:], rs4[xi][:], AF.Sqrt, [r1], scale=1.0 / D, bias=EPS)
                        r3 = h.recip(rs4[xi][:], rs4[xi][:], [r2])
                        ss_rd[xi] = r3
                        n1 = h.act(nr4[xi][:].rearrange("p (h c) -> p h c", h=2), psO4[oi][:], AF.Copy, [r3, xr_rd[xi]], scale=rs4[xi][:, 0:1])
                        psO_rd[oi] = n1
                        n2 = h.tt("vector", nr4[xi][:], nr4[xi][:], g4b[:], ALU.mult, [n1])
                        n3 = h.tt("gpsimd", xr[xi][:], nr4[xi][:], xr[xi][:], ALU.add, [n2, xl])
                        xr_rd[xi] = h.dma("sync", (lambda ctx, k=(tl % 2) * 4 + s, tl=tl: y8[k][UB(ctx, tl)]), xr[xi][:], ev_y[xi], [n3])
                    if tl + 1 < NTI4:
                        hw_next = hw_new
                P.emit((NT // (TT * NTI4),))
    return nc


def _invf():
    half = 64
    inv = np.power(np.float32(10000.0), -np.arange(half, dtype=np.float32) / np.float32(half)).astype(np.float32)
    return np.concatenate([inv, inv]).reshape(128, 1).astype(np.float32)


_CACHE = {}


def kernel(x_prompt, x_sample, w_in, sink, w_a, w_b, w_o, g_pre_mix, g_post_mix, g_pre_mlp, g_post_mlp, w_1, w_2):
    x_prompt = np.asarray(x_prompt, dtype=np.float32)
    x_sample = np.asarray(x_sample, dtype=np.float32)
    if "nc" not in _CACHE:
        _CACHE["nc"] = build_program([(4, 2048), (1, 4096)])
    nc = _CACHE["nc"]
    f = lambda a: np.ascontiguousarray(np.asarray(a, dtype=np.float32))
    shared = {
        "w_in": f(w_in[0]), "sink": f(sink[0]).reshape(1, 8), "w_a": f(w_a[0]), "w_b": f(w_b[0]), "w_o": f(w_o[0]),
        "g_pre_mix": f(g_pre_mix[0]).reshape(1, D), "g_post_mix": f(g_post_mix[0]).reshape(1, D),
        "g_pre_mlp": f(g_pre_mlp[0]).reshape(1, D), "g_post_mlp": f(g_post_mlp[0]).reshape(1, D),
        "w_1": f(w_1[0]), "w_2": f(w_2[0]), "invf": _invf(),
    }
    in_maps = []
    for c in range(N_CORES):
        xc = np.concatenate([x_prompt[4 * c:4 * c + 4].reshape(4 * 2048, D), x_sample[c].reshape(4096, D)], axis=0)
        m = dict(shared)
        xr_ = xc.reshape(-1, 8, 128, D)
        for k in range(8):
            m[f"x{k}"] = np.ascontiguousarray(xr_[:, k])
        in_maps.append(m)
    res = run_bass_kernel_spmd(nc, in_maps, core_ids=list(range(N_CORES)))
    y_prompt = np.empty((32, 2048, D), np.float32)
    y_sample = np.empty((8, 4096, D), np.float32)
    for c in range(N_CORES):
        y = np.stack([res.results[c][f"y{k}"] for k in range(8)], axis=1).reshape(-1, D)
        y_prompt[4 * c:4 * c + 4] = y[:8192].reshape(4, 2048, D)
        y_sample[c] = y[8192:].reshape(4096, D)
    return (y_prompt, y_sample)
```

```python
import math
from contextlib import ExitStack

import numpy as np
import concourse.bass as bass
import concourse.mybir as mybir
from concourse.bass_utils import run_bass_kernel_spmd

F32 = mybir.dt.float32
BF16 = mybir.dt.bfloat16
I32 = mybir.dt.int32
ALU = mybir.AluOpType
AF = mybir.ActivationFunctionType

D = 1024
DIN = 8192
DFF = 4096
HD = 128
EPS = 1e-6
TT = 512
SCALE = HD ** -0.5
PI = float(np.pi)
TWO_PI = float(2 * np.pi)
N_CORES = 8
B_DIL = (1, 4, 16)

ENGS = ["sync", "scalar", "vector", "gpsimd", "tensor"]


class Ev:
    def __init__(self, sem):
        self.sem = sem
        self.count = 0


class Op:
    __slots__ = ("eng", "fn", "deps", "sig", "val", "sem", "kind")

    def __init__(self, eng, fn, deps, kind):
        self.eng, self.fn, self.deps, self.kind = eng, fn, list(deps), kind
        self.sig = False
        self.val = None
        self.sem = None


class Prog:
    def __init__(self, nc, eng_sems, evs, extra=()):
        self.nc = nc
        self.eng_sems = eng_sems
        self.evs = evs
        self.extra = list(extra)
        for ev in evs:
            assert ev.count == 0
        self.ops = {e: [] for e in ENGS}

    def op(self, eng, fn, deps=()):
        o = Op(eng, fn, [d for d in deps if d is not None], "c")
        self.ops[eng].append(o)
        return o

    def dma(self, eng, fn, ev, deps=()):
        o = Op(eng, fn, [d for d in deps if d is not None], "d")
        ev.count += 16
        o.val = ev.count
        o.sem = ev.sem
        o.sig = True
        self.ops[eng].append(o)
        return o

    def emit(self, loops=(), pre=None):
        nc = self.nc
        for e in ENGS:
            for o in self.ops[e]:
                for d in o.deps:
                    d.sig = True
        last = {}
        for e in ENGS:
            c = 0
            for o in self.ops[e]:
                if o.kind == "c":
                    last[e] = o
            if e in last:
                last[e].sig = True
            for o in self.ops[e]:
                if o.kind == "c" and o.sig:
                    c += 1
                    o.val = c
                    o.sem = self.eng_sems[e]
        finals = [(o.sem, o.val) for o in last.values()] + [(ev.sem, ev.count) for ev in self.evs + self.extra if ev.count > 0]
        all_sems = list(self.eng_sems.values()) + [ev.sem for ev in self.evs]

        def run(ctx):
            if pre is not None:
                pre(ctx)
            for ename in ENGS:
                eng = getattr(nc, ename)
                seen = {}

                def wait(sem, val):
                    k = id(sem)
                    if seen.get(k, 0) >= val:
                        return
                    eng.wait_ge(sem, val)
                    seen[k] = val
                for o in self.ops[ename]:
                    for d in o.deps:
                        wait(d.sem, d.val)
                    ins = o.fn(eng, ctx)
                    if o.kind == "d":
                        ins.then_inc(o.sem, 16)
                    elif o.sig:
                        ins.then_inc(o.sem, 1)
                for (s_, v_) in finals:
                    wait(s_, v_)
            nc.all_engine_barrier()
            for s_ in all_sems:
                nc.gpsimd.sem_clear(s_)
            nc.all_engine_barrier()

        if len(loops) == 0:
            run({})
        elif len(loops) == 1:
            with nc.Fori(0, loops[0], hint_back_edge=True) as a0:
                run({"i0": a0})
        else:
            with nc.Fori(0, loops[0]) as a0:
                with nc.Fori(0, loops[1]) as a1:
                    run({"i0": a0, "i1": a1})
        for ev in self.evs:
            ev.count = 0


def dsl(start, size):
    return slice(start, start + size) if isinstance(start, int) else bass.ds(start, size)


def qk_col(ci):
    if ci < 8:
        return 128 * ci
    if ci < 10:
        return 1024 + 128 * (ci - 8)
    g, r = divmod(ci - 10, 8)
    base = 1536 + 1536 * g
    return base + 128 * r if r < 4 else base + 512 + 128 * (r - 4)


V_GROUPS = [(1280, 256, 0), (1536 + 1024, 512, 256), (3072 + 1024, 512, 768), (4608 + 1024, 512, 1280)]


class H:
    def __init__(self, P):
        self.P = P

    def tt(self, eng, out, in0, in1, op, deps=()):
        return self.P.op(eng, lambda e, ctx: e.tensor_tensor(out=out, in0=in0, in1=in1, op=op), deps)

    def ts(self, eng, out, in0, s1, op0, deps=(), s2=None, op1=None):
        if op1 is None:
            return self.P.op(eng, lambda e, ctx: e.tensor_scalar(out=out, in0=in0, scalar1=s1, scalar2=None, op0=op0), deps)
        return self.P.op(eng, lambda e, ctx: e.tensor_scalar(out=out, in0=in0, scalar1=s1, scalar2=s2, op0=op0, op1=op1), deps)

    def stt(self, eng, out, in0, scalar, in1, op0, op1, deps=()):
        return self.P.op(eng, lambda e, ctx: e.scalar_tensor_tensor(out=out, in0=in0, scalar=scalar, in1=in1, op0=op0, op1=op1), deps)

    def act(self, out, in_, func, deps=(), scale=None, bias=None, accum=None):
        kw = {}
        if scale is not None:
            kw["scale"] = scale
        if bias is not None:
            kw["bias"] = bias
        if accum is not None:
            kw["accum_out"] = accum
        return self.P.op("scalar", lambda e, ctx: e.activation(out=out, in_=in_, func=func, **kw), deps)

    def cp(self, eng, out, in_, deps=()):
        return self.P.op(eng, lambda e, ctx: e.tensor_copy(out=out, in_=in_), deps)

    def recip(self, out, in_, deps=()):
        return self.P.op("vector", lambda e, ctx: e.reciprocal(out=out, in_=in_), deps)

    def memset(self, eng, out, val, deps=()):
        return self.P.op(eng, lambda e, ctx: e.memset(out, val), deps)

    def asel(self, out, in_, cmp, fill, base, pattern, cm, deps=()):
        return self.P.op("gpsimd", lambda e, ctx: e.affine_select(out=out, in_=in_, compare_op=cmp, fill=fill, base=base, pattern=pattern, channel_multiplier=cm), deps)

    def dma(self, eng, out, in_, ev, deps=()):
        def f(e, ctx):
            o_ = out(ctx) if callable(out) else out
            i_ = in_(ctx) if callable(in_) else in_
            return e.dma_start(out=o_, in_=i_)
        return self.P.dma(eng, f, ev, deps)

    def mms(self, lst, deps=()):
        lst = list(lst)

        def f(e, ctx):
            ins = None
            for (out, lhsT, rhs, st, sp) in lst:
                ins = e.matmul(out, lhsT=lhsT, rhs=rhs, start=st, stop=sp)
            return ins
        return self.P.op("tensor", f, deps)

    def trs(self, lst, ident, deps=()):
        lst = list(lst)

        def f(e, ctx):
            ins = None
            for (out, in_) in lst:
                ins = e.transpose(out=out, in_=in_, identity=ident)
            return ins
        return self.P.op("tensor", f, deps)


def build_program(groups, debug=False, phases=(0, 1, 2, 3, 4)):
    NT = sum(c * L for c, L in groups)
    LMAX = max(L for _, L in groups)
    NTI = 2
    NTI3 = 4
    assert NT % (TT * NTI3) == 0

    nc = bass.Bass("TRN2", target_bir_lowering=False)
    NB = NT // 1024
    x8 = [nc.dram_tensor(f"x{k}", [NB, 128, D], F32, kind="ExternalInput").ap() for k in range(8)]
    w_in_d = nc.dram_tensor("w_in", [D, DIN], F32, kind="ExternalInput").ap()
    sink_d = nc.dram_tensor("sink", [1, 8], F32, kind="ExternalInput").ap()
    w_a_d = nc.dram_tensor("w_a", [D, D], F32, kind="ExternalInput").ap()
    w_b_d = nc.dram_tensor("w_b", [512, D], F32, kind="ExternalInput").ap()
    w_o_d = nc.dram_tensor("w_o", [D, D], F32, kind="ExternalInput").ap()
    g1_d = nc.dram_tensor("g_pre_mix", [1, D], F32, kind="ExternalInput").ap()
    g2_d = nc.dram_tensor("g_post_mix", [1, D], F32, kind="ExternalInput").ap()
    g3_d = nc.dram_tensor("g_pre_mlp", [1, D], F32, kind="ExternalInput").ap()
    g4_d = nc.dram_tensor("g_post_mlp", [1, D], F32, kind="ExternalInput").ap()
    w_1_d = nc.dram_tensor("w_1", [D, DFF], F32, kind="ExternalInput").ap()
    w_2_d = nc.dram_tensor("w_2", [DFF, D], F32, kind="ExternalInput").ap()
    invf_d = nc.dram_tensor("invf", [128, 1], F32, kind="ExternalInput").ap()
    y8 = [nc.dram_tensor(f"y{k}", [NB, 128, D], F32, kind="ExternalOutput").ap() for k in range(8)]

    skind = "ExternalOutput" if debug else "Internal"
    BK = 1024
    qk_t = [nc.dram_tensor(f"qk{c}", [128, NT], BF16, kind="Internal").ap() for c in range(34)]
    v_s = nc.dram_tensor("v_s", [NT, 1792], BF16, kind=skind).ap()
    gt_t = [nc.dram_tensor(f"gt{c}", [128, NT], F32, kind="Internal").ap() for c in range(16)]
    o_t = [nc.dram_tensor(f"o{c}", [128, NT], BF16, kind="Internal").ap() for c in range(12)]
    x1_8 = [nc.dram_tensor(f"x1_{k}", [NB, 128, D], F32, kind="Internal").ap() for k in range(8)]
    cos_s = nc.dram_tensor("cos_s", [NB, 128, BK], F32, kind="Internal").ap()
    sin_s = nc.dram_tensor("sin_s", [NB, 128, BK], F32, kind="Internal").ap()

    with ExitStack() as gs:
        evs = []
        sem_ctr = [0]

        def new_sem(nm):
            sem_ctr[0] += 1
            return gs.enter_context(nc.semaphore(f"{nm}{sem_ctr[0]}"))

        def new_ev(nm="ev"):
            e = Ev(new_sem(nm))
            evs.append(e)
            return e

        ENG_SEMS = {e: new_sem("p" + e[:2]) for e in ENGS}

        def new_prog(extra=()):
            return Prog(nc, ENG_SEMS, evs, extra)

        EVW = [Ev(new_sem("wld")) for _ in range(5)]

        def sb(es, name, shape, dt):
            return es.enter_context(nc.sbuf_tensor(name, shape, dt))

        def pstile(es, name, shape, dt=F32):
            return es.enter_context(nc.psum_tensor(name, shape, dt))

        EVP = [new_ev() for _ in range(24)]

        ident_t = sb(gs, "ident", [128, 128], BF16)
        ones_t = sb(gs, "ones_b", [128, 128], BF16)
        maskA = sb(gs, "maskA", [128, 3, 128], BF16)
        maskB = sb(gs, "maskB", [128, 256], BF16)
        esink = sb(gs, "esink", [128, 8], F32)
        ident = ident_t[:]
        ones_b = ones_t[:]

        posblk = []
        for (c_, L_) in groups:
            for _ in range(c_):
                posblk += list(range(L_ // BK))
        gbase = []
        o = 0
        for (c, L) in groups:
            gbase.append(o)
            o += c * L

        if 0 in phases:
            with ExitStack() as es:
                P = new_prog()
                h = H(P)
                invf = sb(es, "invf_sb", [128, 1], F32)
                sgn = sb(es, "sgn", [128, 1], F32)
                posi = sb(es, "posi", [128, LMAX], I32)
                ang = sb(es, "ang", [128, LMAX], F32)
                t1 = sb(es, "t1", [128, LMAX], F32)
                t2 = sb(es, "t2", [128, LMAX], F32)
                ki = sb(es, "ki", [128, LMAX], I32)
                res = sb(es, "res", [128, LMAX], F32)
                ld = h.dma("sync", invf[:], invf_d, EVP[0])
                lds = h.dma("sync", esink[:], sink_d.partition_broadcast(128), EVP[1])
                h.memset("gpsimd", ident, 0.0)
                h.asel(ident, ident, ALU.not_equal, 1.0, 0, [[-1, 128]], 1)
                h.memset("gpsimd", ones_b, 1.0)
                h.memset("gpsimd", maskA[:], 1.0)
                h.asel(maskA[:, 0, :], maskA[:, 0, :], ALU.is_ge, 0.0, 0, [[1, 128]], -1)
                h.asel(maskA[:, 2, :], maskA[:, 2, :], ALU.is_ge, 0.0, 0, [[-1, 128]], 1)
                h.memset("gpsimd", maskB[:], 1.0)
                h.asel(maskB[:], maskB[:], ALU.is_ge, 0.0, 0, [[1, 256]], -1)
                h.asel(maskB[:], maskB[:], ALU.is_ge, 0.0, 128, [[-1, 256]], 1)
                h.memset("gpsimd", sgn[0:64, :], 1.0)
                sg = h.memset("gpsimd", sgn[64:128, :], -1.0)
                io = P.op("gpsimd", lambda e, ctx: e.iota(posi[:], pattern=[[1, LMAX]], base=0, channel_multiplier=0))
                h.act(esink[:], esink[:], AF.Exp, [lds])
                a0 = h.cp("vector", t1[:], posi[:], [io])
                a1 = h.ts("vector", ang[:], t1[:], invf[:, 0:1], ALU.mult, [a0, ld])

                def table(src_dep, shift, dst_dram, sign_ap, evx):
                    d = h.ts("vector", t2[:], ang[:], shift, ALU.add, [src_dep], s2=1.0 / TWO_PI, op1=ALU.mult)
                    d = h.cp("vector", ki[:], t2[:], [d])
                    d = h.cp("vector", t2[:], ki[:], [d])
                    d0 = h.ts("vector", t1[:], ang[:], shift, ALU.add, [d])
                    d = h.stt("vector", res[:], t2[:], -TWO_PI, t1[:], ALU.mult, ALU.add, [d0])
                    d = h.ts("vector", t2[:], res[:], PI, ALU.is_gt, [d], s2=-TWO_PI, op1=ALU.mult)
                    d = h.tt("vector", res[:], res[:], t2[:], ALU.add, [d])
                    d = h.ts("vector", t2[:], res[:], -PI, ALU.is_lt, [d], s2=TWO_PI, op1=ALU.mult)
                    d = h.tt("vector", res[:], res[:], t2[:], ALU.add, [d])
                    d = h.ts("vector", res[:], res[:], PI, ALU.min, [d], s2=-PI, op1=ALU.max)
                    d = h.act(res[:], res[:], AF.Sin, [d])
                    if sign_ap is not None:
                        d = h.ts("vector", res[:], res[:], sign_ap, ALU.mult, [d, sg])
                    l = None
                    for U_, pb in enumerate(posblk):
                        l = h.dma("sync", dst_dram[U_], res[:, pb * BK:(pb + 1) * BK], evx, [d])
                    return l

                st = table(a1, PI / 2, cos_s, None, EVP[2])
                table(st, 0.0, sin_s, sgn[:, 0:1], EVP[3])
                P.emit()

        if 1 in phases:
            with ExitStack() as es:
                w_sb = sb(es, "w_in_sb", [128, 8, DIN], BF16)
                g1b = sb(es, "g1b", [128, D], F32)
                stg = [sb(es, f"stg{i}", [128, 2, TT], F32) for i in range(2)]
                NXB = 2
                xb = [sb(es, f"xb{i}", [128, D], F32) for i in range(NXB)]
                junk = sb(es, "junk1", [128, D], BF16)
                hn = [sb(es, f"hn{i}", [128, D], BF16) for i in range(2)]
                ssq = [sb(es, f"ssq{i}", [128, 1], F32) for i in range(2)]
                rstd = [sb(es, f"rstd{i}", [128, 1], F32) for i in range(2)]
                hT = [sb(es, f"hT{i}", [128, 8, TT], BF16) for i in range(2)]
                RI_ = TT * NTI
                cosb2 = sb(es, "cosb2", [128, RI_], F32)
                sinb2 = sb(es, "sinb2", [128, RI_], F32)
                cosb = [cosb2[:, i * TT:(i + 1) * TT] for i in range(2)]
                sinb = [sinb2[:, i * TT:(i + 1) * TT] for i in range(2)]
                NRT = 3
                rt_a = [sb(es, f"rta{i}", [128, TT], F32) for i in range(NRT)]
                rt_c = [sb(es, f"rtc{i}", [128, TT], F32) for i in range(NRT)]
                NST = 2
                stq = [sb(es, f"stq{i}", [128, 4, TT], BF16) for i in range(NST)]
                stv = [sb(es, f"stv{i}", [128, 1792], BF16) for i in range(2)]
                NPS = 5
                psb = [pstile(es, f"ps1_{i}", [128, 512]) for i in range(NPS)]
                pst = [pstile(es, f"pst_{i}", [128, 8, 128], BF16) for i in range(2)]
                ev_x = EVP[1:1 + NXB]
                ev_tab = EVP[4:6]
                ev_stq = EVP[6:8]
                ev_stv = EVP[8:10]
                ev_stg = EVP[11:13]

                P = new_prog([EVW[1]])
                h = H(P)
                for kc in range(8):
                    for hh in range(4):
                        h.dma("gpsimd", w_sb[:, kc, hh * 2048:(hh + 1) * 2048], w_in_d[kc * 128:(kc + 1) * 128, hh * 2048:(hh + 1) * 2048], EVW[1])
                h.dma("sync", g1b[:], g1_d.partition_broadcast(128), EVP[10])
                P.emit()

                RI = TT * NTI
                assert RI == BK
                v_v1 = v_s.rearrange("(n r) c -> n r c", r=RI)

                v_l = nc.dram_tensor("v_l1", [RI, 1792], BF16, kind="Internal").ap()

                def p1_body(loops):
                    T0 = lambda ctx: ctx["i0"]
                    P0 = lambda ctx: ctx["i0"]
                    stores = []
                    P = new_prog()
                    h = H(P)
                    xb_rd = [None] * NXB
                    hn_rd = [None] * 2
                    rt_rd = [None] * NRT
                    stq_rd = [None] * NST
                    stv_rd = [None] * 2
                    stg_rd = [None] * 2
                    ps_rd = [[] for _ in range(NPS)]
                    pst_rd = [None] * 2
                    ssq_rd = [None] * 2
                    cnt = {"x": 0, "ps": 0, "rt": 0, "stq": 0, "stv": 0, "pst": 0}
                    tl_box = [None]
                    for tl in range(NTI):
                        toff = tl * TT
                        hb = tl % 2
                        hT_w = []
                        for s in range(4):
                            xi = cnt["x"] % NXB
                            cnt["x"] += 1
                            roff = toff + 128 * s
                            xl = h.dma("sync", xb[xi][:], (lambda ctx, k=tl * 4 + s: x8[k][T0(ctx)]), ev_x[xi], [xb_rd[xi]])
                            if tl == 0 and s == 1:
                                h.dma("sync", cosb2[:], (lambda ctx: cos_s[P0(ctx)]), ev_tab[0])
                                tl_box[0] = h.dma("sync", sinb2[:], (lambda ctx: sin_s[P0(ctx)]), ev_tab[0])
                            sp = s % 2
                            sq = h.act(junk[:], xb[xi][:], AF.Square, [xl, ssq_rd[sp]], accum=ssq[sp][:])
                            r1 = h.act(rstd[sp][:], ssq[sp][:], AF.Sqrt, [sq, hn_rd[sp]], scale=1.0 / D, bias=EPS)
                            r2 = h.recip(rstd[sp][:], rstd[sp][:], [r1])
                            ssq_rd[sp] = r2
                            hq = h.stt("vector", hn[sp][:], xb[xi][:], rstd[sp][:, 0:1], g1b[:], ALU.mult, ALU.mult, [r2, xl, hn_rd[sp]])
                            xb_rd[xi] = hq
                            pi_ = cnt["pst"] % 2
                            cnt["pst"] += 1
                            trp = h.trs([(pst[pi_][:, kc, :], hn[sp][:, kc * 128:(kc + 1) * 128]) for kc in range(8)], ident, [hq, pst_rd[pi_]])
                            hn_rd[sp] = trp
                            evc = h.cp("vector", hT[hb][:, :, 128 * s:128 * (s + 1)], pst[pi_][:], [trp])
                            pst_rd[pi_] = evc
                            hT_w.append(evc)

                        def fm_matmul(col, deps):
                            pi = cnt["ps"] % NPS
                            cnt["ps"] += 1
                            m = h.mms([(psb[pi][:], w_sb[:, kc, col:col + 128], hT[hb][:, kc, :], kc == 0, kc == 7) for kc in range(8)], deps + ps_rd[pi])
                            ps_rd[pi] = []
                            return pi, m

                        first = hT_w
                        tl_ = tl_box[0]
                        for c0 in range(0, 34, 4):
                            nch = min(4, 34 - c0)
                            si = cnt["stq"] % NST
                            cnt["stq"] += 1
                            writers = []
                            for j in range(nch):
                                pi, m = fm_matmul(qk_col(c0 + j), first)
                                first = []
                                ri = cnt["rt"] % NRT
                                cnt["rt"] += 1
                                d1 = h.tt("vector", rt_a[ri][0:64, :], psb[pi][64:128, :], sinb2[64:128, hb * TT:(hb + 1) * TT], ALU.mult, [m, tl_, rt_rd[ri]])
                                d2 = h.tt("vector", rt_a[ri][64:128, :], psb[pi][0:64, :], sinb2[0:64, hb * TT:(hb + 1) * TT], ALU.mult, [m, tl_, rt_rd[ri]])
                                d3 = h.tt("vector", rt_c[ri][:], psb[pi][:], cosb2[:, hb * TT:(hb + 1) * TT], ALU.mult, [m, tl_, rt_rd[ri]])
                                ps_rd[pi] = [d3]
                                d4 = h.tt("gpsimd", stq[si][:, j, :], rt_c[ri][:], rt_a[ri][:], ALU.add, [d1, d2, d3, stq_rd[si]])
                                rt_rd[ri] = d4
                                writers.append(d4)
                            for j in range(nch):
                                stq_rd[si] = h.dma("sync", (lambda ctx, c=c0 + j, o=toff: qk_t[c][:, bass.ds(T0(ctx) * BK + o, TT)]), stq[si][:, j, :], ev_stq[si], writers)
                        for c0 in range(0, 16, 2):
                            si = (c0 // 2) % 2
                            writers = []
                            for j in range(2):
                                pi, m = fm_matmul(6144 + 128 * (c0 + j), [])
                                a = h.act(stg[si][:, j, :], psb[pi][:], AF.Sigmoid, [m, stg_rd[si]])
                                ps_rd[pi] = [a]
                                writers.append(a)
                            for j in range(2):
                                stg_rd[si] = h.dma("scalar", (lambda ctx, c=c0 + j, o=toff: gt_t[c][:, bass.ds(T0(ctx) * BK + o, TT)]), stg[si][:, j, :], ev_stg[si], writers)
                        for s in range(4):
                            vi = cnt["stv"] % 2
                            cnt["stv"] += 1
                            writers = []
                            for (wc, width, vc) in V_GROUPS:
                                pi = cnt["ps"] % NPS
                                cnt["ps"] += 1
                                m = h.mms([(psb[pi][:, 0:width], hT[hb][:, kc, 128 * s:128 * (s + 1)], w_sb[:, kc, wc:wc + width], kc == 0, kc == 7) for kc in range(8)], ps_rd[pi])
                                a = h.act(stv[vi][:, vc:vc + width], psb[pi][:, 0:width], AF.Copy, [m, stv_rd[vi]])
                                ps_rd[pi] = [a]
                                writers.append(a)
                            roff = toff + 128 * s
                            stv_rd[vi] = h.dma("scalar", (lambda ctx, o=roff: v_s[bass.ds(T0(ctx) * BK + o, 128), :]), stv[vi][:], ev_stv[vi], writers)
                    P.emit(loops)

                p1_body((NB,))

        if 2 in phases:
            with ExitStack() as es:
                NQ = 2
                qT = [sb(es, f"qT{i}", [128, 4, LMAX], BF16) for i in range(NQ)]
                kT = [sb(es, f"kT{i}", [128, LMAX], BF16) for i in range(NQ)]
                vt = [sb(es, f"vt{i}", [128, LMAX // 128, 128], BF16) for i in range(NQ)]
                NPT = 16
                pT = [sb(es, f"pT{i}", [128, 512], BF16) for i in range(NPT)]
                accOD = sb(es, "accOD", [128, 2, LMAX], F32)
                rdn = [sb(es, f"rdn{i}", [128, 512], F32) for i in range(2)]
                ost = [sb(es, f"ost{i}", [128, 512], BF16) for i in range(2)]
                ostB = [sb(es, f"ostB{i}", [128, LMAX], BF16) for i in range(2)]
                NS = 4
                psS = [pstile(es, f"psS{i}", [128, 512]) for i in range(NS)]
                psO = [pstile(es, f"psO{i}", [128, 512]) for i in range(2)]
                psD = [pstile(es, f"psD{i}", [128, 512]) for i in range(2)]
                ev_q = EVP[0:2]
                ev_ost = EVP[2:4]
                ev_ostB = EVP[4:6]
                JCH = 8

                v_l2 = nc.dram_tensor("v_l2", [LMAX, 1792], BF16, kind="Internal").ap()
                ostA = sb(es, "ostA", [128, 4, LMAX], BF16)

                def p2_body(UF, L, loops):
                    SO = lambda ctx: ctx["U"]
                    stores = []
                    P = new_prog()
                    h = H(P)
                    q_rd = [[] for _ in range(NQ)]
                    pT_rd = [[] for _ in range(NPT)]
                    psS_rd = [None] * NS
                    psO_rd = [None] * 2
                    rdn_rd = [None] * 2
                    ost_rd = [None] * 2
                    ostB_rd = [None] * 2
                    cnt = {"q": 0, "pT": 0, "S": 0, "O": 0, "ost": 0, "ostB": 0}
                    nb = L // 128
                    v_v = v_s.rearrange("(n r) c -> n r c", r=L)
                    cin2 = h.dma("sync", v_l2[0:L, :], (lambda ctx: v_v[SO(ctx)]), EVP[14])
                    ostA_rd = []

                    def row(c):
                        return lambda ctx: qk_t[c][:, bass.ds(SO(ctx) * L, L)]

                    def load_v(dst, src_fn, nbk, ev, deps):
                        l = None
                        for j0 in range(0, nbk, JCH):
                            j1 = min(nbk, j0 + JCH)
                            l = h.dma("sync", dst[:, j0:j1, :], (lambda ctx, j0=j0, j1=j1: src_fn(ctx)[:, j0:j1, :]), ev, deps)
                        return l

                    for hk in range(2):
                        qi = cnt["q"] % NQ
                        cnt["q"] += 1
                        dq = q_rd[qi]
                        for hh_ in range(4):
                            h.dma("sync", qT[qi][:, hh_, 0:L], row(4 * hk + hh_), ev_q[qi], dq + [cin2])
                        h.dma("sync", kT[qi][:, 0:L], row(8 + hk), ev_q[qi], dq)
                        l3 = load_v(vt[qi], (lambda ctx, hk=hk: v_l2[0:L, 128 * hk:128 * hk + 128].rearrange("(j p) c -> p j c", p=128)), nb, ev_q[qi], dq)
                        q_rd[qi] = []
                        pts = {}
                        last_pe = [None]
                        last_d3 = [None]

                        def do_pv_A(i):
                            oi = cnt["O"] % 2
                            cnt["O"] += 1
                            js = [j for j in (i - 1, i, i + 1) if 0 <= j < nb]
                            deps = [pts[(j, hh)][1] for j in js for hh in range(4)] + [psO_rd[oi], l3]
                            lst = []
                            for dst, use_v in ((psO[oi], True), (psD[oi], False)):
                                for hh in range(4):
                                    for n, j in enumerate(js):
                                        pi_, _, qlo = pts[(j, hh)]
                                        blk = i - qlo
                                        lhs = vt[qi][:, j, :] if use_v else ones_b
                                        lst.append((dst[:, 128 * hh:128 * (hh + 1)], lhs, pT[pi_][:, 128 * blk:128 * (blk + 1)], n == 0, n == len(js) - 1))
                            m = h.mms(lst, deps)
                            last_pe[0] = m
                            for j in js:
                                for hh in range(4):
                                    pT_rd[pts[(j, hh)][0]].append(m)
                            d1 = h.tt("vector", rdn[oi][:].rearrange("p (h q) -> p h q", h=4), psD[oi][:].rearrange("p (h q) -> p h q", h=4),
                                      esink[:, 4 * hk:4 * hk + 4].unsqueeze(2).to_broadcast([128, 4, 128]), ALU.add, [m, rdn_rd[oi]])
                            l1 = h.act(rdn[oi][:], rdn[oi][:], AF.Ln, [d1])
                            d2 = h.act(rdn[oi][:], rdn[oi][:], AF.Exp, [l1], scale=-1.0)
                            d3 = h.tt("vector", ostA[:, :, 128 * i:128 * (i + 1)], psO[oi][:].rearrange("p (h q) -> p h q", h=4), rdn[oi][:].rearrange("p (h q) -> p h q", h=4), ALU.mult, [d2] + ostA_rd)
                            psO_rd[oi] = d3
                            rdn_rd[oi] = d3
                            last_d3[0] = d3

                        for j in range(nb):
                            qlo = max(0, j - 1)
                            qhi = min(nb, j + 2)
                            nq = qhi - qlo
                            for hh in range(4):
                                s_i = cnt["S"] % NS
                                cnt["S"] += 1
                                pi_ = cnt["pT"] % NPT
                                cnt["pT"] += 1
                                m = h.mms([(psS[s_i][:, 0:128 * nq], kT[qi][:, 128 * j:128 * (j + 1)], qT[qi][:, hh, 128 * qlo:128 * (qlo + nq)], True, True)], [l3, psS_rd[s_i]])
                                last_pe[0] = m
                                a = h.act(pT[pi_][:, 0:128 * nq], psS[s_i][:, 0:128 * nq], AF.Exp, [m] + pT_rd[pi_], scale=SCALE)
                                pT_rd[pi_] = []
                                psS_rd[s_i] = a
                                rdy = a
                                for qb in range(qlo, qhi):
                                    if qb == j:
                                        continue
                                    mb = 0 if qb == j - 1 else 2
                                    blk = qb - qlo
                                    rdy = h.tt("gpsimd", pT[pi_][:, 128 * blk:128 * (blk + 1)], pT[pi_][:, 128 * blk:128 * (blk + 1)], maskA[:, mb, :], ALU.mult, [rdy])
                                pts[(j, hh)] = (pi_, rdy, qlo)
                            if j >= 2:
                                do_pv_A(j - 2)
                        if nb >= 2:
                            do_pv_A(nb - 2)
                        do_pv_A(nb - 1)
                        q_rd[qi] = [last_pe[0]]
                        ostA_rd = []
                        for hh_ in range(4):
                            ostA_rd.append(h.dma("sync", (lambda ctx, c=4 * hk + hh_: o_t[c][:, bass.ds(SO(ctx) * L, L)]), ostA[:, hh_, 0:L], ev_ost[0], [last_d3[0]]))
                        ostA_rd = [ostA_rd[-1]]

                    acc_guard = None
                    bslots = [psO[0], psO[1], psD[0], psD[1]]
                    bslot_rd = [psO_rd[0], psO_rd[1], psO_rd[0], psO_rd[1]]
                    cnt["OB"] = 0
                    for hb_ in range(4):
                        bi = cnt["ostB"] % 2
                        cnt["ostB"] += 1
                        last_acc = [None, None]
                        for g, dil in enumerate(B_DIL):
                            Lf = L // dil
                            nbf = Lf // 128
                            qi = cnt["q"] % NQ
                            cnt["q"] += 1
                            dq = q_rd[qi]
                            cq = 10 + 8 * g + hb_
                            ck = 10 + 8 * g + 4 + hb_
                            vc = 256 + 512 * g + 128 * hb_
                            h.dma("sync", qT[qi][:, 0, 0:L], row(cq), ev_q[qi], dq)
                            h.dma("sync", kT[qi][:, 0:L], row(ck), ev_q[qi], dq)
                            l3 = None
                            for r in range(dil):
                                l3 = load_v(vt[qi][:, r * nbf:(r + 1) * nbf, :],
                                            (lambda ctx, vc=vc, dil=dil, r=r: v_l2[0:L, vc:vc + 128].rearrange("(j p r) c -> r p j c", p=128, r=dil)[r]),
                                            nbf, ev_q[qi], dq)
                            q_rd[qi] = []
                            last_pe = [None]
                            qv = qT[qi][:, 0, 0:L].rearrange("p (s r) -> p r s", r=dil)
                            kv = kT[qi][:, 0:L].rearrange("p (s r) -> p r s", r=dil)
                            aOD = accOD[:, :, 0:L].rearrange("p t (s r) -> p t r s", r=dil)
                            pts = {}
                            pend = []
                            step = [0]
                            LAG = 3

                            def do_pv_B(r, i):
                                sl = cnt["OB"] % 4
                                cnt["OB"] += 1
                                bank = bslots[sl]
                                contrib = [(i, 64, 128, 0)]
                                if i - 1 >= 0:
                                    contrib.append((i - 1, 192, 64, 0))
                                if i + 1 < nbf:
                                    contrib.append((i + 1, 0, 64, 64))
                                deps = [pts[(r, j)][1] for (j, _, _, _) in contrib] + [bslot_rd[sl], l3]
                                lst = []
                                for doff, use_v in ((0, True), (128, False)):
                                    for n, (j, c0, w, d0) in enumerate(contrib):
                                        pi_, _, clo = pts[(r, j)]
                                        lhs = vt[qi][:, r * nbf + j, :] if use_v else ones_b
                                        lst.append((bank[:, doff + d0:doff + d0 + w], lhs, pT[pi_][:, c0 - clo:c0 - clo + w], n == 0, n == len(contrib) - 1))
                                m = h.mms(lst, deps)
                                last_pe[0] = m
                                for (j, _, _, _) in contrib:
                                    pT_rd[pts[(r, j)][0]].append(m)
                                dsl_ = slice(128 * i, 128 * (i + 1))
                                bankv = bank[:, 0:256].rearrange("p (t q) -> p t q", t=2)
                                if g == 0:
                                    d2 = h.cp("vector", aOD[:, :, r, dsl_], bankv, [m, acc_guard])
                                else:
                                    d2 = h.tt("vector", aOD[:, :, r, dsl_], bankv, aOD[:, :, r, dsl_], ALU.add, [m])
                                bslot_rd[sl] = d2
                                last_acc[0], last_acc[1] = d2, d2

                            for r in range(dil):
                                for j in range(nbf):
                                    q_lo = 128 * j - 64
                                    c_lo = 64 if j == 0 else 0
                                    c_hi = 192 if j == nbf - 1 else 256
                                    nqc = c_hi - c_lo
                                    s_i = cnt["S"] % NS
                                    cnt["S"] += 1
                                    pi_ = cnt["pT"] % NPT
                                    cnt["pT"] += 1
                                    m = h.mms([(psS[s_i][:, 0:nqc], kv[:, r, 128 * j:128 * (j + 1)], qv[:, r, q_lo + c_lo:q_lo + c_lo + nqc], True, True)], [l3, psS_rd[s_i]])
                                    last_pe[0] = m
                                    a = h.act(pT[pi_][:, 0:nqc], psS[s_i][:, 0:nqc], AF.Exp, [m] + pT_rd[pi_], scale=SCALE)
                                    pT_rd[pi_] = []
                                    psS_rd[s_i] = a
                                    mk_ = h.tt("gpsimd", pT[pi_][:, 0:nqc], pT[pi_][:, 0:nqc], maskB[:, c_lo:c_lo + nqc], ALU.mult, [a])
                                    pts[(r, j)] = (pi_, mk_, c_lo)
                                    if j >= 1:
                                        pend.append((r, j - 1, step[0]))
                                    if j == nbf - 1:
                                        pend.append((r, j, step[0]))
                                    step[0] += 1
                                    while pend and pend[0][2] <= step[0] - 1 - LAG:
                                        r_, i_, _ = pend.pop(0)
                                        do_pv_B(r_, i_)
                            while pend:
                                r_, i_, _ = pend.pop(0)
                                do_pv_B(r_, i_)
                            q_rd[qi] = [last_pe[0]]
                        l1 = h.act(accOD[:, 1, 0:L], accOD[:, 1, 0:L], AF.Ln, [last_acc[0], last_acc[1]])
                        d1 = h.act(accOD[:, 1, 0:L], accOD[:, 1, 0:L], AF.Exp, [l1], scale=-1.0)
                        d2 = h.tt("gpsimd", ostB[bi][:, 0:L], accOD[:, 0, 0:L], accOD[:, 1, 0:L], ALU.mult, [d1, ostB_rd[bi]])
                        acc_guard = d2
                        ostB_rd[bi] = h.dma("sync", (lambda ctx, c=8 + hb_: o_t[c][:, bass.ds(SO(ctx) * L, L)]), ostB[bi][:, 0:L], ev_ostB[bi], [d2])
                    P.emit(loops, pre=lambda ctx: ctx.__setitem__("U", nc.sync.snap(UF(ctx))))

                for gi, (cg, L) in enumerate(groups):
                    base = gbase[gi]
                    p2_body(lambda ctx, base=base, L=L: base // L + ctx["i0"], L, (cg,))

        if 3 in phases:
            with ExitStack() as es:
                wa = sb(es, "wa", [128, 8, D], BF16)
                wb_ = sb(es, "wb", [128, 4, D], BF16)
                wo = sb(es, "wo", [128, 8, D], BF16)
                g2b = sb(es, "g2b", [128, D], F32)
                oT = [sb(es, f"oT{i}", [128, 12, TT], BF16) for i in range(2)]
                gT = [sb(es, f"gT{i}", [128, 16, TT], F32) for i in range(2)]
                mixT = [sb(es, f"mixT{i}", [128, 8, TT], BF16) for i in range(2)]
                ta = [sb(es, f"ta{i}", [128, TT], F32) for i in range(2)]
                tb = [sb(es, f"tb{i}", [128, TT], F32) for i in range(2)]
                xb3 = [sb(es, f"xb3{i}", [128, D], F32) for i in range(2)]
                nrm = [sb(es, f"nrm{i}", [128, D], F32) for i in range(2)]
                junk3 = sb(es, "junk3", [128, 512], BF16)
                ssa = [sb(es, f"ssa{i}", [128, 2], F32) for i in range(2)]
                rs3 = [sb(es, f"rs3{i}", [128, 1], F32) for i in range(2)]
                psY = [pstile(es, f"psY{i}", [128, 512]) for i in range(4)]
                psM = [pstile(es, f"psM{i}", [128, 2, 512]) for i in range(2)]
                ev_o = EVP[1:3]
                ev_x3 = EVP[3:5]
                ev_x1 = EVP[5:7]
                P = new_prog([EVW[3]])
                h = H(P)
                for kc in range(8):
                    h.dma("gpsimd", wa[:, kc, :], w_a_d[kc * 128:(kc + 1) * 128, :], EVW[3])
                    h.dma("gpsimd", wo[:, kc, :], w_o_d[kc * 128:(kc + 1) * 128, :], EVW[3])
                for kc in range(4):
                    h.dma("gpsimd", wb_[:, kc, :], w_b_d[kc * 128:(kc + 1) * 128, :], EVW[3])
                h.dma("sync", g2b[:], g2_d.partition_broadcast(128), EVP[10])
                P.emit()

                P = new_prog()
                h = H(P)
                T0 = lambda ctx: ctx["i0"]
                U3 = lambda ctx, tl: ctx["i0"] * (NTI3 // 2) + tl // 2

                def load_tile(tl, deps):
                    b = tl % 2
                    o = (tl % 2) * TT
                    l = None
                    for c_ in range(12):
                        l = h.dma("sync", oT[b][:, c_, :], (lambda ctx, c=c_, o=o, tl=tl: o_t[c][:, bass.ds(U3(ctx, tl) * BK + o, TT)]), ev_o[b], deps)
                    for c_ in range(16):
                        l = h.dma("sync", gT[b][:, c_, :], (lambda ctx, c=c_, o=o, tl=tl: gt_t[c][:, bass.ds(U3(ctx, tl) * BK + o, TT)]), ev_o[b], deps)
                    return l
                o_rd = [[] for _ in range(2)]
                mix_rd = [[] for _ in range(2)]
                psY_rd = [None] * 4
                psM_rd = [None] * 2
                tab_rd3 = [None] * 2
                xb3_rd = [None] * 2
                ss_rd = [None] * 2
                cnt = {"Y": 0, "t": 0, "M": 0, "x": 0}
                lgs = [load_tile(0, []), load_tile(1, [])] + [None] * (NTI3 - 2)
                for tl in range(NTI3):
                    toff = tl * TT
                    b = tl % 2
                    lg = lgs[tl]
                    o_rd[b] = []
                    mix_w = []
                    lastr = []
                    for m_ in range(8):
                        ya = cnt["Y"] % 4
                        cnt["Y"] += 1
                        yb = cnt["Y"] % 4
                        cnt["Y"] += 1
                        ma = h.mms([(psY[ya][:], wa[:, kc, 128 * m_:128 * (m_ + 1)], oT[b][:, kc, :], kc == 0, kc == 7) for kc in range(8)], [lg, psY_rd[ya]])
                        mb = h.mms([(psY[yb][:], wb_[:, kc, 128 * m_:128 * (m_ + 1)], oT[b][:, 8 + kc, :], kc == 0, kc == 3) for kc in range(4)], [lg, psY_rd[yb]])
                        tt_ = cnt["t"] % 2
                        cnt["t"] += 1
                        d1 = h.tt("vector", ta[tt_][:], psY[ya][:], gT[b][:, m_, :], ALU.mult, [ma, lg, tab_rd3[tt_]])
                        d2 = h.tt("vector", tb[tt_][:], psY[yb][:], gT[b][:, 8 + m_, :], ALU.mult, [mb, lg, tab_rd3[tt_]])
                        psY_rd[ya] = d1
                        psY_rd[yb] = d2
                        d3 = h.tt("gpsimd", mixT[b][:, m_, :], ta[tt_][:], tb[tt_][:], ALU.add, [d1, d2] + mix_rd[b])
                        tab_rd3[tt_] = d3
                        mix_w.append(d3)
                        lastr = [ma, mb, d2]
                    mix_rd[b] = []
                    o_rd[b] = lastr
                    if tl + 2 < NTI3:
                        lgs[tl + 2] = load_tile(tl + 2, lastr)
                    for s in range(4):
                        mi = cnt["M"] % 2
                        cnt["M"] += 1
                        xi = cnt["x"] % 2
                        cnt["x"] += 1
                        roff = toff + 128 * s
                        xl = h.dma("sync", xb3[xi][:], (lambda ctx, k=(tl % 2) * 4 + s, tl=tl: x8[k][U3(ctx, tl)]), ev_x3[xi], [xb3_rd[xi]])
                        lst = []
                        for hf in range(2):
                            for kc in range(8):
                                lst.append((psM[mi][:, hf, :], mixT[b][:, kc, 128 * s:128 * (s + 1)], wo[:, kc, 512 * hf:512 * (hf + 1)], kc == 0, kc == 7))
                        mo = h.mms(lst, mix_w + [psM_rd[mi]])
                        mix_w = []
                        mix_rd[b] = [mo]
                        s0 = h.act(junk3[:], psM[mi][:, 0, :], AF.Square, [mo, ss_rd[xi]], accum=ssa[xi][:, 0:1])
                        s1 = h.act(junk3[:], psM[mi][:, 1, :], AF.Square, [mo, s0], accum=ssa[xi][:, 1:2])
                        r1 = h.tt("vector", rs3[xi][:], ssa[xi][:, 0:1], ssa[xi][:, 1:2], ALU.add, [s0, s1, xb3_rd[xi]])
                        r2 = h.act(rs3[xi][:], rs3[xi][:], AF.Sqrt, [r1], scale=1.0 / D, bias=EPS)
                        r3 = h.recip(rs3[xi][:], rs3[xi][:], [r2])
                        ss_rd[xi] = r3
                        n1 = h.act(nrm[xi][:].rearrange("p (h c) -> p h c", h=2), psM[mi][:], AF.Copy, [r3, xb3_rd[xi]], scale=rs3[xi][:, 0:1])
                        psM_rd[mi] = n1
                        n2 = h.tt("vector", nrm[xi][:], nrm[xi][:], g2b[:], ALU.mult, [n1])
                        n3 = h.tt("gpsimd", xb3[xi][:], nrm[xi][:], xb3[xi][:], ALU.add, [n2, xl])
                        xb3_rd[xi] = h.dma("sync", (lambda ctx, k=(tl % 2) * 4 + s, tl=tl: x1_8[k][U3(ctx, tl)]), xb3[xi][:], ev_x1[xi], [n3])
                P.emit((NT // (TT * NTI3),))

        if 4 in phases:
            with ExitStack() as es:
                w1 = sb(es, "w1", [128, 8, DFF], BF16)
                w2 = sb(es, "w2", [128, 32, D], BF16)
                g3b = sb(es, "g3b", [128, D], F32)
                g4b = sb(es, "g4b", [128, D], F32)
                xa = [sb(es, f"xa{i}", [128, D], F32) for i in range(1)]
                xr = [sb(es, f"xr{i}", [128, D], F32) for i in range(1)]
                junk4 = sb(es, "junk4", [128, D], BF16)
                hn4 = [sb(es, f"hn4{i}", [128, D], BF16) for i in range(1)]
                ssq4 = [sb(es, f"ssq4{i}", [128, 1], F32) for i in range(2)]
                rstd4 = [sb(es, f"rstd4{i}", [128, 1], F32) for i in range(2)]
                h2T = [sb(es, f"h2T{i}", [128, 8, TT], BF16) for i in range(2)]
                uT = sb(es, "uT", [128, 32, TT], BF16)
                ur = [sb(es, f"ur{i}", [128, TT], F32) for i in range(2)]
                nr4 = [sb(es, f"nr4{i}", [128, D], F32) for i in range(1)]
                ssb = [sb(es, f"ssb{i}", [128, 2], F32) for i in range(2)]
                rs4 = [sb(es, f"rs4{i}", [128, 1], F32) for i in range(2)]
                psU = [pstile(es, f"psU{i}", [128, 512]) for i in range(3)]
                psT4 = pstile(es, "psT4", [128, 8, 128], BF16)
                psO4 = [pstile(es, f"psO4{i}", [128, 2, 512]) for i in range(2)]
                ev_xa = EVP[1:3]
                ev_xr = EVP[3:5]
                ev_y = EVP[5:7]
                P = new_prog([EVW[4]])
                h = H(P)
                for kc in range(8):
                    for hh in range(2):
                        h.dma("gpsimd", w1[:, kc, hh * 2048:(hh + 1) * 2048], w_1_d[kc * 128:(kc + 1) * 128, hh * 2048:(hh + 1) * 2048], EVW[4])
                for kc in range(32):
                    h.dma("gpsimd", w2[:, kc, :], w_2_d[kc * 128:(kc + 1) * 128, :], EVW[4])
                h.dma("sync", g3b[:], g3_d.partition_broadcast(128), EVP[10])
                h.dma("sync", g4b[:], g4_d.partition_broadcast(128), EVP[11])
                P.emit()

                P = new_prog()
                h = H(P)
                T0 = lambda ctx: ctx["i0"]
                R4 = TT * NTI
                NTI4 = 4 if NT % (TT * 4) == 0 else 2
                UB = lambda ctx, tl: ctx["i0"] * (NTI4 // 2) + tl // 2
                h2T_rd = [None] * 2
                assert R4 == BK
                xa_rd = [None] * 2
                xr_rd = [None] * 2
                hn_rd = [None] * 2
                ssq_rd = [None] * 2
                uT_rd = []
                psU_rd = [None] * 3
                ur_rd = [None] * 2
                psT_rd = None
                psO_rd = [None] * 2
                ss_rd = [None] * 2
                cnt = {"x": 0, "U": 0, "R": 0, "O": 0, "y": 0}

                def norm_sub(tl, s):
                    nonlocal psT_rd
                    b = tl % 2
                    if True:
                        xi = 0
                        xl = h.dma("sync", xa[xi][:], (lambda ctx, k=(tl % 2) * 4 + s, tl=tl: x1_8[k][UB(ctx, tl)]), ev_xa[xi], [xa_rd[xi]])
                        sq = h.act(junk4[:], xa[xi][:], AF.Square, [xl, ssq_rd[xi]], accum=ssq4[xi][:])
                        r1 = h.act(rstd4[xi][:], ssq4[xi][:], AF.Sqrt, [sq, hn_rd[xi]], scale=1.0 / D, bias=EPS)
                        r2 = h.recip(rstd4[xi][:], rstd4[xi][:], [r1])
                        ssq_rd[xi] = r2
                        hq = h.stt("vector", hn4[xi][:], xa[xi][:], rstd4[xi][:, 0:1], g3b[:], ALU.mult, ALU.mult, [r2, xl, hn_rd[xi]])
                        xa_rd[xi] = hq
                        trp = h.trs([(psT4[:, kc, :], hn4[xi][:, kc * 128:(kc + 1) * 128]) for kc in range(8)], ident, [hq, psT_rd])
                        hn_rd[xi] = trp
                        evc = h.cp("vector", h2T[b][:, :, 128 * s:128 * (s + 1)], psT4[:], [trp, h2T_rd[b]])
                        psT_rd = evc
                        return evc

                hw_next = [norm_sub(0, s_) for s_ in range(4)]
                for tl in range(NTI4):
                    toff = tl * TT
                    b = tl % 2
                    hw = hw_next
                    u_w = []
                    first = hw + uT_rd
                    uT_rd = []
                    for m_ in range(32):
                        ui = cnt["U"] % 3
                        cnt["U"] += 1
                        ri = cnt["R"] % 2
                        cnt["R"] += 1
                        m1 = h.mms([(psU[ui][:], w1[:, kc, 128 * m_:128 * (m_ + 1)], h2T[b][:, kc, :], kc == 0, kc == 7) for kc in range(8)], first + [psU_rd[ui]])
                        first = []
                        a = h.act(ur[ri][:], psU[ui][:], AF.Relu, [m1, ur_rd[ri]])
                        psU_rd[ui] = a
                        eng = "gpsimd" if m_ % 2 == 0 else "vector"
                        u2 = h.tt(eng, uT[:, m_, :], ur[ri][:], ur[ri][:], ALU.mult, [a])
                        ur_rd[ri] = u2
                        u_w.append(u2)
                    h2T_rd[b] = m1
                    hw_new = []
                    for s in range(4):
                        oi = cnt["O"] % 2
                        cnt["O"] += 1
                        xi = 0
                        roff = toff + 128 * s
                        xl = h.dma("sync", xr[xi][:], (lambda ctx, k=(tl % 2) * 4 + s, tl=tl: x1_8[k][UB(ctx, tl)]), ev_xr[xi], [xr_rd[xi]])
                        lst = []
                        for hf in range(2):
                            for kc in range(32):
                                lst.append((psO4[oi][:, hf, :], uT[:, kc, 128 * s:128 * (s + 1)], w2[:, kc, 512 * hf:512 * (hf + 1)], kc == 0, kc == 31))
                        m2 = h.mms(lst, u_w + [psO_rd[oi]])
                        u_w = []
                        uT_rd = [m2]
                        if tl + 1 < NTI4:
                            hw_new.append(norm_sub(tl + 1, s))
                        s0 = h.act(junk4[:, 0:512], psO4[oi][:, 0, :], AF.Square, [m2, ss_rd[xi]], accum=ssb[xi][:, 0:1])
                        s1 = h.act(junk4[:, 512:1024], psO4[oi][:, 1, :], AF.Square, [m2, s0], accum=ssb[xi][:, 1:2])
                        r1 = h.tt("vector", rs4[xi][:], ssb[xi][:, 0:1], ssb[xi][:, 1:2], ALU.add, [s0, s1, xr_rd[xi]])
                        r2 = h.act(rs4[xi][:], rs4[xi][:], AF.Sqrt, [r1], scale=1.0 / D, bias=EPS)
                        r3 = h.recip(rs4[xi][:], rs4[xi][:], [r2])
                        ss_rd[xi] = r3
                        n1 = h.act(nr4[xi][:].rearrange("p (h c) -> p h c", h=2), psO4[oi][:], AF.Copy, [r3, xr_rd[xi]], scale=rs4[xi][:, 0:1])
                        psO_rd[oi] = n1
                        n2 = h.tt("vector", nr4[xi][:], nr4[xi][:], g4b[:], ALU.mult, [n1])
                        n3 = h.tt("gpsimd", xr[xi][:], nr4[xi][:], xr[xi][:], ALU.add, [n2, xl])
                        xr_rd[xi] = h.dma("sync", (lambda ctx, k=(tl % 2) * 4 + s, tl=tl: y8[k][UB(ctx, tl)]), xr[xi][:], ev_y[xi], [n3])
                    if tl + 1 < NTI4:
                        hw_next = hw_new
                P.emit((NT // (TT * NTI4),))
    return nc


def _invf():
    half = 64
    inv = np.power(np.float32(10000.0), -np.arange(half, dtype=np.float32) / np.float32(half)).astype(np.float32)
    return np.concatenate([inv, inv]).reshape(128, 1).astype(np.float32)


_CACHE = {}


def kernel(x_prompt, x_sample, w_in, sink, w_a, w_b, w_o, g_pre_mix, g_post_mix, g_pre_mlp, g_post_mlp, w_1, w_2):
    x_prompt = np.asarray(x_prompt, dtype=np.float32)
    x_sample = np.asarray(x_sample, dtype=np.float32)
    if "nc" not in _CACHE:
        _CACHE["nc"] = build_program([(4, 2048), (1, 4096)])
    nc = _CACHE["nc"]
    f = lambda a: np.ascontiguousarray(np.asarray(a, dtype=np.float32))
    shared = {
        "w_in": f(w_in[0]), "sink": f(sink[0]).reshape(1, 8), "w_a": f(w_a[0]), "w_b": f(w_b[0]), "w_o": f(w_o[0]),
        "g_pre_mix": f(g_pre_mix[0]).reshape(1, D), "g_post_mix": f(g_post_mix[0]).reshape(1, D),
        "g_pre_mlp": f(g_pre_mlp[0]).reshape(1, D), "g_post_mlp": f(g_post_mlp[0]).reshape(1, D),
        "w_1": f(w_1[0]), "w_2": f(w_2[0]), "invf": _invf(),
    }
    in_maps = []
    for c in range(N_CORES):
        xc = np.concatenate([x_prompt[4 * c:4 * c + 4].reshape(4 * 2048, D), x_sample[c].reshape(4096, D)], axis=0)
        m = dict(shared)
        xr_ = xc.reshape(-1, 8, 128, D)
        for k in range(8):
            m[f"x{k}"] = np.ascontiguousarray(xr_[:, k])
        in_maps.append(m)
    res = run_bass_kernel_spmd(nc, in_maps, core_ids=list(range(N_CORES)))
    y_prompt = np.empty((32, 2048, D), np.float32)
    y_sample = np.empty((8, 4096, D), np.float32)
    for c in range(N_CORES):
        y = np.stack([res.results[c][f"y{k}"] for k in range(8)], axis=1).reshape(-1, D)
        y_prompt[4 * c:4 * c + 4] = y[:8192].reshape(4, 2048, D)
        y_sample[c] = y[8192:].reshape(4096, D)
    return (y_prompt, y_sample)
```
